# Optimizing a Trainium2 kernel written in Bass

```python
import jax
import jax.numpy as jnp
from jax import lax
import numpy as np

D_MODEL = 2048
BATCH = 8
SEQ = 2048
DEPTH = 4

N_MEM = 256
EPS = 1e-6
DN_QK_HEADS = 16
DN_V_HEADS = 32
DN_HEAD_DIM = 128
DN_QK_WIDTH = DN_QK_HEADS * DN_HEAD_DIM
DN_V_WIDTH = DN_V_HEADS * DN_HEAD_DIM
DN_CONV_WIDTH = 5
DN_CHUNK = 64
DN_CONV_CH = 2 * DN_QK_WIDTH + DN_V_WIDTH
POOL_WINDOWS = (2, 4, 8, 16)
POOL_GROUPS = 4
POOL_WIDTH = DN_V_WIDTH
POOL_GROUP_WIDTH = POOL_WIDTH // POOL_GROUPS
XA_HEADS = 4
XA_HEAD_DIM = D_MODEL // 4
XA_WIDTH = XA_HEADS * XA_HEAD_DIM
INNER = POOL_WIDTH + XA_WIDTH
POOL_IN = POOL_WIDTH + XA_WIDTH + INNER
DN_IN = DN_CONV_CH + XA_WIDTH + INNER + 4 * DN_V_HEADS
N_POOL_LAYERS = (DEPTH + 1) // 2
N_DN_LAYERS = DEPTH // 2

kernel_name = "hybrid_pool_deltanet_memxattn_encoder"


def rmsnorm(x, w):
    xf = x.astype(jnp.float32)
    y = xf * lax.rsqrt(jnp.mean(xf * xf, axis=-1, keepdims=True) + EPS)
    return (y * w.astype(jnp.float32)).astype(x.dtype)


def l2norm(x):
    return x * lax.rsqrt(jnp.sum(x * x, axis=-1, keepdims=True) + EPS)


def memory_cross_attention(xq, mem_k, mem_v):
    b, s, _ = xq.shape
    q = xq.reshape(b, s, XA_HEADS, XA_HEAD_DIM)
    scores = jnp.einsum('bshd,bmhd->bhsm', q, mem_k).astype(jnp.float32) * (XA_HEAD_DIM ** -0.5)
    p = jax.nn.softmax(scores, axis=-1).astype(xq.dtype)
    o = jnp.einsum('bhsm,bmhd->bshd', p, mem_v)
    return o.reshape(b, s, XA_WIDTH)


def multiscale_centred_mean(u):
    s = u.shape[1]
    cs = jnp.cumsum(u.astype(jnp.float32), axis=1)
    cs = jnp.pad(cs, ((0, 0), (1, 0), (0, 0)))
    t = np.arange(s)
    outs = []
    for g, w in enumerate(POOL_WINDOWS):
        lo = np.maximum(t - w // 2, 0)
        hi = np.minimum(t + w // 2, s)
        grp = cs[:, :, g * POOL_GROUP_WIDTH:(g + 1) * POOL_GROUP_WIDTH]
        cnt = jnp.asarray((hi - lo).astype(np.float32))[None, :, None]
        outs.append((grp[:, hi] - grp[:, lo]) / cnt)
    return jnp.concatenate(outs, axis=-1).astype(u.dtype)


def centred_depthwise_conv(u, w):
    c = u.shape[-1]
    k = w.shape[0]
    return lax.conv_general_dilated(
        u, w[:, None, :].astype(u.dtype), window_strides=(1,),
        padding=[(k // 2, k // 2)], dimension_numbers=('NWC', 'WIO', 'NWC'),
        feature_group_count=c)


def chunk_gated_delta(q, k, v, g, beta):
    b, h, s, dk = q.shape
    dv = v.shape[-1]
    c = DN_CHUNK
    n = s // c
    q = q.reshape(b, h, n, c, dk)
    k = k.reshape(b, h, n, c, dk)
    v = v.reshape(b, h, n, c, dv)
    g = jnp.cumsum(g.reshape(b, h, n, c), axis=-1)
    beta = beta.reshape(b, h, n, c)
    k_beta = k * beta[..., None]
    v_beta = v * beta[..., None]
    tril = jnp.tril(jnp.ones((c, c), bool))
    strict = jnp.tril(jnp.ones((c, c), bool), -1)
    diff = g[..., :, None] - g[..., None, :]
    decay = jnp.where(tril, jnp.exp(jnp.where(tril, diff, 0.0)), 0.0)
    lower = jnp.where(strict, jnp.einsum('bhnid,bhnjd->bhnij', k_beta, k) * decay, 0.0)
    a_mat = lower + jnp.eye(c, dtype=lower.dtype)
    rhs = jnp.concatenate([v_beta, k_beta * jnp.exp(g)[..., None]], axis=-1)
    sol = lax.linalg.triangular_solve(a_mat, rhs, left_side=True, lower=True, unit_diagonal=True)
    u_pseudo = sol[..., :dv]
    w_cum = sol[..., dv:]
    attn_intra = jnp.where(tril, jnp.einsum('bhnid,bhnjd->bhnij', q, k) * decay, 0.0)

    def step(state, xs):
        q_c, k_c, u_c, w_c, g_c, a_c = xs
        v_new = u_c - jnp.einsum('bhck,bhkv->bhcv', w_c, state)
        o_c = (jnp.einsum('bhck,bhkv->bhcv', q_c * jnp.exp(g_c)[..., None], state)
               + jnp.einsum('bhcj,bhjv->bhcv', a_c, v_new))
        g_last = g_c[..., -1:]
        k_dec = k_c * jnp.exp(g_last - g_c)[..., None]
        state = state * jnp.exp(g_last)[..., None] + jnp.einsum('bhck,bhcv->bhkv', k_dec, v_new)
        return state, o_c

    xs = tuple(jnp.moveaxis(t, 2, 0) for t in (q, k, u_pseudo, w_cum, g, attn_intra))
    state0 = jnp.zeros((b, h, dk, dv), jnp.float32)
    _, o = lax.scan(step, state0, xs)
    return jnp.moveaxis(o, 0, 2).reshape(b, h, s, dv)


def pooling_branch(h, w_in, w_group, scale, mem_k, mem_v):
    b, s, _ = h.shape
    proj = h @ w_in
    u = proj[..., :POOL_WIDTH]
    xq = proj[..., POOL_WIDTH:POOL_WIDTH + XA_WIDTH]
    gate = proj[..., POOL_WIDTH + XA_WIDTH:]
    pooled = multiscale_centred_mean(u) - u
    mixed = jnp.einsum('bsgc,gcd->bsgd', pooled.reshape(b, s, POOL_GROUPS, POOL_GROUP_WIDTH), w_group)
    mixed = mixed.reshape(b, s, POOL_WIDTH) * scale
    xa = memory_cross_attention(xq, mem_k, mem_v)
    return jnp.concatenate([mixed, xa], axis=-1) * jax.nn.silu(gate)


def deltanet_branch(h, w_in, conv_w, a_log, dt_bias, norm_w, mem_k, mem_v):
    b, s, _ = h.shape
    proj = h @ w_in
    qkv = proj[..., :DN_CONV_CH]
    xq = proj[..., DN_CONV_CH:DN_CONV_CH + XA_WIDTH]
    gate = proj[..., DN_CONV_CH + XA_WIDTH:DN_CONV_CH + XA_WIDTH + INNER]
    ba = proj[..., DN_CONV_CH + XA_WIDTH + INNER:].astype(jnp.float32)
    qkv = jax.nn.silu(centred_depthwise_conv(qkv, conv_w)).astype(jnp.float32)
    q = qkv[..., :DN_QK_WIDTH].reshape(b, s, DN_QK_HEADS, DN_HEAD_DIM)
    k = qkv[..., DN_QK_WIDTH:2 * DN_QK_WIDTH].reshape(b, s, DN_QK_HEADS, DN_HEAD_DIM)
    v = qkv[..., 2 * DN_QK_WIDTH:].reshape(b, s, DN_V_HEADS, DN_HEAD_DIM)
    rep = DN_V_HEADS // DN_QK_HEADS
    q = jnp.repeat(l2norm(q) * (DN_HEAD_DIM ** -0.5), rep, axis=2).transpose(0, 2, 1, 3)
    k = jnp.repeat(l2norm(k), rep, axis=2).transpose(0, 2, 1, 3)
    v = v.transpose(0, 2, 1, 3)
    b_f, b_b, a_f, a_b = jnp.split(ba, 4, axis=-1)

    def decay_gate(a, d):
        g = -jnp.exp(a_log[d].astype(jnp.float32)) * jax.nn.softplus(a + dt_bias[d].astype(jnp.float32))
        return g.transpose(0, 2, 1)

    g_f, g_b = decay_gate(a_f, 0), decay_gate(a_b, 1)
    beta_f = jax.nn.sigmoid(b_f).transpose(0, 2, 1)
    beta_b = jax.nn.sigmoid(b_b).transpose(0, 2, 1)
    o_fwd = chunk_gated_delta(q, k, v, g_f, beta_f)
    flip = lambda t: jnp.flip(t, axis=2)
    o_bwd = flip(chunk_gated_delta(flip(q), flip(k), flip(v), flip(g_b), flip(beta_b)))
    o = o_fwd + o_bwd
    o = o * lax.rsqrt(jnp.mean(o * o, axis=-1, keepdims=True) + EPS) * norm_w.astype(jnp.float32)
    o = o.transpose(0, 2, 1, 3).reshape(b, s, DN_V_WIDTH).astype(h.dtype)
    xa = memory_cross_attention(xq, mem_k, mem_v)
    return jnp.concatenate([o, xa], axis=-1) * jax.nn.silu(gate)


def setup_inputs(seed: int = 0) -> dict:
    key = jax.random.key(seed)
    ks = jax.random.split(key, 16)
    f32 = jnp.float32
    nrm = lambda k, shape, fan_in: jax.random.normal(k, shape, f32) * (fan_in ** -0.5)
    x = jax.random.normal(ks[0], (BATCH, SEQ, D_MODEL), f32)
    mem = jax.random.normal(ks[1], (BATCH, N_MEM, D_MODEL), f32)
    norm_w = 1.0 + 0.05 * jax.random.normal(ks[2], (DEPTH, D_MODEL), f32)
    mem_norm_w = 1.0 + 0.05 * jax.random.normal(ks[3], (DEPTH, D_MODEL), f32)
    w_kv_mem = nrm(ks[4], (DEPTH, D_MODEL, 2 * XA_WIDTH), D_MODEL)
    w_out = nrm(ks[5], (DEPTH, INNER, D_MODEL), INNER)
    pool_w_in = nrm(ks[6], (N_POOL_LAYERS, D_MODEL, POOL_IN), D_MODEL)
    pool_w_group = nrm(ks[7], (N_POOL_LAYERS, POOL_GROUPS, POOL_GROUP_WIDTH, POOL_GROUP_WIDTH), POOL_GROUP_WIDTH)
    pool_scale = 1.0 + 0.1 * jax.random.normal(ks[8], (N_POOL_LAYERS, POOL_WIDTH), f32)
    dn_w_in = nrm(ks[9], (N_DN_LAYERS, D_MODEL, DN_IN), D_MODEL)
    dn_conv_w = nrm(ks[10], (N_DN_LAYERS, DN_CONV_WIDTH, DN_CONV_CH), DN_CONV_WIDTH)
    dn_a_log = jnp.log(jax.random.uniform(ks[11], (N_DN_LAYERS, 2, DN_V_HEADS), f32, minval=1.0, maxval=16.0))
    dt = jnp.exp(jax.random.uniform(ks[12], (N_DN_LAYERS, 2, DN_V_HEADS), f32,
                                    minval=float(np.log(1e-3)), maxval=float(np.log(1e-1))))
    dn_dt_bias = dt + jnp.log(-jnp.expm1(-dt))
    dn_norm_w = 1.0 + 0.05 * jax.random.normal(ks[13], (N_DN_LAYERS, DN_HEAD_DIM), f32)
    final_norm_w = 1.0 + 0.05 * jax.random.normal(ks[14], (D_MODEL,), f32)
    return {'x': x, 'mem': mem, 'norm_w': norm_w, 'mem_norm_w': mem_norm_w, 'w_kv_mem': w_kv_mem,
            'w_out': w_out, 'pool_w_in': pool_w_in, 'pool_w_group': pool_w_group, 'pool_scale': pool_scale,
            'dn_w_in': dn_w_in, 'dn_conv_w': dn_conv_w, 'dn_a_log': dn_a_log, 'dn_dt_bias': dn_dt_bias,
            'dn_norm_w': dn_norm_w, 'final_norm_w': final_norm_w}


def reference(x, mem, norm_w, mem_norm_w, w_kv_mem, w_out, pool_w_in, pool_w_group, pool_scale,
              dn_w_in, dn_conv_w, dn_a_log, dn_dt_bias, dn_norm_w, final_norm_w):
    b, m, _ = mem.shape
    for i in range(DEPTH):
        h = rmsnorm(x, norm_w[i])
        kv = rmsnorm(mem, mem_norm_w[i]) @ w_kv_mem[i]
        mem_k = kv[..., :XA_WIDTH].reshape(b, m, XA_HEADS, XA_HEAD_DIM)
        mem_v = kv[..., XA_WIDTH:].reshape(b, m, XA_HEADS, XA_HEAD_DIM)
        j = i // 2
        if i % 2 == 0:
            y = pooling_branch(h, pool_w_in[j], pool_w_group[j], pool_scale[j], mem_k, mem_v)
        else:
            y = deltanet_branch(h, dn_w_in[j], dn_conv_w[j], dn_a_log[j], dn_dt_bias[j], dn_norm_w[j],
                                mem_k, mem_v)
        x = x + y @ w_out[i]
    return rmsnorm(x, final_norm_w)
```

```python
import numpy as np
from contextlib import ExitStack
import concourse.bass as bass
import concourse.mybir as mybir
from concourse.bass_utils import run_bass_kernel_spmd

F32 = mybir.dt.float32
BF16 = mybir.dt.bfloat16
AF = mybir.ActivationFunctionType
ALU = mybir.AluOpType
AX = mybir.AxisListType

D = 2048
T = 2048
NMEM = 256
DEPTH = 4
EPS = 1e-6
NTB = T // 512
KC = D // 128
POOL_WINDOWS = (2, 4, 8, 16)


class _Op:
    __slots__ = ("eng", "fn", "deps", "is_dma", "has_dep", "sig", "idx")

    def __init__(self, eng, fn, is_dma):
        self.eng = eng
        self.fn = fn
        self.deps = set()
        self.is_dma = is_dma
        self.has_dep = False
        self.sig = 0
        self.idx = 0


class Prog:
    CE = ("pe", "dve", "act", "pool")
    QE = ("sp", "act", "pool")
    ALLE = ("pe", "dve", "act", "pool", "sp")
    NSLOT = 8

    def __init__(self, nc):
        self.nc = nc
        self.stream = {e: [] for e in self.ALLE}
        self.last_w = {}
        self.rd_c = {}
        self.rd_d = {}
        self.ndma = {q: 0 for q in self.QE}
        self.dmas_since_barrier = []

    def add(self, eng, fn, reads=(), writes=(), dma=False):
        op = _Op(eng, fn, dma)
        deps = op.deps
        for r in reads:
            w = self.last_w.get(r)
            if w is not None:
                deps.add(w)
        for r in writes:
            w = self.last_w.get(r)
            if w is not None:
                deps.add(w)
            for o in self.rd_c.get(r, {}).values():
                deps.add(o)
            for o in self.rd_d.get(r, ()):
                deps.add(o)
        if eng == "pe":
            for d_ in [d_ for d_ in deps if d_.eng == "pe" and not d_.is_dma]:
                deps.discard(d_)
        for r in writes:
            self.last_w[r] = op
            self.rd_c[r] = {}
            self.rd_d[r] = []
        for r in reads:
            if dma:
                self.rd_d.setdefault(r, []).append(op)
            else:
                self.rd_c.setdefault(r, {})[eng] = op
        if dma:
            op.idx = self.ndma[eng]
            self.ndma[eng] += 1
            self.dmas_since_barrier.append(op)
        self.stream[eng].append(op)
        return op

    def dma(self, q, out, in_, reads, writes):
        return self.add(q, lambda e: e.dma_start(out=out, in_=in_), reads, writes, dma=True)

    def act(self, out, in_, func, reads, writes, **kw):
        return self.add("act", lambda e: e.activation(out=out, in_=in_, func=func, **kw), reads, writes)

    def tt(self, eng, out, in0, in1, op, reads, writes):
        return self.add(eng, lambda e: e.tensor_tensor(out=out, in0=in0, in1=in1, op=op), reads, writes)

    def ts(self, eng, out, in0, s1, s2, op0, op1, reads, writes):
        if op1 is None:
            return self.add(eng, lambda e: e.tensor_scalar(out=out, in0=in0, scalar1=s1, scalar2=None, op0=op0),
                            reads, writes)
        return self.add(eng, lambda e: e.tensor_scalar(out=out, in0=in0, scalar1=s1, scalar2=s2, op0=op0, op1=op1),
                        reads, writes)

    def stt(self, eng, out, in0, scalar, in1, op0, op1, reads, writes):
        return self.add(eng, lambda e: e.scalar_tensor_tensor(out=out, in0=in0, scalar=scalar, in1=in1,
                                                              op0=op0, op1=op1), reads, writes)

    def copy(self, eng, out, in_, reads, writes):
        if eng == "act":
            return self.add("act", lambda e: e.copy(out=out, in_=in_), reads, writes)
        return self.add(eng, lambda e: e.tensor_copy(out=out, in_=in_), reads, writes)

    def memset(self, eng, ap, val, writes):
        return self.add(eng, lambda e: e.memset(ap, val), (), writes)

    def mmgroup(self, out, pairs, reads, writes):
        n = len(pairs)

        def fn(e):
            ins = None
            for i, (l, r) in enumerate(pairs):
                ins = e.matmul(out, l, r, start=(i == 0), stop=(i == n - 1))
            return ins
        return self.add("pe", fn, reads, writes)

    def transpose(self, out, in_, ident, reads, writes):
        return self.add("pe", lambda e: e.transpose(out, in_, ident), reads, writes)

    def barrier(self):
        tails = [self.stream[e][-1] for e in self.CE if self.stream[e] and not self.stream[e][-1].is_dma]
        for e in self.CE:
            for o in reversed(self.stream[e]):
                if not o.is_dma:
                    tails.append(o)
                    break
        b = _Op("sp", lambda e: e.nop(), False)
        b.deps = set(tails) | set(self.dmas_since_barrier)
        self.stream["sp"].append(b)
        self.dmas_since_barrier = []
        for e in self.CE:
            o = _Op(e, lambda en: en.nop(), False)
            o.deps = {b}
            self.stream[e].append(o)
        self.last_w = {}
        self.rd_c = {}
        self.rd_d = {}
        return b

    def emit(self):
        nc = self.nc
        for ops in self.stream.values():
            for op in ops:
                for d_ in op.deps:
                    d_.has_dep = True
        for e, ops in self.stream.items():
            c = 0
            for op in ops:
                if (not op.is_dma) and op.has_dep:
                    c += 1
                    op.sig = c
        NS = self.NSLOT
        with ExitStack() as st:
            sems = {e: st.enter_context(nc.semaphore(f"s_{e}")) for e in self.ALLE}
            dsems = {q: [st.enter_context(nc.semaphore(f"d_{q}{i}")) for i in range(NS)] for q in self.QE}
            block = st.enter_context(nc.Block())

            def run(ename, eng):
                waited = {e: 0 for e in self.ALLE}
                waited_d = {}
                for op in self.stream[ename]:
                    need = {}
                    needd = {}
                    for d_ in op.deps:
                        if d_.is_dma:
                            k = (d_.eng, d_.idx % NS)
                            rnd = d_.idx // NS + 1
                            if waited_d.get(k, 0) < rnd:
                                needd[k] = max(needd.get(k, 0), rnd)
                        else:
                            if waited[d_.eng] < d_.sig:
                                need[d_.eng] = max(need.get(d_.eng, 0), d_.sig)
                    if op.is_dma and op.idx >= NS:
                        k = (ename, op.idx % NS)
                        rnd = op.idx // NS
                        if waited_d.get(k, 0) < rnd:
                            needd[k] = max(needd.get(k, 0), rnd)
                    for e2, v in need.items():
                        eng.wait_ge(sems[e2], v)
                        waited[e2] = v
                    for (q, slot), r in needd.items():
                        eng.wait_ge(dsems[q][slot], 16 * r)
                        waited_d[(q, slot)] = r
                    ins = op.fn(eng)
                    if op.is_dma:
                        ins.then_inc(dsems[ename][op.idx % NS], 16)
                    elif op.has_dep:
                        ins.then_inc(sems[ename], 1)

            @block.tensor
            def _(e):
                run("pe", e)

            @block.vector
            def _(e):
                run("dve", e)

            @block.scalar
            def _(e):
                run("act", e)

            @block.gpsimd
            def _(e):
                run("pool", e)

            @block.sync
            def _(e):
                run("sp", e)


class Arena:
    def __init__(self, t):
        self.t = t

    def view(self, off, shape, dt):
        n = int(np.prod(shape))
        esz = 4 if dt == F32 else 2
        assert off % 4 == 0 and (n * esz) % 4 == 0
        ap = self.t[:, off // 4:(off + n * esz) // 4]
        if dt != F32:
            ap = ap.bitcast(dt)
        if len(shape) == 2:
            ap = ap.rearrange("p (a b) -> p a b", a=shape[0])
        elif len(shape) == 3:
            ap = ap.rearrange("p (a b c) -> p a b c", a=shape[0], b=shape[1])
        return ap


ARENA_BYTES = 207 * 1024
O_HT = 0
O_WB = 65536
O_R1 = O_WB + 3 * 8192
O_R2 = O_R1 + 32768
O_SG = O_R2 + 33792
O_KV = O_SG + 16384
O_CONST = O_KV + 24576
C_IDB = O_CONST
C_IDF = C_IDB + 256
C_ONE = C_IDF + 512
C_VEC = C_ONE + 512
NVEC = 864
C_RSTD = C_VEC + 4 * 864
C_MRSTD = C_RSTD + 2048
C_SM = C_MRSTD + 1024
C_ONE1 = C_SM + 64
C_MASK = C_ONE1 + 512
C_END = C_MASK + 2048
assert NVEC <= 864 and C_END <= ARENA_BYTES, (NVEC, C_END)

V_NORM = 0
V_MNORM = 64
V_FIN = 128
V_PSCALE = 144
V_CONV = 208
V_DNNORM = 848
V_EPS = 850
V_ONE = 851
V_DTB = 852
V_ALOG = 854


class Ctx:
    pass


class WStream:
    def __init__(self, P, A, base, nslots=3, slot_bytes=8192):
        self.P, self.A, self.base, self.n, self.sb = P, A, base, nslots, slot_bytes
        self.i = 0
        self.cache = {}
        self.owner = [None] * nslots

    def get(self, key, src, shape):
        if key in self.cache:
            return self.cache[key]
        s = self.i % self.n
        self.i += 1
        if self.owner[s] is not None:
            del self.cache[self.owner[s]]
        self.owner[s] = key
        view = self.A.view(self.base + s * self.sb, shape, BF16)
        fshape = list(src.shape[1:])
        flat = self.A.view(self.base + s * self.sb, fshape, BF16)
        assert int(np.prod(fshape)) == int(np.prod(shape)), (fshape, shape)
        self.P.dma("pool", flat, src, reads=[], writes=[("wb", self.base, s)])
        self.cache[key] = (("wb", self.base, s), view)
        return self.cache[key]


class Rot:
    def __init__(self, items):
        self.items = items
        self.i = 0

    def next(self):
        x = self.items[self.i % len(self.items)]
        self.i += 1
        return x


def build_program(layers=(0, 1, 2, 3), final=True, dbg_out=None):
    nc = bass.Bass("TRN2", target_bir_lowering=False)
    C = Ctx()
    C.nc = nc

    def din(name, shape, dt=F32):
        return nc.dram_tensor(name, shape, dt, kind="ExternalInput").ap()

    C.xT_in = din("xT", [D, T])
    C.memT_in = din("memT", [D, NMEM])
    C.vecs_in = din("vecs", [128, NVEC])
    C.cmat_in = din("cmat", [128, 3 * 128])
    C.cbf_in = din("cbf", [128, 128], BF16)
    C.redge_in = din("redge", [128, 64])
    C.cmask_in = din("cmask", [128, T])
    C.masks_in = din("masks", [128, 4 * 128])
    C.w_kvk = din("w_kvk", [DEPTH, 16, 128, 16 * 128])
    C.w_kvv = din("w_kvv", [DEPTH, 8, 128, 16 * 256])
    C.w_out = din("w_out", [DEPTH, 16, 128, 48 * 128])
    C.pw_in = din("pw_in", [2, 96, 128, 16 * 128])
    C.pw_g = din("pw_g", [2, 32, 128, 8 * 128])
    C.dw_in = din("dw_in", [2, 129, 128, 16 * 128])
    C.outT = nc.dram_tensor("outT", [D, T], F32, kind="ExternalOutput").ap()
    C.xS = nc.dram_tensor("xS", [D, T], F32).ap()
    C.yS = nc.dram_tensor("yS", [48, 128, T], BF16).ap()
    C.qS = nc.dram_tensor("qS", [16, 128, T], BF16).ap()
    C.kS = nc.dram_tensor("kS", [16, 128, T], BF16).ap()
    C.vS = nc.dram_tensor("vS", [32, 128, T], BF16).ap()
    C.sgS = nc.dram_tensor("sgS", [32, 128, T], BF16).ap()
    C.baS = nc.dram_tensor("baS", [128, T], F32).ap()

    with ExitStack() as st:
        arena_t = st.enter_context(nc.sbuf_tensor("arena", [128, ARENA_BYTES // 4], F32))
        C.A = A = Arena(arena_t)
        C.psum = [st.enter_context(nc.psum_tensor(f"ps{i}", [128, 512], F32)) for i in range(8)]
        C.P = P = Prog(nc)

        C.identb = A.view(C_IDB, [128], BF16)
        C.identf = A.view(C_IDF, [128], F32)
        C.onesm = A.view(C_ONE, [128], F32)
        C.vecs = A.view(C_VEC, [NVEC], F32)
        C.rstd = A.view(C_RSTD, [512], F32)
        C.sm = A.view(C_SM, [16], F32)
        C.redge = A.view(C_MRSTD, [64], F32)
        P.dma("sp", C.identb, C.cbf_in, [], ["c0"])
        P.dma("sp", C.identf, C.cmat_in[:, 0:128], [], ["c1"])
        P.dma("sp", C.onesm, C.cmat_in[:, 128:256], [], ["c2"])
        P.dma("sp", C.vecs, C.vecs_in, [], ["c3"])
        P.dma("sp", C.redge, C.redge_in, [], ["c4"])
        C.ones1 = A.view(C_ONE1, [128], F32)
        C.masks = A.view(C_MASK, [4, 128], F32)
        P.dma("sp", C.ones1, C.cmat_in[:, 256:384], [], ["c5"])
        P.dma("sp", C.masks, C.masks_in.rearrange("p (a b) -> p a b", a=4), [], ["c6"])
        P.barrier()

        xsrc = C.xT_in
        for li in layers:
            phase_norm(C, xsrc, C.vecs[:, V_NORM + li * 16: V_NORM + li * 16 + 16])
            phase_memkv(C, li)
            P.barrier()
            if li % 2 == 0:
                phase_pool(C, li)
            else:
                phase_dn(C, li)
            P.barrier()
            phase_out(C, li, xsrc)
            P.barrier()
            xsrc = C.xS
        if final:
            phase_final(C, xsrc)
        else:
            for oc in range(16):
                P.dma("sp", C.outT[oc * 128:(oc + 1) * 128, :], xsrc[oc * 128:(oc + 1) * 128, :], reads=[], writes=[("o", oc)])
        P.barrier()
        P.emit()
    return nc


def rsqrt_eps(C, out, in_, reads, wkey):
    P = C.P
    P.act(out, in_, AF.Sqrt, reads=list(reads), writes=[wkey], bias=C.vecs[:, V_EPS:V_EPS + 1], scale=1.0)
    P.add("dve", lambda e: e.reciprocal(out=out, in_=out), reads=[wkey], writes=[wkey])


def phase_norm(C, src, wcol):
    P, A = C.P, C.A
    hT = A.view(O_HT, [16, T], BF16)
    xt = A.view(O_R1, [16, 512], F32)
    sq = [A.view(O_R2 + i * 2048, [512], F32) for i in range(2)]
    srcv = src.rearrange("(kc p) t -> p kc t", p=128)
    ps = C.psum[6]
    for tb in range(NTB):
        P.dma("sp", xt, srcv[:, :, tb * 512:(tb + 1) * 512], reads=[], writes=["xt"])
        for kc in range(16):
            P.act(sq[kc % 2], xt[:, kc, :], AF.Square, reads=["xt"], writes=[("sq", kc % 2)])
            P.add("pe", (lambda kc=kc: lambda e: e.matmul(ps[:, :], C.onesm, sq[kc % 2],
                                                           start=(kc == 0), stop=(kc == 15)))(),
                  reads=[("sq", kc % 2)], writes=[("ps", 6)] if kc in (0, 15) else [])
        rsqrt_eps(C, C.rstd, ps[:, :], [("ps", 6)], "rstd")
        for kc in range(16):
            eng = "dve"
            P.stt(eng, hT[:, kc, tb * 512:(tb + 1) * 512], xt[:, kc, :], wcol[:, kc:kc + 1], C.rstd,
                  ALU.mult, ALU.mult, reads=["xt", "rstd"], writes=[("hT", tb, kc)])


def phase_memkv(C, li):
    P, A = C.P, C.A
    memf = A.view(O_R2 + 4096, [16, NMEM], F32)
    sq = [A.view(O_R2 + 4096 + 16384 + i * 1024, [NMEM], F32) for i in range(2)]
    mrstd = A.view(O_R2 + 4096 + 16384 + 2048, [NMEM], F32)
    kT = A.view(O_KV, [16, NMEM], BF16)
    v = A.view(O_KV + 8192, [2, D], BF16)
    memn = A.view(O_KV + 16384, [16, NMEM], BF16)
    wcol = C.vecs[:, V_MNORM + li * 16: V_MNORM + li * 16 + 16]
    ps = C.psum[7]
    P.dma("sp", memf, C.memT_in.rearrange("(kc p) m -> p kc m", p=128), reads=[], writes=["memf"])
    for kc in range(16):
        P.act(sq[kc % 2], memf[:, kc, :], AF.Square, reads=["memf"], writes=[("msq", kc % 2)])
        P.add("pe", (lambda kc=kc: lambda e: e.matmul(ps[:, 0:NMEM], C.onesm, sq[kc % 2],
                                                       start=(kc == 0), stop=(kc == 15)))(),
              reads=[("msq", kc % 2)], writes=[("ps", 7)] if kc in (0, 15) else [])
    rsqrt_eps(C, mrstd, ps[:, 0:NMEM], [("ps", 7)], "mrstd")
    for kc in range(16):
        P.stt("dve", memn[:, kc, :], memf[:, kc, :], wcol[:, kc:kc + 1], mrstd, ALU.mult, ALU.mult,
              reads=["memf", "mrstd"], writes=[("memn", kc)])
    memn_keys = [("memn", kc) for kc in range(16)]
    W = WStream(P, A, O_WB)
    rot = Rot([4, 5])
    for c in range(16):
        pair = c // 2
        wkey, wv = W.get(("kvk", li, pair),
                         C.w_kvk[li, pair * 2:pair * 2 + 2].rearrange("n p f -> p n f"), [2, 16, 128])
        b = rot.next()
        pst = C.psum[b]
        P.mmgroup(pst[:, 0:NMEM], [(wv[:, c % 2, kc, :], memn[:, kc, :]) for kc in range(16)],
                  reads=[wkey] + memn_keys, writes=[("ps", b)])
        P.copy("act" if c % 2 else "dve", kT[:, c, :], pst[:, 0:NMEM], reads=[("ps", b)], writes=[("kT", c)])
    for blk in range(8):
        wkey, wv = W.get(("kvv", li, blk), C.w_kvv[li, blk], [16, 256])
        for mc in range(2):
            b = rot.next()
            pst = C.psum[b]
            P.mmgroup(pst[:, 0:256], [(memn[:, kc, mc * 128:(mc + 1) * 128], wv[:, kc, :]) for kc in range(16)],
                      reads=[wkey] + memn_keys, writes=[("ps", b)])
            P.copy("act" if mc else "dve", v[:, mc, blk * 256:(blk + 1) * 256], pst[:, 0:256],
                   reads=[("ps", b)], writes=[("v", mc, blk)])


def gemm_cols(C, W, wsrc_fn, col, rot, epilogue, hT):
    P = C.P
    pair = col // 2
    src = wsrc_fn("src", pair)
    wkey, wv = W.get(wsrc_fn("key", pair), src, [int(src.shape[1]), 16, 128])
    for tb in range(NTB):
        b = rot.next()
        pst = C.psum[b]
        P.mmgroup(pst[:, :], [(wv[:, col % 2, kc, :], hT[:, kc, tb * 512:(tb + 1) * 512]) for kc in range(16)],
                  reads=[wkey], writes=[("ps", b)])
        epilogue(tb, pst, ("ps", b))


def gate_chunk(C, W, wsrc_fn, col, rot, hT, sg, sgkey):
    P = C.P

    def epi(tb, pst, pkey):
        P.act(sg[:, tb * 512:(tb + 1) * 512], pst[:, :], AF.Silu, reads=[pkey], writes=[(sgkey, tb)])
    gemm_cols(C, W, wsrc_fn, col, rot, epi, hT)


def cross_attn(C, W, wsrc_fn, rot, hT, xq_col0, gate_col0, li):
    P, A = C.P, C.A
    kT = A.view(O_KV, [16, NMEM], BF16)
    v = A.view(O_KV + 8192, [2, D], BF16)
    xqT = A.view(O_R2, [4, T], BF16)
    pT = A.view(O_R2 + 16384, [2, T], BF16)
    p32 = [A.view(O_R2 + 24576 + i * 1024, [NMEM], F32) for i in range(2)]
    pbf = [A.view(O_R2 + 26624 + i * 512, [NMEM], BF16) for i in range(2)]
    sgb = [A.view(O_SG + i * 4096, [T], BF16) for i in range(2)]
    yst = [A.view(O_SG + 8192 + i * 4096, [T], BF16) for i in range(2)]
    scale = float(512 ** -0.5)
    sm = C.sm
    cnt = 0
    for hd in range(4):
        for dc in range(4):
            def epi(tb, pst, pkey, dc=dc):
                P.copy("dve" if tb % 2 else "act", xqT[:, dc, tb * 512:(tb + 1) * 512], pst[:, :],
                       reads=[pkey], writes=[("xqT", dc, tb)])
            gemm_cols(C, W, wsrc_fn, xq_col0 + hd * 4 + dc, rot, epi, hT)
        for stl in range(16):
            tb = stl // 4
            i2 = stl % 2
            pst = C.psum[6]
            P.mmgroup(pst[:, 0:NMEM], [(xqT[:, dc, stl * 128:(stl + 1) * 128], kT[:, hd * 4 + dc, :])
                                       for dc in range(4)],
                      reads=[("xqT", dc, tb) for dc in range(4)], writes=[("ps", 6)])
            P.add("dve", lambda e, pst=pst: e.reduce_max(out=sm[:, 0:1], in_=pst[:, 0:NMEM], axis=AX.X),
                  reads=[("ps", 6)], writes=["sm0"])
            P.ts("dve", sm[:, 1:2], sm[:, 0:1], -scale, None, ALU.mult, None, reads=["sm0"], writes=["sm1"])
            P.act(p32[i2], pst[:, 0:NMEM], AF.Exp, reads=[("ps", 6), "sm1"], writes=[("p32", i2)],
                  bias=sm[:, 1:2], scale=scale)
            P.add("dve", lambda e, i2=i2: e.reduce_sum(out=sm[:, 2:3], in_=p32[i2], axis=AX.X),
                  reads=[("p32", i2)], writes=["sm2"])
            P.add("dve", lambda e: e.reciprocal(out=sm[:, 3:4], in_=sm[:, 2:3]), reads=["sm2"], writes=["sm3"])
            P.ts("dve", pbf[i2], p32[i2], sm[:, 3:4], None, ALU.mult, None, reads=[("p32", i2), "sm3"],
                 writes=[("pbf", i2)])
            pt = C.psum[7][:, 0:128].bitcast(BF16)
            for mc in range(2):
                P.transpose(pt[:, mc * 128:(mc + 1) * 128], pbf[i2][:, mc * 128:(mc + 1) * 128], C.identb,
                            reads=[("pbf", i2)], writes=[("ps", 7)])
            P.copy("act", pT[:, :, stl * 128:(stl + 1) * 128], pt.rearrange("p (a b) -> p a b", a=2),
                   reads=[("ps", 7)], writes=[("pT", stl)])
        for dc in range(4):
            j = 32 + hd * 4 + dc
            sg = sgb[cnt % 2]
            ys = yst[cnt % 2]
            gate_chunk(C, W, wsrc_fn, gate_col0 + j, rot, hT, sg, ("sg", cnt % 2))
            for tb in range(NTB):
                b = rot.next()
                pst = C.psum[b]
                P.mmgroup(pst[:, :], [(v[:, mc, hd * 512 + dc * 128: hd * 512 + (dc + 1) * 128],
                                       pT[:, mc, tb * 512:(tb + 1) * 512]) for mc in range(2)],
                          reads=[("pT", s_) for s_ in range(tb * 4, tb * 4 + 4)], writes=[("ps", b)])
                P.tt("dve", ys[:, tb * 512:(tb + 1) * 512], pst[:, :], sg[:, tb * 512:(tb + 1) * 512], ALU.mult,
                     reads=[("ps", b), (("sg", cnt % 2), tb)], writes=[("yst", cnt % 2, tb)])
            P.dma("sp", C.yS[j], ys, reads=[("yst", cnt % 2, tb) for tb in range(NTB)], writes=[("yS", j)])
            cnt += 1


def phase_pool(C, li):
    P, A = C.P, C.A
    j_ = li // 2
    hT = A.view(O_HT, [16, T], BF16)
    pg = A.view(O_R1, [8, T], BF16)
    LP = T + 32
    ub = A.view(O_R2, [LP], F32)
    sA = A.view(O_R2 + LP * 4, [LP], F32)
    sB = A.view(O_R2 + 2 * LP * 4, [LP], F32)
    sgb = [A.view(O_SG + i * 4096, [T], BF16) for i in range(2)]
    yst = [A.view(O_SG + 8192 + i * 4096, [T], BF16) for i in range(2)]
    W = WStream(P, A, O_WB)
    rot = Rot([0, 1, 2, 3, 4, 5])

    def wsrc(kind, pair):
        if kind == "key":
            return ("pw_in", li, pair)
        return C.pw_in[j_, pair * 2:pair * 2 + 2].rearrange("n p f -> p n f")

    P.memset("pool", ub[:, 0:16], 0.0, writes=["ub_padl"])
    P.memset("pool", ub[:, 16 + T:LP], 0.0, writes=["ub_padr"])
    cnt = 0
    for g in range(4):
        w = POOL_WINDOWS[g]
        half = w // 2
        for cc in range(8):
            def epi(tb, pst, pkey):
                P.copy("act" if tb % 2 else "dve", ub[:, 16 + tb * 512:16 + (tb + 1) * 512], pst[:, :],
                       reads=[pkey], writes=[("ub", tb)])
            gemm_cols(C, W, wsrc, g * 8 + cc, rot, epi, hT)
            ubk = [("ub", tb) for tb in range(NTB)] + ["ub_padl", "ub_padr"]
            P.tt("pool", sA[:, 0:LP - 1], ub[:, 0:LP - 1], ub[:, 1:LP], ALU.add, reads=ubk, writes=["sA"])
            cur, curk, n, sh = sA, "sA", LP - 1, 2
            oth, othk = sB, "sB"
            while sh < w:
                P.tt("pool", oth[:, 0:n - sh], cur[:, 0:n - sh], cur[:, sh:n], ALU.add, reads=[curk], writes=[othk])
                cur, curk, oth, othk = oth, othk, cur, curk
                n -= sh
                sh *= 2
            off = 16 - half
            P.stt("dve", pg[:, cc, :], cur[:, off:off + T], 1.0 / w, ub[:, 16:16 + T], ALU.mult, ALU.subtract,
                  reads=[curk] + ubk, writes=[("pg", cc)])
            nl = half
            nr = half - 1
            re = C.redge[:, g * 16:(g + 1) * 16]
            P.tt("dve", C.sm[:, 4:4 + nl], cur[:, off:off + nl], re[:, 0:nl], ALU.mult, reads=[curk], writes=["edl"])
            P.tt("dve", pg[:, cc, 0:nl], C.sm[:, 4:4 + nl], ub[:, 16:16 + nl], ALU.subtract,
                 reads=["edl"] + ubk, writes=[("pg", cc)])
            if nr > 0:
                P.tt("dve", C.sm[:, 4:4 + nr], cur[:, off + T - nr:off + T], re[:, 8:8 + nr], ALU.mult,
                     reads=[curk], writes=["edl"])
                P.tt("dve", pg[:, cc, T - nr:T], C.sm[:, 4:4 + nr], ub[:, 16 + T - nr:16 + T], ALU.subtract,
                     reads=["edl"] + ubk, writes=[("pg", cc)])
        pgk = [("pg", cc) for cc in range(8)]
        for oc in range(8):
            j = g * 8 + oc
            sg = sgb[cnt % 2]
            ys = yst[cnt % 2]
            gate_chunk(C, W, wsrc, 48 + j, rot, hT, sg, ("sg", cnt % 2))
            wkey, wv = W.get(("pw_g", li, g, oc // 4),
                             C.pw_g[j_, g * 8 + (oc // 4) * 4: g * 8 + (oc // 4) * 4 + 4].rearrange("n p f -> p n f"),
                             [4, 8, 128])
            scol = C.vecs[:, V_PSCALE + j_ * 32 + j: V_PSCALE + j_ * 32 + j + 1]
            for tb in range(NTB):
                b = rot.next()
                pst = C.psum[b]
                P.mmgroup(pst[:, :], [(wv[:, oc % 4, kc, :], pg[:, kc, tb * 512:(tb + 1) * 512]) for kc in range(8)],
                          reads=[wkey] + pgk, writes=[("ps", b)])
                P.stt("dve", ys[:, tb * 512:(tb + 1) * 512], pst[:, :], scol, sg[:, tb * 512:(tb + 1) * 512],
                      ALU.mult, ALU.mult, reads=[("ps", b), (("sg", cnt % 2), tb)], writes=[("yst", cnt % 2, tb)])
            P.dma("sp", C.yS[j], ys, reads=[("yst", cnt % 2, tb) for tb in range(NTB)], writes=[("yS", j)])
            cnt += 1
    P.barrier()
    cross_attn(C, W, wsrc, rot, hT, 32, 48, li)


def phase_dn(C, li):
    P, A = C.P, C.A
    j_ = li // 2
    hT = A.view(O_HT, [16, T], BF16)
    LC = T + 4
    cb = A.view(O_R2, [LC], F32)
    acc = A.view(O_R2 + LC * 4, [T], F32)
    sqb = A.view(O_R2 + LC * 4 + T * 4, [T], F32)
    r32 = A.view(O_R1, [T], F32)
    stb = [A.view(O_R1 + 8192 + i * 4096, [T], BF16) for i in range(2)]
    baf = A.view(O_R1 + 16384, [T], F32)
    sgb = [A.view(O_SG + i * 4096, [T], BF16) for i in range(2)]
    W = WStream(P, A, O_WB)
    rot = Rot([0, 1, 2, 3, 4, 5])

    def wsrc(kind, pair):
        if kind == "key":
            return ("dw_in", li, pair)
        n = 2 if pair * 2 + 2 <= 129 else 1
        return C.dw_in[j_, pair * 2:pair * 2 + n].rearrange("n p f -> p n f")

    P.memset("pool", cb[:, 0:2], 0.0, writes=["cb_padl"])
    P.memset("pool", cb[:, 2 + T:LC], 0.0, writes=["cb_padr"])
    cnt = 0
    for c in range(64):
        def epi(tb, pst, pkey):
            P.copy("act", cb[:, 2 + tb * 512:2 + (tb + 1) * 512], pst[:, :], reads=[pkey], writes=[("cb", tb)])
        gemm_cols(C, W, wsrc, c, rot, epi, hT)
        cbk = [("cb", tb) for tb in range(NTB)] + ["cb_padl", "cb_padr"]
        wc = C.vecs[:, V_CONV + j_ * 320 + c * 5: V_CONV + j_ * 320 + c * 5 + 5]
        P.ts("dve", acc, cb[:, 0:T], wc[:, 0:1], None, ALU.mult, None, reads=cbk, writes=["acc"])
        for k in range(1, 5):
            P.stt("dve", acc, cb[:, k:k + T], wc[:, k:k + 1], acc, ALU.mult, ALU.add, reads=cbk + ["acc"], writes=["acc"])
        st_ = stb[cnt % 2]
        stk = ("stb", cnt % 2)
        cnt += 1
        if c >= 32:
            P.act(st_, acc, AF.Silu, reads=["acc"], writes=[stk])
            P.dma("sp", C.vS[c - 32], st_, reads=[stk], writes=[("vS", c - 32)])
        else:
            P.act(acc, acc, AF.Silu, reads=["acc"], writes=["acc"])
            P.act(sqb, acc, AF.Square, reads=["acc"], writes=["sqb"])
            for tb in range(NTB):
                b = rot.next()
                pst = C.psum[b]
                P.add("pe", lambda e, pst=pst, tb=tb: e.matmul(pst[:, :], C.ones1, sqb[:, tb * 512:(tb + 1) * 512],
                                                              start=True, stop=True),
                      reads=["sqb"], writes=[("ps", b)])
                P.act(r32[:, tb * 512:(tb + 1) * 512], pst[:, :], AF.Sqrt, reads=[("ps", b)], writes=[("r32", tb)],
                      bias=C.vecs[:, V_EPS:V_EPS + 1], scale=1.0)
            r32k = [("r32", tb) for tb in range(NTB)]
            P.add("dve", lambda e: e.reciprocal(out=r32, in_=r32), reads=r32k, writes=["r32f"])
            qs = float(128 ** -0.5) if c < 16 else 1.0
            P.stt("dve", st_, acc, qs, r32, ALU.mult, ALU.mult, reads=["acc", "r32f"], writes=[stk])
            dst = C.qS[c] if c < 16 else C.kS[c - 16]
            P.dma("sp", dst, st_, reads=[stk], writes=[("qk", c)])
    for j in range(32):
        sg = sgb[j % 2]
        gate_chunk(C, W, wsrc, 80 + j, rot, hT, sg, ("sg", j % 2))
        P.dma("sp", C.sgS[j], sg, reads=[(("sg", j % 2), tb) for tb in range(NTB)], writes=[("sgS", j)])
    def epi_ba(tb, pst, pkey):
        P.copy("act", baf[:, tb * 512:(tb + 1) * 512], pst[:, :], reads=[pkey], writes=[("baf", tb)])
    gemm_cols(C, W, wsrc, 128, rot, epi_ba, hT)
    P.dma("sp", C.baS, baf, reads=[("baf", tb) for tb in range(NTB)], writes=["baS"])
    P.barrier()
    cross_attn(C, W, wsrc, rot, hT, 64, 80, li)
    P.barrier()
    dn_core(C, li)


def phase_out(C, li, xsrc):
    P, A = C.P, C.A
    yblk = A.view(0, [48, 512], BF16)
    xt = [A.view(49152 + i * 2048, [512], F32) for i in range(2)]
    xo = [A.view(49152 + 4096 + i * 2048, [512], F32) for i in range(2)]
    WB = 65536
    W = WStream(P, A, WB, nslots=2, slot_bytes=12288)
    rot = Rot([0, 1, 2, 3])
    xs = xsrc.rearrange("(oc p) t -> oc p t", p=128)
    xd = C.xS.rearrange("(oc p) t -> oc p t", p=128)
    cnt = 0
    for tb in range(NTB):
        P.dma("sp", yblk, C.yS[:, :, tb * 512:(tb + 1) * 512].rearrange("c p t -> p c t"), reads=[], writes=["yblk"])
        for oc in range(16):
            i2 = cnt % 2
            cnt += 1
            wkey, wv = W.get(("w_out", li, oc, tb), C.w_out[li, oc], [48, 128])
            P.dma("sp", xt[i2], xs[oc][:, tb * 512:(tb + 1) * 512], reads=[("xS", oc, tb)], writes=[("xt", i2)])
            b = rot.next()
            pst = C.psum[b]
            P.mmgroup(pst[:, :], [(wv[:, kc, :], yblk[:, kc, :]) for kc in range(48)],
                      reads=[wkey, "yblk"], writes=[("ps", b)])
            P.tt("dve", xo[i2], pst[:, :], xt[i2], ALU.add, reads=[("ps", b), ("xt", i2)], writes=[("xo", i2)])
            P.dma("sp", xd[oc][:, tb * 512:(tb + 1) * 512], xo[i2], reads=[("xo", i2)], writes=[("xS", oc, tb)])


def phase_final(C, xsrc):
    P, A = C.P, C.A
    xt = A.view(O_R1, [16, 512], F32)
    ot = A.view(0, [16, 512], F32)
    sq = [A.view(O_R2 + i * 2048, [512], F32) for i in range(2)]
    wcol = C.vecs[:, V_FIN:V_FIN + 16]
    srcv = xsrc.rearrange("(kc p) t -> p kc t", p=128)
    dstv = C.outT.rearrange("(kc p) t -> p kc t", p=128)
    ps = C.psum[6]
    for tb in range(NTB):
        P.dma("sp", xt, srcv[:, :, tb * 512:(tb + 1) * 512], reads=[], writes=["xt"])
        for kc in range(16):
            P.act(sq[kc % 2], xt[:, kc, :], AF.Square, reads=["xt"], writes=[("sq", kc % 2)])
            P.add("pe", (lambda kc=kc: lambda e: e.matmul(ps[:, :], C.onesm, sq[kc % 2],
                                                           start=(kc == 0), stop=(kc == 15)))(),
                  reads=[("sq", kc % 2)], writes=[("ps", 6)] if kc in (0, 15) else [])
        rsqrt_eps(C, C.rstd, ps[:, :], [("ps", 6)], "rstd")
        for kc in range(16):
            P.stt("dve", ot[:, kc, :], xt[:, kc, :], wcol[:, kc:kc + 1], C.rstd,
                  ALU.mult, ALU.mult, reads=["xt", "rstd"], writes=[("ot", kc)])
        P.dma("sp", dstv[:, :, tb * 512:(tb + 1) * 512], ot, reads=[("ot", kc) for kc in range(16)],
              writes=[("out", tb)])


def _blk(w, kc, nb):
    K, N = w.shape
    return np.ascontiguousarray(w.reshape(K // 128, 128, N // nb, nb).transpose(2, 1, 0, 3)).reshape(
        N // nb, 128, (K // 128) * nb)


def _col(v):
    return np.ascontiguousarray(v.reshape(-1, 128).T)


def prep_shared(inp):
    import ml_dtypes
    f = np.float32
    vecs = np.zeros((128, NVEC), f)
    for i in range(DEPTH):
        vecs[:, V_NORM + i * 16:V_NORM + (i + 1) * 16] = _col(inp["norm_w"][i])
        vecs[:, V_MNORM + i * 16:V_MNORM + (i + 1) * 16] = _col(inp["mem_norm_w"][i])
    vecs[:, V_FIN:V_FIN + 16] = _col(inp["final_norm_w"])
    for j in range(2):
        vecs[:, V_PSCALE + j * 32:V_PSCALE + (j + 1) * 32] = _col(inp["pool_scale"][j])
        cw = inp["dn_conv_w"][j]
        vecs[:, V_CONV + j * 320:V_CONV + (j + 1) * 320] = np.ascontiguousarray(
            cw.reshape(5, 64, 128).transpose(2, 1, 0)).reshape(128, 320)
        vecs[:, V_DNNORM + j] = inp["dn_norm_w"][j]
    vecs[:, V_EPS] = EPS
    vecs[:, V_ONE] = 1.0
    cmat = np.zeros((128, 3 * 128), f)
    cmat[:, 0:128] = np.eye(128, dtype=f)
    cmat[:, 128:256] = 1.0 / D
    cmat[:, 256:384] = 1.0
    cbf = np.eye(128, dtype=f).astype(ml_dtypes.bfloat16)
    redge = np.zeros((128, 64), f)
    for g, w in enumerate(POOL_WINDOWS):
        half = w // 2
        for t in range(half):
            redge[:, g * 16 + t] = 1.0 / (t + half)
        nr = half - 1
        for i in range(nr):
            t = T - nr + i
            redge[:, g * 16 + 8 + i] = 1.0 / (T - t + half)
    for j in range(2):
        dtb = inp["dn_dt_bias"][j]
        alog = inp["dn_a_log"][j]
        vecs[0:32, V_DTB + j] = dtb[1]
        vecs[32:64, V_DTB + j] = dtb[0]
        vecs[0:32, V_ALOG + j] = alog[1]
        vecs[32:64, V_ALOG + j] = alog[0]
    cmask = np.ones((128, T), f)
    cmask[:, ::64] = 0.0
    ii = np.arange(128)[:, None]
    jj = np.arange(128)[None, :]
    same = (ii // 64) == (jj // 64)
    masks = np.concatenate([(same & (jj < ii)), (same & (jj > ii)), (same & (jj <= ii)), (same & (jj >= ii))],
                           axis=1).astype(f)
    sh = {"vecs": vecs, "cmat": cmat, "cbf": cbf, "redge": redge, "cmask": cmask, "masks": masks}
    wkv = inp["w_kv_mem"]
    sh["w_kvk"] = np.stack([_blk(wkv[i][:, :2048], 16, 128) for i in range(DEPTH)])
    sh["w_kvv"] = np.stack([_blk(wkv[i][:, 2048:], 16, 256) for i in range(DEPTH)])
    sh["w_out"] = np.stack([_blk(inp["w_out"][i], 48, 128) for i in range(DEPTH)])
    sh["pw_in"] = np.stack([_blk(inp["pool_w_in"][j], 16, 128) for j in range(2)])
    sh["pw_g"] = np.stack([np.concatenate([_blk(inp["pool_w_group"][j][g], 8, 128) for g in range(4)])
                           for j in range(2)])
    dws = []
    for j in range(2):
        w = inp["dn_w_in"][j]
        ba = w[:, 16384:]
        w2 = np.concatenate([w[:, :16384], ba[:, 96:128], ba[:, 64:96], ba[:, 0:32], ba[:, 32:64]], axis=1)
        dws.append(_blk(w2, 16, 128))
    sh["dw_in"] = np.stack(dws)
    return sh


_NC_CACHE = {}


def kernel(**inp):
    inp = {k: np.asarray(v) for k, v in inp.items()}
    sh = prep_shared(inp)
    if "full" not in _NC_CACHE:
        _NC_CACHE["full"] = build_program()
    nc = _NC_CACHE["full"]
    in_maps = []
    for b in range(8):
        m = dict(sh)
        m["xT"] = np.ascontiguousarray(inp["x"][b].T)
        m["memT"] = np.ascontiguousarray(inp["mem"][b].T)
        in_maps.append(m)
    res = run_bass_kernel_spmd(nc, in_maps, core_ids=list(range(8)))
    out = np.stack([np.ascontiguousarray(r["outT"].T) for r in res.results])
    return out.astype(np.float32)


def dn_core(C, li):
    P, A = C.P, C.A
    j_ = li // 2
    off = [0]

    def alloc(shape, dt):
        n = int(np.prod(shape)) * (4 if dt == F32 else 2)
        n = (n + 31) // 32 * 32
        o = off[0]
        off[0] += n
        return A.view(o, shape, dt)

    BG = alloc([T], F32)
    TM1 = alloc([16, 128], F32)
    TMB = alloc([16, 64], F32)
    TMK = alloc([16, 64], F32)
    GTOT = alloc([32], F32)
    base = off[0]
    BAf = alloc([T], F32)
    G0 = alloc([T], F32)
    PF = alloc([T], F32)
    B2 = alloc([T], F32)
    cmask = alloc([T], F32)
    TMGD = alloc([16, 64], F32)
    assert off[0] <= O_CONST
    dtb = C.vecs[0:64, V_DTB + j_:V_DTB + j_ + 1]
    alog = C.vecs[0:64, V_ALOG + j_:V_ALOG + j_ + 1]
    one = C.vecs[0:64, V_ONE:V_ONE + 1]
    sm = C.sm
    P.dma("sp", BAf, C.baS, reads=[], writes=["BAf"])
    P.dma("sp", cmask, C.cmask_in, reads=[], writes=["cmask"])
    P.act(BG[64:128, :], BAf[64:128, :], AF.Sigmoid, reads=["BAf"], writes=["BGb"])
    P.act(sm[0:64, 12:13], alog, AF.Exp, reads=[], writes=["nA"])
    P.ts("dve", sm[0:64, 13:14], sm[0:64, 12:13], -1.0, None, ALU.mult, None, reads=["nA"], writes=["nA2"])
    P.ts("dve", B2[0:64, :], BAf[0:64, :], dtb, None, ALU.add, None, reads=["BAf"], writes=["B2"])
    P.ts("dve", PF[0:64, :], B2[0:64, :], -1.0, None, ALU.mult, None, reads=["B2"], writes=["PF"])
    P.tt("dve", PF[0:64, :], PF[0:64, :], B2[0:64, :], ALU.max, reads=["PF", "B2"], writes=["PF"])
    P.act(PF[0:64, :], PF[0:64, :], AF.Exp, reads=["PF"], writes=["PF"], scale=-1.0)
    P.act(PF[0:64, :], PF[0:64, :], AF.Ln, reads=["PF"], writes=["PF"], bias=one, scale=1.0)
    P.ts("dve", G0[0:64, :], B2[0:64, :], 0.0, None, ALU.max, None, reads=["B2"], writes=["G0"])
    P.tt("dve", G0[0:64, :], G0[0:64, :], PF[0:64, :], ALU.add, reads=["G0", "PF"], writes=["G0"])
    P.ts("dve", G0[0:64, :], G0[0:64, :], sm[0:64, 13:14], None, ALU.mult, None, reads=["G0", "nA2"], writes=["G0"])
    P.add("dve", lambda e: e.tensor_tensor_scan(out=PF[0:64, :], data0=cmask[0:64, :], data1=G0[0:64, :],
                                                 initial=0.0, op0=ALU.mult, op1=ALU.add),
          reads=["G0", "cmask", "PF"], writes=["PF"])
    PF3 = PF.rearrange("p (c k) -> p c k", k=64)
    BG3 = BG.rearrange("p (c k) -> p c k", k=64)
    B23 = B2.rearrange("p (c k) -> p c k", k=64)
    G03 = G0.rearrange("p (c k) -> p c k", k=64)
    P.memset("dve", GTOT, 0.0, writes=["GTOT"])
    P.copy("dve", GTOT[0:64, :], PF3[0:64, :, 63], reads=["PF", "GTOT"], writes=["GTOT"])
    P.copy("dve", BG[32:64, :], PF[32:64, :], reads=["PF"], writes=["BGf"])

    def gbc(r0, r1):
        return GTOT[r0:r1, :, None].broadcast_to([r1 - r0, 32, 64])
    P.tt("dve", B23[0:32], gbc(0, 32), PF3[0:32], ALU.subtract, reads=["GTOT", "PF"], writes=["B2"])
    P.tt("dve", BG3[0:32], B23[0:32], G03[0:32], ALU.add, reads=["B2", "G0"], writes=["BGr"])
    P.tt("dve", B23[0:64], gbc(0, 64), BG3[0:64], ALU.subtract, reads=["GTOT", "BGr", "BGf", "B2"], writes=["B2"])
    bgk = ["BGb", "BGf", "BGr"]
    for q4 in range(4):
        pst = C.psum[q4 % 2]
        for k in range(4):
            tau = q4 * 4 + k
            P.transpose(pst[:, k * 128:(k + 1) * 128], BG[:, tau * 128:(tau + 1) * 128], C.identf,
                        reads=bgk, writes=[("ps", q4 % 2)])
        P.copy("dve", TM1[:, q4 * 4:(q4 + 1) * 4, :], pst[:, :].rearrange("p (a b) -> p a b", a=4),
               reads=[("ps", q4 % 2)], writes=[("TM1", q4)])
    for q8 in range(2):
        pst = C.psum[2 + q8]
        for k in range(8):
            tau = q8 * 8 + k
            P.transpose(pst[:, k * 64:(k + 1) * 64], B2[0:64, tau * 128:(tau + 1) * 128], C.identf[0:64, 0:64],
                        reads=["B2"], writes=[("ps", 2 + q8)])
        P.copy("dve", TMGD[:, q8 * 8:(q8 + 1) * 8, :], pst[:, :].rearrange("p (a b) -> p a b", a=8),
               reads=[("ps", 2 + q8)], writes=[("TMGD", q8)])
    tm1k = [("TM1", q) for q in range(4)]
    P.act(TMK, TMGD, AF.Exp, reads=[("TMGD", 0), ("TMGD", 1)], writes=["TMK"])
    P.act(TMB, TM1[:, :, 0:64], AF.Exp, reads=tm1k, writes=["TMB"])
    P.tt("dve", TMB[:, :, 0:32], TMB[:, :, 0:32], TM1[:, :, 96:128], ALU.mult, reads=["TMB"] + tm1k, writes=["TMB"])
    P.tt("dve", TMB[:, :, 32:64], TMB[:, :, 32:64], TM1[:, :, 64:96], ALU.mult, reads=["TMB"] + tm1k, writes=["TMB"])
    P.barrier()

    off[0] = base
    qT = alloc([T], BF16)
    kT = alloc([T], BF16)
    Ktm = alloc([16, 128], BF16)
    vT = alloc([T], BF16)
    Vtm = alloc([16, 128], BF16)
    nGMs = [alloc([4, 128], F32) for _ in range(2)]
    KQMt = [alloc([4, 128], F32) for _ in range(2)]
    t1 = alloc([4, 128], F32)
    t2 = alloc([4, 128], F32)
    tmp = alloc([4, 128], F32)
    egr = alloc([512], F32)
    Qb = [alloc([4, 128], BF16) for _ in range(2)]
    Pb = [alloc([4, 128], BF16) for _ in range(2)]
    Rb = [alloc([4, 128], BF16) for _ in range(2)]
    bV = alloc([4, 128], BF16)
    Kp = alloc([4, 128], BF16)
    U = [alloc([16, 128], F32) for _ in range(2)]
    WT = [alloc([T], BF16) for _ in range(2)]
    QgT = [alloc([T], BF16) for _ in range(2)]
    Kd = [alloc([16, 128], BF16) for _ in range(2)]
    AT = [alloc([16, 128], BF16) for _ in range(2)]
    S = [alloc([128], F32) for _ in range(2)]
    Sbf = [alloc([128], BF16) for _ in range(2)]
    egt = [alloc([32], F32) for _ in range(2)]
    vnew = [alloc([128], BF16) for _ in range(2)]
    sel = [alloc([128], F32) for _ in range(4)]
    oT = [alloc([T], F32) for _ in range(2)]
    sg = alloc([T], BF16)
    yst = alloc([T], BF16)
    osq = alloc([T], F32)
    rs = alloc([T], F32)
    assert off[0] <= O_CONST, off[0]
    Ms = [C.masks[:, 0, :], C.masks[:, 1, :]]
    Mt = [C.masks[:, 2, :], C.masks[:, 3, :]]

    def b4(ap):
        return ap[:, None, :].broadcast_to([128, 4, 128])

    def p4(pst):
        return pst[:, :].rearrange("p (a b) -> p a b", a=4)
    nwcol = C.vecs[:, V_DNNORM + j_:V_DNNORM + j_ + 1]
    rot = Rot([4, 5, 6, 7])

    for hv in range(32):
        h = hv // 2
        if hv % 2 == 0:
            P.dma("sp", qT, C.qS[h], reads=[], writes=["qT"])
            P.dma("sp", kT, C.kS[h], reads=[], writes=["kT"])
            for q8 in range(2):
                b = rot.next()
                ptb = C.psum[b][:, :].bitcast(BF16)
                for k in range(8):
                    tau = q8 * 8 + k
                    P.transpose(ptb[:, k * 128:(k + 1) * 128], kT[:, tau * 128:(tau + 1) * 128], C.identb,
                                reads=["kT"], writes=[("ps", b)])
                P.copy("act", Ktm[:, q8 * 8:(q8 + 1) * 8, :], ptb.rearrange("p (a b) -> p a b", a=8),
                       reads=[("ps", b)], writes=[("Ktm", q8)])
        P.dma("sp", vT, C.vS[hv], reads=[], writes=["vT"])
        P.dma("sp", sg, C.sgS[hv], reads=[], writes=["sg"])
        for q8 in range(2):
            b = rot.next()
            ptb = C.psum[b][:, :].bitcast(BF16)
            for k in range(8):
                tau = q8 * 8 + k
                P.transpose(ptb[:, k * 128:(k + 1) * 128], vT[:, tau * 128:(tau + 1) * 128], C.identb,
                            reads=["vT"], writes=[("ps", b)])
            P.copy("act", Vtm[:, q8 * 8:(q8 + 1) * 8, :], ptb.rearrange("p (a b) -> p a b", a=8),
                   reads=[("ps", b)], writes=[("Vtm", q8)])
        ktmk = [("Ktm", 0), ("Ktm", 1)]
        vtmk = [("Vtm", 0), ("Vtm", 1)]
        rows = [32 + hv, hv, 64 + hv, 96 + hv]
        for i, r in enumerate(rows):
            P.ts("dve", sel[i], C.ones1, C.identf[:, r:r + 1], None, ALU.mult, None, reads=[], writes=[("sel", i)])
        for d in range(2):
            b = rot.next()
            pst = C.psum[b]
            P.add("pe", lambda e, pst=pst, d=d: e.matmul(pst[:, 0:32], sel[d], GTOT, start=True, stop=True),
                  reads=[("sel", d)], writes=[("ps", b)])
            P.act(egt[d], pst[:, 0:32], AF.Exp, reads=[("ps", b)], writes=[("egt", d)])
        cgc = [32 + hv, hv]
        cbe = [64 + hv, 96 + hv]
        for tg in range(4):
            tsl = slice(tg * 512, (tg + 1) * 512)
            psG, psKQ = C.psum[0], C.psum[1]
            for k in range(4):
                tau = tg * 4 + k
                sl = slice(tau * 128, (tau + 1) * 128)
                P.add("pe", lambda e, k=k, sl=sl: e.matmul(psG[:, k * 128:(k + 1) * 128], kT[:, sl], kT[:, sl],
                                                           start=True, stop=True),
                      reads=["kT"], writes=[("ps", 0)])
            for k in range(4):
                tau = tg * 4 + k
                sl = slice(tau * 128, (tau + 1) * 128)
                P.add("pe", lambda e, k=k, sl=sl: e.matmul(psKQ[:, k * 128:(k + 1) * 128], kT[:, sl], qT[:, sl],
                                                           start=True, stop=True),
                      reads=["kT", "qT"], writes=[("ps", 1)])
            for d in range(2):
                P.stt("dve", nGMs[d], p4(psG), -1.0, b4(Ms[d]), ALU.mult, ALU.mult, reads=[("ps", 0)],
                      writes=[("nGMs", d)])
                P.tt("dve", KQMt[d], p4(psKQ), b4(Mt[1 - d]), ALU.mult, reads=[("ps", 1)], writes=[("KQMt", d)])
            for d in range(2):
                psRg, psRb = C.psum[2], C.psum[3]
                P.add("pe", lambda e, d=d, tsl=tsl: e.matmul(psRg[:, :], sel[d], BG[:, tsl], start=True, stop=True),
                      reads=[("sel", d)], writes=[("ps", 2)])
                P.add("pe", lambda e, d=d, tsl=tsl: e.matmul(psRb[:, :], sel[2 + d], BG[:, tsl], start=True, stop=True),
                      reads=[("sel", 2 + d)], writes=[("ps", 3)])
                for k in range(4):
                    tau = tg * 4 + k
                    gci = TM1[:, tau, cgc[d]:cgc[d] + 1]
                    P.ts("dve", t1[:, k, :], psRg[:, k * 128:(k + 1) * 128], gci, 0.0, ALU.subtract, ALU.max,
                         reads=[("ps", 2)], writes=[("t1", k)])
                    P.ts("dve", t2[:, k, :], psRg[:, k * 128:(k + 1) * 128], gci, 0.0, ALU.subtract, ALU.min,
                         reads=[("ps", 2)], writes=[("t2", k)])
                t1k = [("t1", k) for k in range(4)]
                t2k = [("t2", k) for k in range(4)]
                P.act(t1, t1, AF.Exp, reads=t1k, writes=["D"], scale=-1.0)
                P.act(t2, t2, AF.Exp, reads=t2k, writes=["DT"])
                P.act(egr, psRg[:, :], AF.Exp, reads=[("ps", 2)], writes=["egr"])
                for k in range(4):
                    tau = tg * 4 + k
                    bei = TM1[:, tau, cbe[d]:cbe[d] + 1]
                    P.stt("dve", Qb[0][:, k, :], t1[:, k, :], bei, nGMs[d][:, k, :], ALU.mult, ALU.mult,
                          reads=["D", ("nGMs", d)], writes=[("Q", 0, k)])
                P.tt("dve", tmp, t2, p4(psRb), ALU.mult, reads=["DT", ("ps", 3)], writes=["tmp"])
                P.tt("dve", Pb[0], tmp, nGMs[1 - d], ALU.mult, reads=["tmp", ("nGMs", 1 - d)], writes=[("P", 0)])
                P.tt("dve", Rb[0], Pb[0], b4(C.identf), ALU.add, reads=[("P", 0)], writes=[("R", 0)])
                P.tt("dve", AT[d][:, tg * 4:(tg + 1) * 4, :], t2, KQMt[d], ALU.mult, reads=["DT", ("KQMt", d)],
                     writes=[("AT", d, tg)])
                P.tt("dve", QgT[d][:, tsl], qT[:, tsl], egr, ALU.mult, reads=["qT", "egr"], writes=[("QgT", d, tg)])
                qk = [("Q", 0, k) for k in range(4)]
                cur = 0
                rc = 0
                for m in range(1, 6):
                    nxt = 1 - cur
                    bA = rot.next()
                    psA = C.psum[bA]

                    def fq(e, psA=psA, cur=cur):
                        ins = None
                        for k in range(4):
                            ins = e.matmul(psA[:, k * 128:(k + 1) * 128], Pb[cur][:, k, :], Qb[cur][:, k, :],
                                           start=True, stop=True)
                        return ins
                    P.add("pe", fq, reads=qk + [("P", cur)], writes=[("ps", bA)])
                    if m < 5:
                        bB = rot.next()
                        psB = C.psum[bB]

                        def fp(e, psB=psB, cur=cur):
                            ins = None
                            for k in range(4):
                                ins = e.matmul(psB[:, k * 128:(k + 1) * 128], Qb[cur][:, k, :], Pb[cur][:, k, :],
                                               start=True, stop=True)
                            return ins
                        P.add("pe", fp, reads=qk + [("P", cur)], writes=[("ps", bB)])
                    P.copy("act", Qb[nxt], p4(psA), reads=[("ps", bA)], writes=[("Q", nxt)])
                    if m < 5:
                        P.copy("act", Pb[nxt], p4(psB), reads=[("ps", bB)], writes=[("P", nxt)])
                    bC = rot.next()
                    psC = C.psum[bC]

                    def fr(e, psC=psC, nxt=nxt, rc=rc):
                        ins = None
                        for k in range(4):
                            ins = e.matmul(psC[:, k * 128:(k + 1) * 128], Qb[nxt][:, k, :], Rb[rc][:, k, :],
                                           start=True, stop=True)
                        return ins
                    P.add("pe", fr, reads=[("Q", nxt), ("R", rc)], writes=[("ps", bC)])
                    P.tt("dve", Rb[1 - rc], Rb[rc], p4(psC), ALU.add, reads=[("R", rc), ("ps", bC)],
                         writes=[("R", 1 - rc)])
                    rc = 1 - rc
                    cur = nxt
                    qk = [("Q", cur)]
                TT = Rb[rc]
                ttk = ("R", rc)
                for k in range(4):
                    tau = tg * 4 + k
                    bei = TM1[:, tau, cbe[d]:cbe[d] + 1]
                    P.ts("dve", bV[:, k, :], Vtm[:, tau, :], bei, None, ALU.mult, None, reads=vtmk, writes=[("bV", k)])
                    P.ts("dve", Kp[:, k, :], Ktm[:, tau, :], TMB[:, tau, cgc[d]:cgc[d] + 1], None, ALU.mult, None,
                         reads=ktmk, writes=[("Kp", k)])
                    P.ts("dve", Kd[d][:, tau, :], Ktm[:, tau, :], TMK[:, tau, cgc[d]:cgc[d] + 1], None, ALU.mult, None,
                         reads=ktmk, writes=[("Kd", d, tau)])
                bU = rot.next()
                psU = C.psum[bU]

                def fu(e, psU=psU, TT=TT):
                    ins = None
                    for k in range(4):
                        ins = e.matmul(psU[:, k * 128:(k + 1) * 128], TT[:, k, :], bV[:, k, :], start=True, stop=True)
                    return ins
                P.add("pe", fu, reads=[ttk] + [("bV", k) for k in range(4)], writes=[("ps", bU)])
                P.copy("act", U[d][:, tg * 4:(tg + 1) * 4, :], p4(psU), reads=[("ps", bU)], writes=[("U", d, tg)])
                bW = rot.next()
                psW = C.psum[bW]

                def fw(e, psW=psW, TT=TT):
                    ins = None
                    for k in range(4):
                        ins = e.matmul(psW[:, k * 128:(k + 1) * 128], Kp[:, k, :], TT[:, k, :], start=True, stop=True)
                    return ins
                P.add("pe", fw, reads=[ttk] + [("Kp", k) for k in range(4)], writes=[("ps", bW)])
                P.copy("act", WT[d][:, tsl], psW[:, :], reads=[("ps", bW)], writes=[("WT", d, tg)])
        for d in range(2):
            P.memset("dve", S[d], 0.0, writes=[("S", d)])
            P.memset("pool", Sbf[d], 0.0, writes=[("Sbf", d)])
        for step in range(32):
            for d in range(2):
                c = step if d == 0 else 31 - step
                tau, hf = c // 2, c % 2
                tg = tau // 4
                r0 = 64 * hf
                psv, pso, psS = C.psum[2 + d], C.psum[0 + d], C.psum[4 + d]
                P.add("pe", lambda e, d=d, tau=tau, psv=psv: e.matmul(
                    psv[:, 0:128], WT[d][:, tau * 128:(tau + 1) * 128], Sbf[d], start=True, stop=True),
                    reads=[("WT", d, tg), ("Sbf", d)], writes=[("ps", 2 + d)])
                P.tt("dve", vnew[d][r0:r0 + 64, :], U[d][r0:r0 + 64, tau, :], psv[r0:r0 + 64, 0:128], ALU.subtract,
                     reads=[("U", d, tg), ("ps", 2 + d)], writes=[("vnew", d)])
                cs = (c % 8) * 64

                def fo(e, d=d, c=c, tau=tau, r0=r0, cs=cs, pso=pso):
                    e.matmul(pso[:, cs:cs + 64], Sbf[d], QgT[d][:, c * 64:(c + 1) * 64], start=True, stop=False)
                    return e.matmul(pso[:, cs:cs + 64], vnew[d][r0:r0 + 64, :], AT[d][r0:r0 + 64, tau, r0:r0 + 64],
                                    start=False, stop=True)
                P.add("pe", fo, reads=[("Sbf", d), ("QgT", d, tg), ("vnew", d), ("AT", d, tg)], writes=[("ps", 0 + d)])
                P.add("pe", lambda e, d=d, tau=tau, r0=r0, psS=psS: e.matmul(
                    psS[:, 0:128], Kd[d][r0:r0 + 64, tau, :], vnew[d][r0:r0 + 64, :], start=True, stop=True),
                    reads=[("Kd", d, tau), ("vnew", d)], writes=[("ps", 4 + d)])
                P.stt("dve", S[d], S[d], egt[d][:, c:c + 1], psS[:, 0:128], ALU.mult, ALU.add,
                      reads=[("S", d), ("egt", d), ("ps", 4 + d)], writes=[("S", d)])
                P.copy("act", Sbf[d], S[d], reads=[("S", d)], writes=[("Sbf", d)])
                last = (c % 8 == 7) if d == 0 else (c % 8 == 0)
                if last:
                    g8 = c // 8
                    P.copy("act", oT[d][:, g8 * 512:(g8 + 1) * 512], pso[:, :], reads=[("ps", 0 + d)],
                           writes=[("oT", d, g8)])
        otk = [("oT", d, g8) for d in range(2) for g8 in range(4)]
        P.tt("dve", oT[0], oT[0], oT[1], ALU.add, reads=otk, writes=["o"])
        P.act(osq, oT[0], AF.Square, reads=["o"], writes=["osq"])
        for tb in range(NTB):
            b = rot.next()
            pst = C.psum[b]
            P.add("pe", lambda e, pst=pst, tb=tb: e.matmul(pst[:, :], C.ones1, osq[:, tb * 512:(tb + 1) * 512],
                                                          start=True, stop=True),
                  reads=["osq"], writes=[("ps", b)])
            P.act(rs[:, tb * 512:(tb + 1) * 512], pst[:, :], AF.Sqrt, reads=[("ps", b)], writes=[("rs", tb)],
                  bias=C.vecs[:, V_EPS:V_EPS + 1], scale=1.0 / 128)
        rsk = [("rs", tb) for tb in range(NTB)]
        P.add("dve", lambda e: e.reciprocal(out=rs, in_=rs), reads=rsk, writes=["rsf"])
        P.stt("dve", osq, oT[0], nwcol, rs, ALU.mult, ALU.mult, reads=["o", "rsf", "osq"], writes=["on"])
        P.tt("dve", yst, osq, sg, ALU.mult, reads=["on", "sg"], writes=["yst"])
        P.dma("sp", C.yS[hv], yst, reads=["yst"], writes=[("yS", hv)])
```

```python
import numpy as np
from contextlib import ExitStack
import concourse.bass as bass
import concourse.mybir as mybir
from concourse.bass_utils import run_bass_kernel_spmd

F32 = mybir.dt.float32
BF16 = mybir.dt.bfloat16
AF = mybir.ActivationFunctionType
ALU = mybir.AluOpType
AX = mybir.AxisListType

D = 2048
T = 2048
NMEM = 256
DEPTH = 4
EPS = 1e-6
NTB = T // 512
KC = D // 128
POOL_WINDOWS = (2, 4, 8, 16)
DEBUG = {}


class _Op:
    __slots__ = ("eng", "fn", "deps", "is_dma", "has_dep", "sig", "idx")

    def __init__(self, eng, fn, is_dma):
        self.eng = eng
        self.fn = fn
        self.deps = set()
        self.is_dma = is_dma
        self.has_dep = False
        self.sig = 0
        self.idx = 0


class Prog:
    CE = ("pe", "dve", "act", "pool")
    QE = ("sp", "act", "pool")
    ALLE = ("pe", "dve", "act", "pool", "sp")
    NSLOT = 8

    def __init__(self, nc):
        self.nc = nc
        self.stream = {e: [] for e in self.ALLE}
        self.last_w = {}
        self.rd_c = {}
        self.rd_d = {}
        self.ndma = {q: 0 for q in self.QE}
        self.dmas_since_barrier = []

    rec = None

    def add(self, eng, fn, reads=(), writes=(), dma=False):
        if self.rec is not None:
            self.rec.append((eng, fn, tuple(reads), tuple(writes), dma))
            return None
        op = _Op(eng, fn, dma)
        deps = op.deps
        for r in reads:
            w = self.last_w.get(r)
            if w is not None:
                deps.add(w)
        for r in writes:
            w = self.last_w.get(r)
            if w is not None:
                deps.add(w)
            for o in self.rd_c.get(r, {}).values():
                deps.add(o)
            for o in self.rd_d.get(r, ()):
                deps.add(o)
        if eng == "pe":
            for d_ in [d_ for d_ in deps if d_.eng == "pe" and not d_.is_dma]:
                deps.discard(d_)
        for r in writes:
            self.last_w[r] = op
            self.rd_c[r] = {}
            self.rd_d[r] = []
        for r in reads:
            if dma:
                self.rd_d.setdefault(r, []).append(op)
            else:
                self.rd_c.setdefault(r, {})[eng] = op
        if dma:
            op.idx = self.ndma[eng]
            self.ndma[eng] += 1
            self.dmas_since_barrier.append(op)
        self.stream[eng].append(op)
        return op

    def dma(self, q, out, in_, reads, writes):
        return self.add(q, lambda e: e.dma_start(out=out, in_=in_), reads, writes, dma=True)

    def act(self, out, in_, func, reads, writes, **kw):
        return self.add("act", lambda e: e.activation(out=out, in_=in_, func=func, **kw), reads, writes)

    def tt(self, eng, out, in0, in1, op, reads, writes):
        return self.add(eng, lambda e: e.tensor_tensor(out=out, in0=in0, in1=in1, op=op), reads, writes)

    def ts(self, eng, out, in0, s1, s2, op0, op1, reads, writes):
        if op1 is None:
            return self.add(eng, lambda e: e.tensor_scalar(out=out, in0=in0, scalar1=s1, scalar2=None, op0=op0),
                            reads, writes)
        return self.add(eng, lambda e: e.tensor_scalar(out=out, in0=in0, scalar1=s1, scalar2=s2, op0=op0, op1=op1),
                        reads, writes)

    def stt(self, eng, out, in0, scalar, in1, op0, op1, reads, writes):
        return self.add(eng, lambda e: e.scalar_tensor_tensor(out=out, in0=in0, scalar=scalar, in1=in1,
                                                              op0=op0, op1=op1), reads, writes)

    def copy(self, eng, out, in_, reads, writes):
        if eng == "act":
            return self.add("act", lambda e: e.copy(out=out, in_=in_), reads, writes)
        return self.add(eng, lambda e: e.tensor_copy(out=out, in_=in_), reads, writes)

    def memset(self, eng, ap, val, writes):
        return self.add(eng, lambda e: e.memset(ap, val), (), writes)

    def mmgroup(self, out, pairs, reads, writes):
        n = len(pairs)

        def fn(e):
            ins = None
            for i, (l, r) in enumerate(pairs):
                ins = e.matmul(out, l, r, start=(i == 0), stop=(i == n - 1))
            return ins
        return self.add("pe", fn, reads, writes)

    def transpose(self, out, in_, ident, reads, writes):
        return self.add("pe", lambda e: e.transpose(out, in_, ident), reads, writes)

    def barrier(self):
        tails = [self.stream[e][-1] for e in self.CE if self.stream[e] and not self.stream[e][-1].is_dma]
        for e in self.CE:
            for o in reversed(self.stream[e]):
                if not o.is_dma:
                    tails.append(o)
                    break
        b = _Op("sp", lambda e: e.nop(), False)
        b.deps = set(tails) | set(self.dmas_since_barrier)
        self.stream["sp"].append(b)
        self.dmas_since_barrier = []
        for e in self.CE:
            o = _Op(e, lambda en: en.nop(), False)
            o.deps = {b}
            self.stream[e].append(o)
        self.last_w = {}
        self.rd_c = {}
        self.rd_d = {}
        return b

    def emit(self):
        nc = self.nc
        for ops in self.stream.values():
            for op in ops:
                for d_ in op.deps:
                    d_.has_dep = True
        for e, ops in self.stream.items():
            c = 0
            for op in ops:
                if (not op.is_dma) and op.has_dep:
                    c += 1
                    op.sig = c
        NS = self.NSLOT
        with ExitStack() as st:
            sems = {e: st.enter_context(nc.semaphore(f"s_{e}")) for e in self.ALLE}
            dsems = {q: [st.enter_context(nc.semaphore(f"d_{q}{i}")) for i in range(NS)] for q in self.QE}
            block = st.enter_context(nc.Block())

            def run(ename, eng):
                waited = {e: 0 for e in self.ALLE}
                waited_d = {}
                for op in self.stream[ename]:
                    need = {}
                    needd = {}
                    for d_ in op.deps:
                        if d_.is_dma:
                            k = (d_.eng, d_.idx % NS)
                            rnd = d_.idx // NS + 1
                            if waited_d.get(k, 0) < rnd:
                                needd[k] = max(needd.get(k, 0), rnd)
                        else:
                            if waited[d_.eng] < d_.sig:
                                need[d_.eng] = max(need.get(d_.eng, 0), d_.sig)
                    if op.is_dma and op.idx >= NS:
                        k = (ename, op.idx % NS)
                        rnd = op.idx // NS
                        if waited_d.get(k, 0) < rnd:
                            needd[k] = max(needd.get(k, 0), rnd)
                    for e2, v in need.items():
                        eng.wait_ge(sems[e2], v)
                        waited[e2] = v
                    for (q, slot), r in needd.items():
                        eng.wait_ge(dsems[q][slot], 16 * r)
                        waited_d[(q, slot)] = r
                    ins = op.fn(eng)
                    if op.is_dma:
                        ins.then_inc(dsems[ename][op.idx % NS], 16)
                    elif op.has_dep:
                        ins.then_inc(sems[ename], 1)

            @block.tensor
            def _(e):
                run("pe", e)

            @block.vector
            def _(e):
                run("dve", e)

            @block.scalar
            def _(e):
                run("act", e)

            @block.gpsimd
            def _(e):
                run("pool", e)

            @block.sync
            def _(e):
                run("sp", e)


class Arena:
    def __init__(self, t):
        self.t = t

    def view(self, off, shape, dt):
        n = int(np.prod(shape))
        esz = 4 if dt == F32 else 2
        assert off % 4 == 0 and (n * esz) % 4 == 0
        ap = self.t[:, off // 4:(off + n * esz) // 4]
        if dt != F32:
            ap = ap.bitcast(dt)
        if len(shape) == 2:
            ap = ap.rearrange("p (a b) -> p a b", a=shape[0])
        elif len(shape) == 3:
            ap = ap.rearrange("p (a b c) -> p a b c", a=shape[0], b=shape[1])
        return ap


ARENA_BYTES = 207 * 1024
O_HT = 0
O_WB = 65536
O_R1 = O_WB + 3 * 8192
O_R2 = O_R1 + 32768
O_SG = O_R2 + 33792
O_KV = O_SG + 16384
O_CONST = O_KV + 24576
C_IDB = O_CONST
C_IDF = C_IDB + 256
C_ONE = C_IDF + 512
C_VEC = C_ONE + 512
NVEC = 864
C_RSTD = C_VEC + 4 * 864
C_MRSTD = C_RSTD + 2048
C_SM = C_MRSTD + 1024
C_ONE1 = C_SM + 64
C_MASK = C_ONE1 + 512
C_END = C_MASK + 2048
assert NVEC <= 864 and C_END <= ARENA_BYTES, (NVEC, C_END)

V_NORM = 0
V_MNORM = 64
V_FIN = 128
V_PSCALE = 144
V_CONV = 208
V_DNNORM = 848
V_EPS = 850
V_ONE = 851
V_DTB = 852
V_ALOG = 854


class Ctx:
    pass


class WStream:
    def __init__(self, P, A, base, nslots=3, slot_bytes=8192):
        self.P, self.A, self.base, self.n, self.sb = P, A, base, nslots, slot_bytes
        self.i = 0
        self.cache = {}
        self.owner = [None] * nslots

    def get(self, key, src, shape):
        if key in self.cache:
            return self.cache[key]
        s = self.i % self.n
        self.i += 1
        if self.owner[s] is not None:
            del self.cache[self.owner[s]]
        self.owner[s] = key
        view = self.A.view(self.base + s * self.sb, shape, BF16)
        fshape = list(src.shape[1:])
        flat = self.A.view(self.base + s * self.sb, fshape, BF16)
        assert int(np.prod(fshape)) == int(np.prod(shape)), (fshape, shape)
        self.P.dma("pool", flat, src, reads=[], writes=[("wb", self.base, s)])
        self.cache[key] = (("wb", self.base, s), view)
        return self.cache[key]


class Rot:
    def __init__(self, items):
        self.items = items
        self.i = 0

    def next(self):
        x = self.items[self.i % len(self.items)]
        self.i += 1
        return x


def build_program(layers=(0, 1, 2, 3), final=True, dbg_out=None):
    nc = bass.Bass("TRN2", target_bir_lowering=False)
    C = Ctx()
    C.nc = nc

    def din(name, shape, dt=F32):
        return nc.dram_tensor(name, shape, dt, kind="ExternalInput").ap()

    C.xT_in = din("xT", [D, T])
    C.memT_in = din("memT", [D, NMEM])
    C.vecs_in = din("vecs", [128, NVEC])
    C.cmat_in = din("cmat", [128, 3 * 128])
    C.cbf_in = din("cbf", [128, 128], BF16)
    C.redge_in = din("redge", [128, 64])
    C.cmask_in = din("cmask", [128, T])
    C.masks_in = din("masks", [128, 4 * 128])
    C.w_kvk = din("w_kvk", [DEPTH, 16, 128, 16 * 128])
    C.w_kvv = din("w_kvv", [DEPTH, 8, 128, 16 * 256])
    C.w_out = din("w_out", [DEPTH, 16, 128, 48 * 128])
    C.pw_in = din("pw_in", [2, 96, 128, 16 * 128])
    C.pw_g = din("pw_g", [2, 32, 128, 8 * 128])
    C.dw_in = din("dw_in", [2, 129, 128, 16 * 128])
    C.outT = nc.dram_tensor("outT", [D, T], F32, kind="ExternalOutput").ap()
    C.xS = nc.dram_tensor("xS", [D, T], F32).ap()
    skind = "ExternalOutput" if DEBUG.get("dump") else "Internal"
    C.yS = nc.dram_tensor("yS", [48, 128, T], BF16, kind=skind).ap()
    C.qS = nc.dram_tensor("qS", [16, 128, T], BF16, kind=skind).ap()
    C.kS = nc.dram_tensor("kS", [16, 128, T], BF16, kind=skind).ap()
    C.vS = nc.dram_tensor("vS", [32, 128, T], BF16, kind=skind).ap()
    C.sgS = nc.dram_tensor("sgS", [32, 128, T], BF16, kind=skind).ap()
    C.baS = nc.dram_tensor("baS", [128, T], F32, kind=skind).ap()

    with ExitStack() as st:
        arena_t = st.enter_context(nc.sbuf_tensor("arena", [128, ARENA_BYTES // 4], F32))
        C.A = A = Arena(arena_t)
        C.psum = [st.enter_context(nc.psum_tensor(f"ps{i}", [128, 512], F32)) for i in range(8)]
        C.P = P = Prog(nc)

        C.identb = A.view(C_IDB, [128], BF16)
        C.identf = A.view(C_IDF, [128], F32)
        C.onesm = A.view(C_ONE, [128], F32)
        C.vecs = A.view(C_VEC, [NVEC], F32)
        C.rstd = A.view(C_RSTD, [512], F32)
        C.sm = A.view(C_SM, [16], F32)
        C.redge = A.view(C_MRSTD, [64], F32)
        P.dma("sp", C.identb, C.cbf_in, [], ["c0"])
        P.dma("sp", C.identf, C.cmat_in[:, 0:128], [], ["c1"])
        P.dma("sp", C.onesm, C.cmat_in[:, 128:256], [], ["c2"])
        P.dma("sp", C.vecs, C.vecs_in, [], ["c3"])
        P.dma("sp", C.redge, C.redge_in, [], ["c4"])
        C.ones1 = A.view(C_ONE1, [128], F32)
        C.masks = A.view(C_MASK, [4, 128], F32)
        P.dma("sp", C.ones1, C.cmat_in[:, 256:384], [], ["c5"])
        P.dma("sp", C.masks, C.masks_in.rearrange("p (a b) -> p a b", a=4), [], ["c6"])
        P.barrier()

        xsrc = C.xT_in
        for li in layers:
            if DEBUG.get("dn_only"):
                dn_core(C, li)
                continue
            phase_norm(C, xsrc, C.vecs[:, V_NORM + li * 16: V_NORM + li * 16 + 16])
            phase_memkv(C, li)
            P.barrier()
            if li % 2 == 0:
                phase_pool(C, li)
            else:
                phase_dn(C, li)
            P.barrier()
            phase_out(C, li, xsrc)
            P.barrier()
            xsrc = C.xS
        if final:
            phase_final(C, xsrc)
        else:
            for oc in range(16):
                P.dma("sp", C.outT[oc * 128:(oc + 1) * 128, :], xsrc[oc * 128:(oc + 1) * 128, :], reads=[], writes=[("o", oc)])
        P.barrier()
        P.emit()
    return nc


def rsqrt_eps(C, out, in_, reads, wkey):
    P = C.P
    P.act(out, in_, AF.Ln, reads=list(reads), writes=[wkey], bias=C.vecs[:, V_EPS:V_EPS + 1], scale=1.0)
    P.act(out, out, AF.Exp, reads=[wkey], writes=[wkey], scale=-0.5)


def phase_norm(C, src, wcol):
    P, A = C.P, C.A
    hT = A.view(O_HT, [16, T], BF16)
    xt = A.view(O_R1, [16, 512], F32)
    sq = [A.view(O_R2 + i * 2048, [512], F32) for i in range(2)]
    srcv = src.rearrange("(kc p) t -> p kc t", p=128)
    ps = C.psum[6]
    for tb in range(NTB):
        P.dma("sp", xt, srcv[:, :, tb * 512:(tb + 1) * 512], reads=[], writes=["xt"])
        for kc in range(16):
            P.act(sq[kc % 2], xt[:, kc, :], AF.Square, reads=["xt"], writes=[("sq", kc % 2)])
            P.add("pe", (lambda kc=kc: lambda e: e.matmul(ps[:, :], C.onesm, sq[kc % 2],
                                                           start=(kc == 0), stop=(kc == 15)))(),
                  reads=[("sq", kc % 2)], writes=[("ps", 6)] if kc in (0, 15) else [])
        rsqrt_eps(C, C.rstd, ps[:, :], [("ps", 6)], "rstd")
        for kc in range(16):
            eng = "dve"
            P.stt(eng, hT[:, kc, tb * 512:(tb + 1) * 512], xt[:, kc, :], wcol[:, kc:kc + 1], C.rstd,
                  ALU.mult, ALU.mult, reads=["xt", "rstd"], writes=[("hT", tb, kc)])


def phase_memkv(C, li):
    P, A = C.P, C.A
    memf = A.view(O_R2 + 4096, [16, NMEM], F32)
    sq = [A.view(O_R2 + 4096 + 16384 + i * 1024, [NMEM], F32) for i in range(2)]
    mrstd = A.view(O_R2 + 4096 + 16384 + 2048, [NMEM], F32)
    kT = A.view(O_KV, [16, NMEM], BF16)
    v = A.view(O_KV + 8192, [2, D], BF16)
    memn = A.view(O_KV + 16384, [16, NMEM], BF16)
    wcol = C.vecs[:, V_MNORM + li * 16: V_MNORM + li * 16 + 16]
    ps = C.psum[7]
    P.dma("sp", memf, C.memT_in.rearrange("(kc p) m -> p kc m", p=128), reads=[], writes=["memf"])
    for kc in range(16):
        P.act(sq[kc % 2], memf[:, kc, :], AF.Square, reads=["memf"], writes=[("msq", kc % 2)])
        P.add("pe", (lambda kc=kc: lambda e: e.matmul(ps[:, 0:NMEM], C.onesm, sq[kc % 2],
                                                       start=(kc == 0), stop=(kc == 15)))(),
              reads=[("msq", kc % 2)], writes=[("ps", 7)] if kc in (0, 15) else [])
    rsqrt_eps(C, mrstd, ps[:, 0:NMEM], [("ps", 7)], "mrstd")
    for kc in range(16):
        P.stt("dve", memn[:, kc, :], memf[:, kc, :], wcol[:, kc:kc + 1], mrstd, ALU.mult, ALU.mult,
              reads=["memf", "mrstd"], writes=[("memn", kc)])
    memn_keys = [("memn", kc) for kc in range(16)]
    W = WStream(P, A, O_WB)
    rot = Rot([4, 5])
    for c in range(16):
        pair = c // 2
        wkey, wv = W.get(("kvk", li, pair),
                         C.w_kvk[li, pair * 2:pair * 2 + 2].rearrange("n p f -> p n f"), [2, 16, 128])
        b = rot.next()
        pst = C.psum[b]
        P.mmgroup(pst[:, 0:NMEM], [(wv[:, c % 2, kc, :], memn[:, kc, :]) for kc in range(16)],
                  reads=[wkey] + memn_keys, writes=[("ps", b)])
        P.copy("act" if c % 2 else "dve", kT[:, c, :], pst[:, 0:NMEM], reads=[("ps", b)], writes=[("kT", c)])
    for blk in range(8):
        wkey, wv = W.get(("kvv", li, blk), C.w_kvv[li, blk], [16, 256])
        for mc in range(2):
            b = rot.next()
            pst = C.psum[b]
            P.mmgroup(pst[:, 0:256], [(memn[:, kc, mc * 128:(mc + 1) * 128], wv[:, kc, :]) for kc in range(16)],
                      reads=[wkey] + memn_keys, writes=[("ps", b)])
            P.copy("act" if mc else "dve", v[:, mc, blk * 256:(blk + 1) * 256], pst[:, 0:256],
                   reads=[("ps", b)], writes=[("v", mc, blk)])


def gemm_cols(C, W, wsrc_fn, col, rot, epilogue, hT):
    P = C.P
    pair = col // 2
    src = wsrc_fn("src", pair)
    wkey, wv = W.get(wsrc_fn("key", pair), src, [int(src.shape[1]), 16, 128])
    for tb in range(NTB):
        b = rot.next()
        pst = C.psum[b]
        P.mmgroup(pst[:, :], [(wv[:, col % 2, kc, :], hT[:, kc, tb * 512:(tb + 1) * 512]) for kc in range(16)],
                  reads=[wkey], writes=[("ps", b)])
        epilogue(tb, pst, ("ps", b))


def gate_chunk(C, W, wsrc_fn, col, rot, hT, sg, sgkey):
    P = C.P

    def epi(tb, pst, pkey):
        P.act(sg[:, tb * 512:(tb + 1) * 512], pst[:, :], AF.Silu, reads=[pkey], writes=[(sgkey, tb)])
    gemm_cols(C, W, wsrc_fn, col, rot, epi, hT)


def cross_attn(C, W, wsrc_fn, rot, hT, xq_col0, gate_col0, li):
    P, A = C.P, C.A
    kT = A.view(O_KV, [16, NMEM], BF16)
    v = A.view(O_KV + 8192, [2, D], BF16)
    xqT = A.view(O_R2, [4, T], BF16)
    pT = A.view(O_R2 + 16384, [2, T], BF16)
    p32 = [A.view(O_R2 + 24576 + i * 1024, [NMEM], F32) for i in range(2)]
    pbf = [A.view(O_R2 + 26624 + i * 512, [NMEM], BF16) for i in range(2)]
    sgb = [A.view(O_SG + i * 4096, [T], BF16) for i in range(2)]
    yst = [A.view(O_SG + 8192 + i * 4096, [T], BF16) for i in range(2)]
    scale = float(512 ** -0.5)
    sm = C.sm
    cnt = 0
    for hd in range(4):
        for dc in range(4):
            def epi(tb, pst, pkey, dc=dc):
                P.copy("dve" if tb % 2 else "act", xqT[:, dc, tb * 512:(tb + 1) * 512], pst[:, :],
                       reads=[pkey], writes=[("xqT", dc, tb)])
            gemm_cols(C, W, wsrc_fn, xq_col0 + hd * 4 + dc, rot, epi, hT)
        for stl in range(16):
            tb = stl // 4
            i2 = stl % 2
            pst = C.psum[6]
            P.mmgroup(pst[:, 0:NMEM], [(xqT[:, dc, stl * 128:(stl + 1) * 128], kT[:, hd * 4 + dc, :])
                                       for dc in range(4)],
                      reads=[("xqT", dc, tb) for dc in range(4)], writes=[("ps", 6)])
            P.add("dve", lambda e, pst=pst: e.reduce_max(out=sm[:, 0:1], in_=pst[:, 0:NMEM], axis=AX.X),
                  reads=[("ps", 6)], writes=["sm0"])
            P.ts("dve", sm[:, 1:2], sm[:, 0:1], -scale, None, ALU.mult, None, reads=["sm0"], writes=["sm1"])
            P.act(p32[i2], pst[:, 0:NMEM], AF.Exp, reads=[("ps", 6), "sm1"], writes=[("p32", i2)],
                  bias=sm[:, 1:2], scale=scale)
            P.add("dve", lambda e, i2=i2: e.reduce_sum(out=sm[:, 2:3], in_=p32[i2], axis=AX.X),
                  reads=[("p32", i2)], writes=["sm2"])
            P.add("dve", lambda e: e.reciprocal(out=sm[:, 3:4], in_=sm[:, 2:3]), reads=["sm2"], writes=["sm3"])
            P.ts("dve", pbf[i2], p32[i2], sm[:, 3:4], None, ALU.mult, None, reads=[("p32", i2), "sm3"],
                 writes=[("pbf", i2)])
            pt = C.psum[7][:, 0:128].bitcast(BF16)
            for mc in range(2):
                P.transpose(pt[:, mc * 128:(mc + 1) * 128], pbf[i2][:, mc * 128:(mc + 1) * 128], C.identb,
                            reads=[("pbf", i2)], writes=[("ps", 7)])
            P.copy("act", pT[:, :, stl * 128:(stl + 1) * 128], pt.rearrange("p (a b) -> p a b", a=2),
                   reads=[("ps", 7)], writes=[("pT", stl)])
        for dc in range(4):
            j = 32 + hd * 4 + dc
            sg = sgb[cnt % 2]
            ys = yst[cnt % 2]
            gate_chunk(C, W, wsrc_fn, gate_col0 + j, rot, hT, sg, ("sg", cnt % 2))
            for tb in range(NTB):
                b = rot.next()
                pst = C.psum[b]
                P.mmgroup(pst[:, :], [(v[:, mc, hd * 512 + dc * 128: hd * 512 + (dc + 1) * 128],
                                       pT[:, mc, tb * 512:(tb + 1) * 512]) for mc in range(2)],
                          reads=[("pT", s_) for s_ in range(tb * 4, tb * 4 + 4)], writes=[("ps", b)])
                P.tt("dve", ys[:, tb * 512:(tb + 1) * 512], pst[:, :], sg[:, tb * 512:(tb + 1) * 512], ALU.mult,
                     reads=[("ps", b), (("sg", cnt % 2), tb)], writes=[("yst", cnt % 2, tb)])
            P.dma("sp", C.yS[j], ys, reads=[("yst", cnt % 2, tb) for tb in range(NTB)], writes=[("yS", j)])
            cnt += 1


def phase_pool(C, li):
    P, A = C.P, C.A
    j_ = li // 2
    hT = A.view(O_HT, [16, T], BF16)
    pg = A.view(O_R1, [8, T], BF16)
    LP = T + 32
    ub = A.view(O_R2, [LP], F32)
    sA = A.view(O_R2 + LP * 4, [LP], F32)
    sB = A.view(O_R2 + 2 * LP * 4, [LP], F32)
    sgb = [A.view(O_SG + i * 4096, [T], BF16) for i in range(2)]
    yst = [A.view(O_SG + 8192 + i * 4096, [T], BF16) for i in range(2)]
    W = WStream(P, A, O_WB)
    rot = Rot([0, 1, 2, 3, 4, 5])

    def wsrc(kind, pair):
        if kind == "key":
            return ("pw_in", li, pair)
        return C.pw_in[j_, pair * 2:pair * 2 + 2].rearrange("n p f -> p n f")

    P.memset("pool", ub[:, 0:16], 0.0, writes=["ub_padl"])
    P.memset("pool", ub[:, 16 + T:LP], 0.0, writes=["ub_padr"])
    cnt = 0
    for g in range(4):
        w = POOL_WINDOWS[g]
        half = w // 2
        for cc in range(8):
            def epi(tb, pst, pkey):
                P.copy("act" if tb % 2 else "dve", ub[:, 16 + tb * 512:16 + (tb + 1) * 512], pst[:, :],
                       reads=[pkey], writes=[("ub", tb)])
            gemm_cols(C, W, wsrc, g * 8 + cc, rot, epi, hT)
            ubk = [("ub", tb) for tb in range(NTB)] + ["ub_padl", "ub_padr"]
            P.tt("pool", sA[:, 0:LP - 1], ub[:, 0:LP - 1], ub[:, 1:LP], ALU.add, reads=ubk, writes=["sA"])
            cur, curk, n, sh = sA, "sA", LP - 1, 2
            oth, othk = sB, "sB"
            while sh < w:
                P.tt("pool", oth[:, 0:n - sh], cur[:, 0:n - sh], cur[:, sh:n], ALU.add, reads=[curk], writes=[othk])
                cur, curk, oth, othk = oth, othk, cur, curk
                n -= sh
                sh *= 2
            off = 16 - half
            P.stt("dve", pg[:, cc, :], cur[:, off:off + T], 1.0 / w, ub[:, 16:16 + T], ALU.mult, ALU.subtract,
                  reads=[curk] + ubk, writes=[("pg", cc)])
            nl = half
            nr = half - 1
            re = C.redge[:, g * 16:(g + 1) * 16]
            P.tt("dve", C.sm[:, 4:4 + nl], cur[:, off:off + nl], re[:, 0:nl], ALU.mult, reads=[curk], writes=["edl"])
            P.tt("dve", pg[:, cc, 0:nl], C.sm[:, 4:4 + nl], ub[:, 16:16 + nl], ALU.subtract,
                 reads=["edl"] + ubk, writes=[("pg", cc)])
            if nr > 0:
                P.tt("dve", C.sm[:, 4:4 + nr], cur[:, off + T - nr:off + T], re[:, 8:8 + nr], ALU.mult,
                     reads=[curk], writes=["edl"])
                P.tt("dve", pg[:, cc, T - nr:T], C.sm[:, 4:4 + nr], ub[:, 16 + T - nr:16 + T], ALU.subtract,
                     reads=["edl"] + ubk, writes=[("pg", cc)])
        pgk = [("pg", cc) for cc in range(8)]
        for oc in range(8):
            j = g * 8 + oc
            sg = sgb[cnt % 2]
            ys = yst[cnt % 2]
            gate_chunk(C, W, wsrc, 48 + j, rot, hT, sg, ("sg", cnt % 2))
            wkey, wv = W.get(("pw_g", li, g, oc // 4),
                             C.pw_g[j_, g * 8 + (oc // 4) * 4: g * 8 + (oc // 4) * 4 + 4].rearrange("n p f -> p n f"),
                             [4, 8, 128])
            scol = C.vecs[:, V_PSCALE + j_ * 32 + j: V_PSCALE + j_ * 32 + j + 1]
            for tb in range(NTB):
                b = rot.next()
                pst = C.psum[b]
                P.mmgroup(pst[:, :], [(wv[:, oc % 4, kc, :], pg[:, kc, tb * 512:(tb + 1) * 512]) for kc in range(8)],
                          reads=[wkey] + pgk, writes=[("ps", b)])
                P.stt("dve", ys[:, tb * 512:(tb + 1) * 512], pst[:, :], scol, sg[:, tb * 512:(tb + 1) * 512],
                      ALU.mult, ALU.mult, reads=[("ps", b), (("sg", cnt % 2), tb)], writes=[("yst", cnt % 2, tb)])
            P.dma("sp", C.yS[j], ys, reads=[("yst", cnt % 2, tb) for tb in range(NTB)], writes=[("yS", j)])
            cnt += 1
    P.barrier()
    cross_attn(C, W, wsrc, rot, hT, 32, 48, li)


def phase_dn(C, li):
    P, A = C.P, C.A
    j_ = li // 2
    hT = A.view(O_HT, [16, T], BF16)
    LC = T + 4
    cb = A.view(O_R2, [LC], F32)
    acc = A.view(O_R2 + LC * 4, [T], F32)
    sqb = A.view(O_R2 + LC * 4 + T * 4, [T], F32)
    r32 = A.view(O_R1, [T], F32)
    stb = [A.view(O_R1 + 8192 + i * 4096, [T], BF16) for i in range(2)]
    baf = A.view(O_R1 + 16384, [T], F32)
    sgb = [A.view(O_SG + i * 4096, [T], BF16) for i in range(2)]
    W = WStream(P, A, O_WB)
    rot = Rot([0, 1, 2, 3, 4, 5])

    def wsrc(kind, pair):
        if kind == "key":
            return ("dw_in", li, pair)
        n = 2 if pair * 2 + 2 <= 129 else 1
        return C.dw_in[j_, pair * 2:pair * 2 + n].rearrange("n p f -> p n f")

    P.memset("pool", cb[:, 0:2], 0.0, writes=["cb_padl"])
    P.memset("pool", cb[:, 2 + T:LC], 0.0, writes=["cb_padr"])
    cnt = 0
    for c in range(64):
        def epi(tb, pst, pkey):
            P.copy("act", cb[:, 2 + tb * 512:2 + (tb + 1) * 512], pst[:, :], reads=[pkey], writes=[("cb", tb)])
        gemm_cols(C, W, wsrc, c, rot, epi, hT)
        cbk = [("cb", tb) for tb in range(NTB)] + ["cb_padl", "cb_padr"]
        wc = C.vecs[:, V_CONV + j_ * 320 + c * 5: V_CONV + j_ * 320 + c * 5 + 5]
        P.ts("dve", acc, cb[:, 0:T], wc[:, 0:1], None, ALU.mult, None, reads=cbk, writes=["acc"])
        for k in range(1, 5):
            P.stt("dve", acc, cb[:, k:k + T], wc[:, k:k + 1], acc, ALU.mult, ALU.add, reads=cbk + ["acc"], writes=["acc"])
        st_ = stb[cnt % 2]
        stk = ("stb", cnt % 2)
        cnt += 1
        if c >= 32:
            P.act(st_, acc, AF.Silu, reads=["acc"], writes=[stk])
            P.dma("sp", C.vS[c - 32], st_, reads=[stk], writes=[("vS", c - 32)])
        else:
            P.act(acc, acc, AF.Silu, reads=["acc"], writes=["acc"])
            P.act(sqb, acc, AF.Square, reads=["acc"], writes=["sqb"])
            for tb in range(NTB):
                b = rot.next()
                pst = C.psum[b]
                P.add("pe", lambda e, pst=pst, tb=tb: e.matmul(pst[:, :], C.ones1, sqb[:, tb * 512:(tb + 1) * 512],
                                                              start=True, stop=True),
                      reads=["sqb"], writes=[("ps", b)])
                P.act(r32[:, tb * 512:(tb + 1) * 512], pst[:, :], AF.Ln, reads=[("ps", b)], writes=[("r32", tb)],
                      bias=C.vecs[:, V_EPS:V_EPS + 1], scale=1.0)
            r32k = [("r32", tb) for tb in range(NTB)]
            P.act(r32, r32, AF.Exp, reads=r32k, writes=r32k, scale=-0.5)
            qs = float(128 ** -0.5) if c < 16 else 1.0
            P.stt("dve", st_, acc, qs, r32, ALU.mult, ALU.mult, reads=["acc"] + r32k, writes=[stk])
            dst = C.qS[c] if c < 16 else C.kS[c - 16]
            P.dma("sp", dst, st_, reads=[stk], writes=[("qk", c)])
    for j in range(32):
        sg = sgb[j % 2]
        gate_chunk(C, W, wsrc, 80 + j, rot, hT, sg, ("sg", j % 2))
        P.dma("sp", C.sgS[j], sg, reads=[(("sg", j % 2), tb) for tb in range(NTB)], writes=[("sgS", j)])
    def epi_ba(tb, pst, pkey):
        P.copy("act", baf[:, tb * 512:(tb + 1) * 512], pst[:, :], reads=[pkey], writes=[("baf", tb)])
    gemm_cols(C, W, wsrc, 128, rot, epi_ba, hT)
    P.dma("sp", C.baS, baf, reads=[("baf", tb) for tb in range(NTB)], writes=["baS"])
    P.barrier()
    cross_attn(C, W, wsrc, rot, hT, 64, 80, li)
    P.barrier()
    dn_core(C, li)


def phase_out(C, li, xsrc):
    P, A = C.P, C.A
    yblk = A.view(0, [48, 512], BF16)
    xt = [A.view(49152 + i * 2048, [512], F32) for i in range(2)]
    xo = [A.view(49152 + 4096 + i * 2048, [512], F32) for i in range(2)]
    WB = 65536
    W = WStream(P, A, WB, nslots=2, slot_bytes=12288)
    rot = Rot([0, 1, 2, 3])
    xs = xsrc.rearrange("(oc p) t -> oc p t", p=128)
    xd = C.xS.rearrange("(oc p) t -> oc p t", p=128)
    cnt = 0
    for tb in range(NTB):
        P.dma("sp", yblk, C.yS[:, :, tb * 512:(tb + 1) * 512].rearrange("c p t -> p c t"), reads=[], writes=["yblk"])
        for oc in range(16):
            i2 = cnt % 2
            cnt += 1
            wkey, wv = W.get(("w_out", li, oc, tb), C.w_out[li, oc], [48, 128])
            P.dma("sp", xt[i2], xs[oc][:, tb * 512:(tb + 1) * 512], reads=[("xS", oc, tb)], writes=[("xt", i2)])
            b = rot.next()
            pst = C.psum[b]
            P.mmgroup(pst[:, :], [(wv[:, kc, :], yblk[:, kc, :]) for kc in range(48)],
                      reads=[wkey, "yblk"], writes=[("ps", b)])
            P.tt("dve", xo[i2], pst[:, :], xt[i2], ALU.add, reads=[("ps", b), ("xt", i2)], writes=[("xo", i2)])
            P.dma("sp", xd[oc][:, tb * 512:(tb + 1) * 512], xo[i2], reads=[("xo", i2)], writes=[("xS", oc, tb)])


def phase_final(C, xsrc):
    P, A = C.P, C.A
    xt = A.view(O_R1, [16, 512], F32)
    ot = A.view(0, [16, 512], F32)
    sq = [A.view(O_R2 + i * 2048, [512], F32) for i in range(2)]
    wcol = C.vecs[:, V_FIN:V_FIN + 16]
    srcv = xsrc.rearrange("(kc p) t -> p kc t", p=128)
    dstv = C.outT.rearrange("(kc p) t -> p kc t", p=128)
    ps = C.psum[6]
    for tb in range(NTB):
        P.dma("sp", xt, srcv[:, :, tb * 512:(tb + 1) * 512], reads=[], writes=["xt"])
        for kc in range(16):
            P.act(sq[kc % 2], xt[:, kc, :], AF.Square, reads=["xt"], writes=[("sq", kc % 2)])
            P.add("pe", (lambda kc=kc: lambda e: e.matmul(ps[:, :], C.onesm, sq[kc % 2],
                                                           start=(kc == 0), stop=(kc == 15)))(),
                  reads=[("sq", kc % 2)], writes=[("ps", 6)] if kc in (0, 15) else [])
        rsqrt_eps(C, C.rstd, ps[:, :], [("ps", 6)], "rstd")
        for kc in range(16):
            P.stt("dve", ot[:, kc, :], xt[:, kc, :], wcol[:, kc:kc + 1], C.rstd,
                  ALU.mult, ALU.mult, reads=["xt", "rstd"], writes=[("ot", kc)])
        P.dma("sp", dstv[:, :, tb * 512:(tb + 1) * 512], ot, reads=[("ot", kc) for kc in range(16)],
              writes=[("out", tb)])


def _blk(w, kc, nb):
    K, N = w.shape
    return np.ascontiguousarray(w.reshape(K // 128, 128, N // nb, nb).transpose(2, 1, 0, 3)).reshape(
        N // nb, 128, (K // 128) * nb)


def _col(v):
    return np.ascontiguousarray(v.reshape(-1, 128).T)


def prep_shared(inp):
    import ml_dtypes
    f = np.float32
    vecs = np.zeros((128, NVEC), f)
    for i in range(DEPTH):
        vecs[:, V_NORM + i * 16:V_NORM + (i + 1) * 16] = _col(inp["norm_w"][i])
        vecs[:, V_MNORM + i * 16:V_MNORM + (i + 1) * 16] = _col(inp["mem_norm_w"][i])
    vecs[:, V_FIN:V_FIN + 16] = _col(inp["final_norm_w"])
    for j in range(2):
        vecs[:, V_PSCALE + j * 32:V_PSCALE + (j + 1) * 32] = _col(inp["pool_scale"][j])
        cw = inp["dn_conv_w"][j]
        vecs[:, V_CONV + j * 320:V_CONV + (j + 1) * 320] = np.ascontiguousarray(
            cw.reshape(5, 64, 128).transpose(2, 1, 0)).reshape(128, 320)
        vecs[:, V_DNNORM + j] = inp["dn_norm_w"][j]
    vecs[:, V_EPS] = EPS
    vecs[:, V_ONE] = 1.0
    cmat = np.zeros((128, 3 * 128), f)
    cmat[:, 0:128] = np.eye(128, dtype=f)
    cmat[:, 128:256] = 1.0 / D
    cmat[:, 256:384] = 1.0
    cbf = np.eye(128, dtype=f).astype(ml_dtypes.bfloat16)
    redge = np.zeros((128, 64), f)
    for g, w in enumerate(POOL_WINDOWS):
        half = w // 2
        for t in range(half):
            redge[:, g * 16 + t] = 1.0 / (t + half)
        nr = half - 1
        for i in range(nr):
            t = T - nr + i
            redge[:, g * 16 + 8 + i] = 1.0 / (T - t + half)
    for j in range(2):
        dtb = inp["dn_dt_bias"][j]
        alog = inp["dn_a_log"][j]
        vecs[0:32, V_DTB + j] = dtb[1]
        vecs[32:64, V_DTB + j] = dtb[0]
        vecs[0:32, V_ALOG + j] = alog[1]
        vecs[32:64, V_ALOG + j] = alog[0]
    cmask = np.ones((128, T), f)
    cmask[:, ::64] = 0.0
    ii = np.arange(128)[:, None]
    jj = np.arange(128)[None, :]
    same = (ii // 64) == (jj // 64)
    masks = np.concatenate([(same & (jj < ii)), (same & (jj > ii)), (same & (jj <= ii)), (same & (jj >= ii))],
                           axis=1).astype(f)
    sh = {"vecs": vecs, "cmat": cmat, "cbf": cbf, "redge": redge, "cmask": cmask, "masks": masks}
    wkv = inp["w_kv_mem"]
    sh["w_kvk"] = np.stack([_blk(wkv[i][:, :2048], 16, 128) for i in range(DEPTH)])
    sh["w_kvv"] = np.stack([_blk(wkv[i][:, 2048:], 16, 256) for i in range(DEPTH)])
    sh["w_out"] = np.stack([_blk(inp["w_out"][i], 48, 128) for i in range(DEPTH)])
    sh["pw_in"] = np.stack([_blk(inp["pool_w_in"][j], 16, 128) for j in range(2)])
    sh["pw_g"] = np.stack([np.concatenate([_blk(inp["pool_w_group"][j][g], 8, 128) for g in range(4)])
                           for j in range(2)])
    dws = []
    for j in range(2):
        w = inp["dn_w_in"][j]
        ba = w[:, 16384:]
        w2 = np.concatenate([w[:, :16384], ba[:, 96:128], ba[:, 64:96], ba[:, 0:32], ba[:, 32:64]], axis=1)
        dws.append(_blk(w2, 16, 128))
    sh["dw_in"] = np.stack(dws)
    return sh


_NC_CACHE = {}


def kernel(**inp):
    inp = {k: np.asarray(v) for k, v in inp.items()}
    sh = prep_shared(inp)
    if "full" not in _NC_CACHE:
        _NC_CACHE["full"] = build_program()
    nc = _NC_CACHE["full"]
    in_maps = []
    for b in range(8):
        m = dict(sh)
        m["xT"] = np.ascontiguousarray(inp["x"][b].T)
        m["memT"] = np.ascontiguousarray(inp["mem"][b].T)
        in_maps.append(m)
    res = run_bass_kernel_spmd(nc, in_maps, core_ids=list(range(8)))
    out = np.stack([np.ascontiguousarray(r["outT"].T) for r in res.results])
    return out.astype(np.float32)


def dn_core(C, li):
    P, A = C.P, C.A
    j_ = li // 2
    off = [0]

    def alloc(shape, dt):
        n = int(np.prod(shape)) * (4 if dt == F32 else 2)
        n = (n + 31) // 32 * 32
        o = off[0]
        off[0] += n
        return A.view(o, shape, dt)

    BG = alloc([T], F32)
    TM1 = alloc([16, 128], F32)
    TMB = alloc([16, 64], F32)
    TMK = alloc([16, 64], F32)
    GTOT = alloc([32], F32)
    base = off[0]
    BAf = alloc([T], F32)
    G0 = alloc([T], F32)
    PF = alloc([T], F32)
    B2 = alloc([T], F32)
    cmask = alloc([T], F32)
    TMGD = alloc([16, 64], F32)
    assert off[0] <= O_CONST
    dtb = C.vecs[0:64, V_DTB + j_:V_DTB + j_ + 1]
    alog = C.vecs[0:64, V_ALOG + j_:V_ALOG + j_ + 1]
    one = C.vecs[0:64, V_ONE:V_ONE + 1]
    sm = C.sm
    P.dma("sp", BAf, C.baS, reads=[], writes=["BAf"])
    P.dma("sp", cmask, C.cmask_in, reads=[], writes=["cmask"])
    P.act(BG[64:128, :], BAf[64:128, :], AF.Sigmoid, reads=["BAf"], writes=["BGb"])
    P.act(sm[0:64, 12:13], alog, AF.Exp, reads=[], writes=["nA"])
    P.ts("dve", sm[0:64, 13:14], sm[0:64, 12:13], -1.0, None, ALU.mult, None, reads=["nA"], writes=["nA2"])
    P.ts("dve", B2[0:64, :], BAf[0:64, :], dtb, None, ALU.add, None, reads=["BAf"], writes=["B2"])
    P.ts("dve", PF[0:64, :], B2[0:64, :], -1.0, None, ALU.mult, None, reads=["B2"], writes=["PF"])
    P.tt("dve", PF[0:64, :], PF[0:64, :], B2[0:64, :], ALU.max, reads=["PF", "B2"], writes=["PF"])
    P.act(PF[0:64, :], PF[0:64, :], AF.Exp, reads=["PF"], writes=["PF"], scale=-1.0)
    P.act(PF[0:64, :], PF[0:64, :], AF.Ln, reads=["PF"], writes=["PF"], bias=one, scale=1.0)
    P.ts("dve", G0[0:64, :], B2[0:64, :], 0.0, None, ALU.max, None, reads=["B2"], writes=["G0"])
    P.tt("dve", G0[0:64, :], G0[0:64, :], PF[0:64, :], ALU.add, reads=["G0", "PF"], writes=["G0"])
    P.ts("dve", G0[0:64, :], G0[0:64, :], sm[0:64, 13:14], None, ALU.mult, None, reads=["G0", "nA2"], writes=["G0"])
    P.add("dve", lambda e: e.tensor_tensor_scan(out=PF[0:64, :], data0=cmask[0:64, :], data1=G0[0:64, :],
                                                 initial=0.0, op0=ALU.mult, op1=ALU.add),
          reads=["G0", "cmask", "PF"], writes=["PF"])
    PF3 = PF.rearrange("p (c k) -> p c k", k=64)
    BG3 = BG.rearrange("p (c k) -> p c k", k=64)
    B23 = B2.rearrange("p (c k) -> p c k", k=64)
    G03 = G0.rearrange("p (c k) -> p c k", k=64)
    P.memset("dve", GTOT, 0.0, writes=["GTOT"])
    P.copy("dve", GTOT[0:64, :], PF3[0:64, :, 63], reads=["PF", "GTOT"], writes=["GTOT"])
    P.copy("dve", BG[32:64, :], PF[32:64, :], reads=["PF"], writes=["BGf"])

    def gbc(r0, r1):
        return GTOT[r0:r1, :, None].broadcast_to([r1 - r0, 32, 64])
    P.tt("dve", B23[0:32], gbc(0, 32), PF3[0:32], ALU.subtract, reads=["GTOT", "PF"], writes=["B2"])
    P.tt("dve", BG3[0:32], B23[0:32], G03[0:32], ALU.add, reads=["B2", "G0"], writes=["BGr"])
    P.tt("dve", B23[0:64], gbc(0, 64), BG3[0:64], ALU.subtract, reads=["GTOT", "BGr", "BGf", "B2"], writes=["B2"])
    bgk = ["BGb", "BGf", "BGr"]
    for q4 in range(4):
        pst = C.psum[q4 % 2]
        for k in range(4):
            tau = q4 * 4 + k
            P.transpose(pst[:, k * 128:(k + 1) * 128], BG[:, tau * 128:(tau + 1) * 128], C.identf,
                        reads=bgk, writes=[("ps", q4 % 2)])
        P.copy("dve", TM1[:, q4 * 4:(q4 + 1) * 4, :], pst[:, :].rearrange("p (a b) -> p a b", a=4),
               reads=[("ps", q4 % 2)], writes=[("TM1", q4)])
    for q8 in range(2):
        pst = C.psum[2 + q8]
        for k in range(8):
            tau = q8 * 8 + k
            P.transpose(pst[:, k * 64:(k + 1) * 64], B2[0:64, tau * 128:(tau + 1) * 128], C.identf[0:64, 0:64],
                        reads=["B2"], writes=[("ps", 2 + q8)])
        P.copy("dve", TMGD[:, q8 * 8:(q8 + 1) * 8, :], pst[:, :].rearrange("p (a b) -> p a b", a=8),
               reads=[("ps", 2 + q8)], writes=[("TMGD", q8)])
    tm1k = [("TM1", q) for q in range(4)]
    P.act(TMK, TMGD, AF.Exp, reads=[("TMGD", 0), ("TMGD", 1)], writes=["TMK"])
    P.act(TMB, TM1[:, :, 0:64], AF.Exp, reads=tm1k, writes=["TMB"])
    P.tt("dve", TMB[:, :, 0:32], TMB[:, :, 0:32], TM1[:, :, 96:128], ALU.mult, reads=["TMB"] + tm1k, writes=["TMB"])
    P.tt("dve", TMB[:, :, 32:64], TMB[:, :, 32:64], TM1[:, :, 64:96], ALU.mult, reads=["TMB"] + tm1k, writes=["TMB"])
    P.barrier()

    off[0] = base
    qT = alloc([T], BF16)
    kT = alloc([T], BF16)
    Ktm = alloc([16, 128], BF16)
    vT = alloc([T], BF16)
    Vtm = alloc([16, 128], BF16)
    nGMs2 = [[alloc([4, 128], F32) for _ in range(2)] for _ in range(2)]
    KQMt2 = [alloc([4, 128], F32) for _ in range(2)]
    SCR = []
    for d in range(2):
        scr = Ctx()
        scr.t1 = alloc([4, 128], F32)
        scr.t2 = alloc([4, 128], F32)
        scr.tmp = alloc([4, 128], F32)
        scr.egr = alloc([512], F32)
        scr.Qb = [alloc([4, 128], BF16) for _ in range(2)]
        scr.Pb = [alloc([4, 128], BF16) for _ in range(2)]
        scr.Rb = [alloc([4, 128], BF16) for _ in range(2)]
        scr.bV = alloc([4, 128], BF16)
        scr.Kp = alloc([4, 128], BF16)
        SCR.append(scr)
    U = [alloc([16, 128], F32) for _ in range(2)]
    WT = [alloc([T], BF16) for _ in range(2)]
    QgT = [alloc([T], BF16) for _ in range(2)]
    Kd = [alloc([16, 128], BF16) for _ in range(2)]
    AT = [alloc([16, 128], BF16) for _ in range(2)]
    S = [alloc([128], F32) for _ in range(2)]
    Sbf = [alloc([128], BF16) for _ in range(2)]
    egt = [[alloc([32], F32) for _ in range(2)] for _ in range(2)]
    vnew = [alloc([128], BF16) for _ in range(2)]
    sel = [alloc([128], F32) for _ in range(4)]
    nsel = [alloc([128], F32) for _ in range(2)]
    oT = [alloc([T], F32) for _ in range(2)]
    sg = alloc([T], BF16)
    yst = alloc([T], BF16)
    osq = [alloc([512], F32) for _ in range(2)]
    rs = [alloc([512], F32) for _ in range(2)]
    assert off[0] <= O_CONST, off[0]
    Ms = [C.masks[:, 0, :], C.masks[:, 1, :]]
    Mt = [C.masks[:, 2, :], C.masks[:, 3, :]]

    def b4(ap):
        return ap[:, None, :].broadcast_to([128, 4, 128])

    def p4(pst):
        return pst[:, :].rearrange("p (a b) -> p a b", a=4)
    nwcol = C.vecs[:, V_DNNORM + j_:V_DNNORM + j_ + 1]
    rot = Rot([3, 4, 5, 6, 7])
    tm1k = []

    def head_setup(hv):
        h = hv // 2
        if hv % 2 == 0:
            P.dma("sp", qT, C.qS[h], reads=[], writes=["qT"])
            P.dma("sp", kT, C.kS[h], reads=[], writes=["kT"])
            yield
            for q8 in range(2):
                b = rot.next()
                ptb = C.psum[b][:, :].bitcast(BF16)
                for k in range(8):
                    tau = q8 * 8 + k
                    P.transpose(ptb[:, k * 128:(k + 1) * 128], kT[:, tau * 128:(tau + 1) * 128], C.identb,
                                reads=["kT"], writes=[("ps", b)])
                P.copy("act", Ktm[:, q8 * 8:(q8 + 1) * 8, :], ptb.rearrange("p (a b) -> p a b", a=8),
                       reads=[("ps", b)], writes=[("Ktm", q8)])
                yield
        P.dma("sp", vT, C.vS[hv], reads=[], writes=["vT"])
        yield
        for q8 in range(2):
            b = rot.next()
            ptb = C.psum[b][:, :].bitcast(BF16)
            for k in range(8):
                tau = q8 * 8 + k
                P.transpose(ptb[:, k * 128:(k + 1) * 128], vT[:, tau * 128:(tau + 1) * 128], C.identb,
                            reads=["vT"], writes=[("ps", b)])
            P.copy("act", Vtm[:, q8 * 8:(q8 + 1) * 8, :], ptb.rearrange("p (a b) -> p a b", a=8),
                   reads=[("ps", b)], writes=[("Vtm", q8)])
            yield
        rows = [32 + hv, hv, 64 + hv, 96 + hv]
        for i, r in enumerate(rows):
            P.ts(DEBUG.get("pe1", "dve"), sel[i], C.ones1, C.identf[:, r:r + 1], None, ALU.mult, None, reads=[], writes=[("sel", i)])
        for d in range(2):
            P.ts(DEBUG.get("pe1", "dve"), nsel[d], sel[d], -1.0, None, ALU.mult, None, reads=[("sel", d)], writes=[("nsel", d)])
        yield
        for d in range(2):
            b = rot.next()
            pst = C.psum[b]
            P.add("pe", lambda e, pst=pst, d=d: e.matmul(pst[:, 0:32], sel[d], GTOT, start=True, stop=True),
                  reads=[("sel", d)], writes=[("ps", b)])
            P.act(egt[hv % 2][d], pst[:, 0:32], AF.Exp, reads=[("ps", b)], writes=[("egt", hv % 2, d)])
            yield

    urot = [Rot([3, 4, 7]), Rot([5, 6])]

    def tg_shared(hv, tg, st):
        nGMs = nGMs2[st]
        rot = urot[st]
        b0, b1 = rot.next(), rot.next()
        psG, psKQ = C.psum[b0], C.psum[b1]

        def fg(e):
            ins = None
            for k in range(4):
                sl = slice((tg * 4 + k) * 128, (tg * 4 + k + 1) * 128)
                ins = e.matmul(psG[:, k * 128:(k + 1) * 128], kT[:, sl], kT[:, sl], start=True, stop=True)
            return ins

        def fkq(e):
            ins = None
            for k in range(4):
                sl = slice((tg * 4 + k) * 128, (tg * 4 + k + 1) * 128)
                ins = e.matmul(psKQ[:, k * 128:(k + 1) * 128], kT[:, sl], qT[:, sl], start=True, stop=True)
            return ins
        P.add("pe", fg, reads=["kT"], writes=[("ps", b0)])
        P.add("pe", fkq, reads=["kT", "qT"], writes=[("ps", b1)])
        yield
        for d in range(2):
            P.stt("dve", nGMs[d], p4(psG), -1.0, b4(Ms[d]), ALU.mult, ALU.mult, reads=[("ps", b0)],
                  writes=[("nGMs", st, d)])
            yield
        P.tt("dve", KQMt2[st], p4(psKQ), b4(Mt[1 - st]), ALU.mult, reads=[("ps", b1)], writes=[("KQMt", st)])
        yield

    def pre_unit(hv, tg, d):
        scr = SCR[d]
        nGMs = nGMs2[d]
        KQMt_d = KQMt2[d]
        t1, t2, tmp, egr, Qb, Pb, Rb, bV, Kp = scr.t1, scr.t2, scr.tmp, scr.egr, scr.Qb, scr.Pb, scr.Rb, scr.bV, scr.Kp
        sk = lambda n, *a: (n, d) + a
        tsl = slice(tg * 512, (tg + 1) * 512)
        cgc = [32 + hv, hv][d]
        cbe = [64 + hv, 96 + hv][d]
        rot = urot[d]
        bD = rot.next()
        psD = C.psum[bD]

        def fdiff(e):
            ins = None
            for k in range(4):
                sl = slice((tg * 4 + k) * 128, (tg * 4 + k + 1) * 128)
                e.matmul(psD[:, k * 128:(k + 1) * 128], sel[d], BG[:, sl], start=True, stop=False)
                ins = e.matmul(psD[:, k * 128:(k + 1) * 128], BG[:, sl], nsel[d], start=False, stop=True)
            return ins
        P.add("pe", fdiff, reads=[("sel", d), ("nsel", d)], writes=[("ps", bD)])
        yield
        P.ts("dve", t1, p4(psD), 0.0, None, ALU.max, None, reads=[("ps", bD)], writes=[sk("t1")])
        P.ts("dve", t2, p4(psD), 0.0, None, ALU.min, None, reads=[("ps", bD)], writes=[sk("t2")])
        bRg = rot.next()
        psRg = C.psum[bRg]
        P.add("pe", lambda e: e.matmul(psRg[:, :], sel[d], BG[:, tsl], start=True, stop=True),
              reads=[("sel", d)], writes=[("ps", bRg)])
        yield
        P.act(t1, t1, AF.Exp, reads=[sk("t1")], writes=[sk("t1")], scale=-1.0)
        P.act(t2, t2, AF.Exp, reads=[sk("t2")], writes=[sk("t2")])
        P.act(egr, psRg[:, :], AF.Exp, reads=[("ps", bRg)], writes=[sk("egr")])
        bRb = rot.next()
        psRb = C.psum[bRb]
        P.add("pe", lambda e: e.matmul(psRb[:, :], sel[2 + d], BG[:, tsl], start=True, stop=True),
              reads=[("sel", 2 + d)], writes=[("ps", bRb)])
        yield
        for k in range(4):
            tau = tg * 4 + k
            bei = TM1[:, tau, cbe:cbe + 1]
            P.stt("dve", Qb[0][:, k, :], t1[:, k, :], bei, nGMs[d][:, k, :], ALU.mult, ALU.mult,
                  reads=[sk("t1"), ("nGMs", d, d)], writes=[sk("Q", 0, k)])
        yield
        P.tt("dve", tmp, t2, p4(psRb), ALU.mult, reads=[sk("t2"), ("ps", bRb)], writes=[sk("tmp")])
        P.tt("dve", Pb[0], tmp, nGMs[1 - d], ALU.mult, reads=[sk("tmp"), ("nGMs", d, 1 - d)], writes=[sk("P", 0)])
        yield
        P.tt("dve", Rb[0], Pb[0], b4(C.identf), ALU.add, reads=[sk("P", 0)], writes=[sk("R", 0)])
        P.tt("dve", AT[d][:, tg * 4:(tg + 1) * 4, :], t2, KQMt_d, ALU.mult, reads=[sk("t2"), ("KQMt", d)],
             writes=[("AT", d, tg)])
        yield
        P.tt("dve", QgT[d][:, tsl], qT[:, tsl], egr, ALU.mult, reads=["qT", sk("egr")], writes=[("QgT", d, tg)])
        for k in range(4):
            tau = tg * 4 + k
            bei = TM1[:, tau, cbe:cbe + 1]
            if DEBUG.get("pe2", "dve") == "pool":
                bc = lambda ap: ap.broadcast_to([128, 128])
                P.tt("pool", bV[:, k, :], Vtm[:, tau, :], bc(bei), ALU.mult, reads=[("Vtm", tau // 8)],
                     writes=[sk("bV", k)])
                P.tt("pool", Kp[:, k, :], Ktm[:, tau, :], bc(TMB[:, tau, cgc:cgc + 1]), ALU.mult,
                     reads=[("Ktm", tau // 8)], writes=[sk("Kp", k)])
                P.tt("pool", Kd[d][:, tau, :], Ktm[:, tau, :], bc(TMK[:, tau, cgc:cgc + 1]), ALU.mult,
                     reads=[("Ktm", tau // 8)], writes=[("Kd", d, tau)])
            else:
                if DEBUG.get("nomul"):
                    P.ts("dve", bV[:, k, :], Vtm[:, tau, :], bei, None, ALU.mult, None, reads=[("Vtm", tau // 8)],
                         writes=[sk("bV", k)])
                    P.ts("dve", Kp[:, k, :], Ktm[:, tau, :], TMB[:, tau, cgc:cgc + 1], None, ALU.mult, None,
                         reads=[("Ktm", tau // 8)], writes=[sk("Kp", k)])
                    P.ts("dve", Kd[d][:, tau, :], Ktm[:, tau, :], TMK[:, tau, cgc:cgc + 1], None, ALU.mult, None,
                         reads=[("Ktm", tau // 8)], writes=[("Kd", d, tau)])
                    continue
                P.add("act", lambda e, k=k, tau=tau, bei=bei: e.mul(out=bV[:, k, :], in_=Vtm[:, tau, :], mul=bei),
                      reads=[("Vtm", tau // 8)], writes=[sk("bV", k)])
                P.add("act", lambda e, k=k, tau=tau: e.mul(out=Kp[:, k, :], in_=Ktm[:, tau, :],
                                                           mul=TMB[:, tau, cgc:cgc + 1]),
                      reads=[("Ktm", tau // 8)], writes=[sk("Kp", k)])
                P.add("act", lambda e, tau=tau: e.mul(out=Kd[d][:, tau, :], in_=Ktm[:, tau, :],
                                                      mul=TMK[:, tau, cgc:cgc + 1]),
                      reads=[("Ktm", tau // 8)], writes=[("Kd", d, tau)])
        yield
        qk = [sk("Q", 0, k) for k in range(4)]
        cur = 0
        rc = 0
        for m in range(1, 6):
            nxt = 1 - cur
            bA = rot.next()
            psA = C.psum[bA]

            def fq(e, psA=psA, cur=cur):
                ins = None
                for k in range(4):
                    ins = e.matmul(psA[:, k * 128:(k + 1) * 128], Pb[cur][:, k, :], Qb[cur][:, k, :],
                                   start=True, stop=True)
                return ins
            P.add("pe", fq, reads=qk + [sk("P", cur)], writes=[("ps", bA)])
            if m < 5:
                bB = rot.next()
                psB = C.psum[bB]

                def fp(e, psB=psB, cur=cur):
                    ins = None
                    for k in range(4):
                        ins = e.matmul(psB[:, k * 128:(k + 1) * 128], Qb[cur][:, k, :], Pb[cur][:, k, :],
                                       start=True, stop=True)
                    return ins
                P.add("pe", fp, reads=qk + [sk("P", cur)], writes=[("ps", bB)])
            P.copy("act", Qb[nxt], p4(psA), reads=[("ps", bA)], writes=[sk("Q", nxt)])
            if m < 5:
                P.copy("act", Pb[nxt], p4(psB), reads=[("ps", bB)], writes=[sk("P", nxt)])
            yield
            bC = rot.next()
            psC = C.psum[bC]

            def fr(e, psC=psC, nxt=nxt, rc=rc):
                ins = None
                for k in range(4):
                    e.matmul(psC[:, k * 128:(k + 1) * 128], Qb[nxt][:, k, :], Rb[rc][:, k, :], start=True, stop=False)
                    ins = e.matmul(psC[:, k * 128:(k + 1) * 128], C.identb, Rb[rc][:, k, :], start=False, stop=True)
                return ins
            P.add("pe", fr, reads=[sk("Q", nxt), sk("R", rc)], writes=[("ps", bC)])
            P.copy("act", Rb[1 - rc], p4(psC), reads=[("ps", bC)], writes=[sk("R", 1 - rc)])
            yield
            rc = 1 - rc
            cur = nxt
            qk = [sk("Q", cur)]
        TT = Rb[rc]
        ttk = sk("R", rc)
        bU = rot.next()
        psU = C.psum[bU]

        def fu(e):
            ins = None
            for k in range(4):
                ins = e.matmul(psU[:, k * 128:(k + 1) * 128], TT[:, k, :], bV[:, k, :], start=True, stop=True)
            return ins
        P.add("pe", fu, reads=[ttk] + [sk("bV", k) for k in range(4)], writes=[("ps", bU)])
        P.copy("act", U[d][:, tg * 4:(tg + 1) * 4, :], p4(psU), reads=[("ps", bU)], writes=[("U", d, tg)])
        yield
        bW = rot.next()
        psW = C.psum[bW]

        def fw(e):
            ins = None
            for k in range(4):
                ins = e.matmul(psW[:, k * 128:(k + 1) * 128], Kp[:, k, :], TT[:, k, :], start=True, stop=True)
            return ins
        P.add("pe", fw, reads=[ttk] + [sk("Kp", k) for k in range(4)], writes=[("ps", bW)])
        P.copy("act", WT[d][:, tsl], psW[:, :], reads=[("ps", bW)], writes=[("WT", d, tg)])
        yield

    def seq_group(hv, u, dirs=(0, 1)):
        eg = egt[hv % 2]
        if u == 0:
            for d in dirs:
                P.memset("dve", S[d], 0.0, writes=[("S", d)])
                P.memset("dve", Sbf[d], 0.0, writes=[("Sbf", d)])
        for st in range(8):
            step = u * 8 + st
            for d in dirs:
                c = step if d == 0 else 31 - step
                tau, hf = c // 2, c % 2
                tg = tau // 4
                r0 = 64 * hf
                bq = 1 + d
                psq = C.psum[bq]
                pso = C.psum[0]
                P.add("pe", lambda e, d=d, tau=tau, psq=psq: e.matmul(
                    psq[:, 0:128], WT[d][:, tau * 128:(tau + 1) * 128], Sbf[d], start=True, stop=True),
                    reads=[("WT", d, tg), ("Sbf", d)], writes=[("ps", bq)])
                P.tt("dve", vnew[d][r0:r0 + 64, :], U[d][r0:r0 + 64, tau, :], psq[r0:r0 + 64, 0:128], ALU.subtract,
                     reads=[("U", d, tg), ("ps", bq)], writes=[("vnew", d)])
                cs = d * 256 + (c % 4) * 64

                def fo(e, d=d, c=c, tau=tau, r0=r0, cs=cs, psq=psq):
                    e.matmul(pso[:, cs:cs + 64], Sbf[d], QgT[d][:, c * 64:(c + 1) * 64], start=True, stop=False)
                    e.matmul(pso[:, cs:cs + 64], vnew[d][r0:r0 + 64, :], AT[d][r0:r0 + 64, tau, r0:r0 + 64],
                             start=False, stop=True)
                    return e.matmul(psq[:, 128:256], Kd[d][r0:r0 + 64, tau, :], vnew[d][r0:r0 + 64, :],
                                    start=True, stop=True)
                P.add("pe", fo, reads=[("Sbf", d), ("QgT", d, tg), ("vnew", d), ("AT", d, tg), ("Kd", d, tau)],
                      writes=[("ps", 0), ("ps", bq)])
                P.stt("dve", Sbf[d], S[d], eg[d][:, c:c + 1], psq[:, 128:256], ALU.mult, ALU.add,
                      reads=[("S", d), ("egt", hv % 2, d), ("ps", bq)], writes=[("Sbf", d)])
                P.stt("dve", S[d], S[d], eg[d][:, c:c + 1], psq[:, 128:256], ALU.mult, ALU.add,
                      reads=[("S", d), ("egt", hv % 2, d), ("ps", bq)], writes=[("S", d)])
                last = (c % 4 == 3) if d == 0 else (c % 4 == 0)
                if last:
                    g4 = c // 4
                    P.copy("act", oT[d][:, g4 * 256:(g4 + 1) * 256], pso[:, d * 256:(d + 1) * 256],
                           reads=[("ps", 0)], writes=[("oT", d, g4)])
                yield

    def finalize(hv):
        P.dma("sp", sg, C.sgS[hv], reads=[], writes=["sg"])
        for tb in range(NTB):
            tsl = slice(tb * 512, (tb + 1) * 512)
            i2 = tb % 2
            otk = [("oT", d, g4) for d in range(2) for g4 in (2 * tb, 2 * tb + 1)]
            P.tt("dve", oT[0][:, tsl], oT[0][:, tsl], oT[1][:, tsl], ALU.add, reads=otk, writes=[("o", tb)])
            P.act(osq[i2], oT[0][:, tsl], AF.Square, reads=[("o", tb)], writes=[("osq", i2)])
            b = 1 + (tb % 2)
            pst = C.psum[b]
            P.add("pe", lambda e, pst=pst, i2=i2: e.matmul(pst[:, :], C.ones1, osq[i2], start=True, stop=True),
                  reads=[("osq", i2)], writes=[("ps", b)])
            P.act(rs[i2], pst[:, :], AF.Ln, reads=[("ps", b)], writes=[("rs", i2)],
                  bias=C.vecs[:, V_EPS:V_EPS + 1], scale=1.0 / 128)
            yield
            P.act(rs[i2], rs[i2], AF.Exp, reads=[("rs", i2)], writes=[("rs", i2)], scale=-0.5)
            P.stt("dve", osq[i2], oT[0][:, tsl], nwcol, rs[i2], ALU.mult, ALU.mult,
                  reads=[("o", tb), ("rs", i2), ("osq", i2)], writes=[("osq", i2)])
            P.tt("dve", yst[:, tsl], osq[i2], sg[:, tsl], ALU.mult, reads=[("osq", i2), "sg"], writes=[("yst", tb)])
            yield
        P.dma("sp", C.yS[hv], yst, reads=[("yst", tb) for tb in range(NTB)], writes=[("yS", hv)])
        yield

    def record(gen):
        P.rec = []
        for _ in gen:
            pass
        out = P.rec
        P.rec = None
        return out

    def merge(*lists):
        lists = [l for l in lists if l]
        pos = [0] * len(lists)
        out = []
        while True:
            best, bf = -1, 2.0
            for i, l in enumerate(lists):
                if pos[i] < len(l):
                    f = pos[i] / len(l)
                    if f < bf:
                        best, bf = i, f
            if best < 0:
                break
            out.append(lists[best][pos[best]])
            pos[best] += 1
        return out

    def pre_ops(hv, u):
        a = record(tg_shared(hv, u, 0)) + record(pre_unit(hv, u, 0))
        b_ = record(tg_shared(hv, 3 - u, 1)) + record(pre_unit(hv, 3 - u, 1))
        ops = merge(a, b_)
        if u == 0:
            ops = record(head_setup(hv)) + ops
        return ops

    def seq_ops(hv, u):
        ops = merge(record(seq_group(hv, u, (0,))), record(seq_group(hv, u, (1,))))
        if u == 3:
            ops = ops + record(finalize(hv))
        return ops

    def play(ops):
        for (eng, fn, reads, writes, dma) in ops:
            P.add(eng, fn, reads, writes, dma)

    groups = [(hv, u) for hv in range(DEBUG.get('nhv', 32)) for u in range(4)]
    play(pre_ops(0, 0))
    for gi, (hv, u) in enumerate(groups):
        so = seq_ops(hv, u)
        po = pre_ops(*groups[gi + 1]) if gi + 1 < len(groups) else []
        play(merge(so, po))
```

```python
import numpy as np
from contextlib import ExitStack
import concourse.bass as bass
import concourse.mybir as mybir
from concourse.bass_utils import run_bass_kernel_spmd

F32 = mybir.dt.float32
BF16 = mybir.dt.bfloat16
AF = mybir.ActivationFunctionType
ALU = mybir.AluOpType
AX = mybir.AxisListType

D = 2048
T = 2048
NMEM = 256
DEPTH = 4
EPS = 1e-6
NTB = T // 512
KC = D // 128
POOL_WINDOWS = (2, 4, 8, 16)
DEBUG = {}


class _Op:
    __slots__ = ("eng", "fn", "deps", "is_dma", "has_dep", "sig", "idx")

    def __init__(self, eng, fn, is_dma):
        self.eng = eng
        self.fn = fn
        self.deps = set()
        self.is_dma = is_dma
        self.has_dep = False
        self.sig = 0
        self.idx = 0


class Prog:
    CE = ("pe", "dve", "act", "pool")
    QE = ("sp", "act", "pool")
    ALLE = ("pe", "dve", "act", "pool", "sp")
    NSLOT = 8

    def __init__(self, nc):
        self.nc = nc
        self.stream = {e: [] for e in self.ALLE}
        self.last_w = {}
        self.rd_c = {}
        self.rd_d = {}
        self.ndma = {q: 0 for q in self.QE}
        self.dmas_since_barrier = []

    rec = None

    def add(self, eng, fn, reads=(), writes=(), dma=False):
        if self.rec is not None:
            self.rec.append((eng, fn, tuple(reads), tuple(writes), dma))
            return None
        op = _Op(eng, fn, dma)
        deps = op.deps
        for r in reads:
            w = self.last_w.get(r)
            if w is not None:
                deps.add(w)
        for r in writes:
            w = self.last_w.get(r)
            if w is not None:
                deps.add(w)
            for o in self.rd_c.get(r, {}).values():
                deps.add(o)
            for o in self.rd_d.get(r, ()):
                deps.add(o)
        if eng == "pe":
            for d_ in [d_ for d_ in deps if d_.eng == "pe" and not d_.is_dma]:
                deps.discard(d_)
        for r in writes:
            self.last_w[r] = op
            self.rd_c[r] = {}
            self.rd_d[r] = []
        for r in reads:
            if dma:
                self.rd_d.setdefault(r, []).append(op)
            else:
                self.rd_c.setdefault(r, {})[eng] = op
        if dma:
            op.idx = self.ndma[eng]
            self.ndma[eng] += 1
            self.dmas_since_barrier.append(op)
        self.stream[eng].append(op)
        return op

    def dma(self, q, out, in_, reads, writes):
        return self.add(q, lambda e: e.dma_start(out=out, in_=in_), reads, writes, dma=True)

    def act(self, out, in_, func, reads, writes, **kw):
        return self.add("act", lambda e: e.activation(out=out, in_=in_, func=func, **kw), reads, writes)

    def tt(self, eng, out, in0, in1, op, reads, writes):
        return self.add(eng, lambda e: e.tensor_tensor(out=out, in0=in0, in1=in1, op=op), reads, writes)

    def ts(self, eng, out, in0, s1, s2, op0, op1, reads, writes):
        if op1 is None:
            return self.add(eng, lambda e: e.tensor_scalar(out=out, in0=in0, scalar1=s1, scalar2=None, op0=op0),
                            reads, writes)
        return self.add(eng, lambda e: e.tensor_scalar(out=out, in0=in0, scalar1=s1, scalar2=s2, op0=op0, op1=op1),
                        reads, writes)

    def stt(self, eng, out, in0, scalar, in1, op0, op1, reads, writes):
        return self.add(eng, lambda e: e.scalar_tensor_tensor(out=out, in0=in0, scalar=scalar, in1=in1,
                                                              op0=op0, op1=op1), reads, writes)

    def copy(self, eng, out, in_, reads, writes):
        if eng == "act":
            return self.add("act", lambda e: e.copy(out=out, in_=in_), reads, writes)
        return self.add(eng, lambda e: e.tensor_copy(out=out, in_=in_), reads, writes)

    def memset(self, eng, ap, val, writes):
        return self.add(eng, lambda e: e.memset(ap, val), (), writes)

    def mmgroup(self, out, pairs, reads, writes):
        n = len(pairs)

        def fn(e):
            ins = None
            for i, (l, r) in enumerate(pairs):
                ins = e.matmul(out, l, r, start=(i == 0), stop=(i == n - 1))
            return ins
        return self.add("pe", fn, reads, writes)

    def transpose(self, out, in_, ident, reads, writes):
        return self.add("pe", lambda e: e.transpose(out, in_, ident), reads, writes)

    def barrier(self):
        tails = [self.stream[e][-1] for e in self.CE if self.stream[e] and not self.stream[e][-1].is_dma]
        for e in self.CE:
            for o in reversed(self.stream[e]):
                if not o.is_dma:
                    tails.append(o)
                    break
        b = _Op("sp", lambda e: e.nop(), False)
        b.deps = set(tails) | set(self.dmas_since_barrier)
        self.stream["sp"].append(b)
        self.dmas_since_barrier = []
        for e in self.CE:
            o = _Op(e, lambda en: en.nop(), False)
            o.deps = {b}
            self.stream[e].append(o)
        self.last_w = {}
        self.rd_c = {}
        self.rd_d = {}
        return b

    def emit(self):
        nc = self.nc
        for ops in self.stream.values():
            for op in ops:
                for d_ in op.deps:
                    d_.has_dep = True
        for e, ops in self.stream.items():
            c = 0
            for op in ops:
                if (not op.is_dma) and op.has_dep:
                    c += 1
                    op.sig = c
        NS = self.NSLOT
        with ExitStack() as st:
            sems = {e: st.enter_context(nc.semaphore(f"s_{e}")) for e in self.ALLE}
            dsems = {q: [st.enter_context(nc.semaphore(f"d_{q}{i}")) for i in range(NS)] for q in self.QE}
            block = st.enter_context(nc.Block())

            def run(ename, eng):
                waited = {e: 0 for e in self.ALLE}
                waited_d = {}
                for op in self.stream[ename]:
                    need = {}
                    needd = {}
                    for d_ in op.deps:
                        if d_.is_dma:
                            k = (d_.eng, d_.idx % NS)
                            rnd = d_.idx // NS + 1
                            if waited_d.get(k, 0) < rnd:
                                needd[k] = max(needd.get(k, 0), rnd)
                        else:
                            if waited[d_.eng] < d_.sig:
                                need[d_.eng] = max(need.get(d_.eng, 0), d_.sig)
                    if op.is_dma and op.idx >= NS:
                        k = (ename, op.idx % NS)
                        rnd = op.idx // NS
                        if waited_d.get(k, 0) < rnd:
                            needd[k] = max(needd.get(k, 0), rnd)
                    for e2, v in need.items():
                        eng.wait_ge(sems[e2], v)
                        waited[e2] = v
                    for (q, slot), r in needd.items():
                        eng.wait_ge(dsems[q][slot], 16 * r)
                        waited_d[(q, slot)] = r
                    ins = op.fn(eng)
                    if op.is_dma:
                        ins.then_inc(dsems[ename][op.idx % NS], 16)
                    elif op.has_dep:
                        ins.then_inc(sems[ename], 1)

            @block.tensor
            def _(e):
                run("pe", e)

            @block.vector
            def _(e):
                run("dve", e)

            @block.scalar
            def _(e):
                run("act", e)

            @block.gpsimd
            def _(e):
                run("pool", e)

            @block.sync
            def _(e):
                run("sp", e)


class Arena:
    def __init__(self, t):
        self.t = t

    def view(self, off, shape, dt):
        n = int(np.prod(shape))
        esz = 4 if dt == F32 else 2
        assert off % 4 == 0 and (n * esz) % 4 == 0
        ap = self.t[:, off // 4:(off + n * esz) // 4]
        if dt != F32:
            ap = ap.bitcast(dt)
        if len(shape) == 2:
            ap = ap.rearrange("p (a b) -> p a b", a=shape[0])
        elif len(shape) == 3:
            ap = ap.rearrange("p (a b c) -> p a b c", a=shape[0], b=shape[1])
        return ap


ARENA_BYTES = 207 * 1024
O_HT = 0
O_WB = 65536
O_R1 = O_WB + 3 * 8192
O_R2 = O_R1 + 32768
O_SG = O_R2 + 33792
O_KV = O_SG + 16384
O_CONST = O_KV + 24576
C_IDB = O_CONST
C_IDF = C_IDB + 256
C_ONE = C_IDF + 512
C_VEC = C_ONE + 512
NVEC = 864
C_RSTD = C_VEC + 4 * 864
C_MRSTD = C_RSTD + 2048
C_SM = C_MRSTD + 1024
C_ONE1 = C_SM + 64
C_MASK = C_ONE1 + 512
C_END = C_MASK + 2048
assert NVEC <= 864 and C_END <= ARENA_BYTES, (NVEC, C_END)

V_NORM = 0
V_MNORM = 64
V_FIN = 128
V_PSCALE = 144
V_CONV = 208
V_DNNORM = 848
V_EPS = 850
V_ONE = 851
V_DTB = 852
V_ALOG = 854


class Ctx:
    pass


class WStream:
    def __init__(self, P, A, base, nslots=3, slot_bytes=8192):
        self.P, self.A, self.base, self.n, self.sb = P, A, base, nslots, slot_bytes
        self.i = 0
        self.cache = {}
        self.owner = [None] * nslots

    def get(self, key, src, shape):
        if key in self.cache:
            return self.cache[key]
        s = self.i % self.n
        self.i += 1
        if self.owner[s] is not None:
            del self.cache[self.owner[s]]
        self.owner[s] = key
        view = self.A.view(self.base + s * self.sb, shape, BF16)
        fshape = list(src.shape[1:])
        flat = self.A.view(self.base + s * self.sb, fshape, BF16)
        assert int(np.prod(fshape)) == int(np.prod(shape)), (fshape, shape)
        self.P.dma("pool", flat, src, reads=[], writes=[("wb", self.base, s)])
        self.cache[key] = (("wb", self.base, s), view)
        return self.cache[key]


class Rot:
    def __init__(self, items):
        self.items = items
        self.i = 0

    def next(self):
        x = self.items[self.i % len(self.items)]
        self.i += 1
        return x


def build_program(layers=(0, 1, 2, 3), final=True, dbg_out=None):
    nc = bass.Bass("TRN2", target_bir_lowering=False)
    C = Ctx()
    C.nc = nc

    def din(name, shape, dt=F32):
        return nc.dram_tensor(name, shape, dt, kind="ExternalInput").ap()

    C.xT_in = din("xT", [D, T])
    C.memT_in = din("memT", [D, NMEM])
    C.vecs_in = din("vecs", [128, NVEC])
    C.cmat_in = din("cmat", [128, 3 * 128])
    C.cbf_in = din("cbf", [128, 128], BF16)
    C.redge_in = din("redge", [128, 64])
    C.cmask_in = din("cmask", [128, T])
    C.masks_in = din("masks", [128, 4 * 128])
    C.w_kvk = din("w_kvk", [DEPTH, 16, 128, 16 * 128])
    C.w_kvv = din("w_kvv", [DEPTH, 8, 128, 16 * 256])
    C.w_out = din("w_out", [DEPTH, 16, 128, 48 * 128])
    C.pw_in = din("pw_in", [2, 96, 128, 16 * 128])
    C.pw_g = din("pw_g", [2, 32, 128, 8 * 128])
    C.dw_in = din("dw_in", [2, 129, 128, 16 * 128])
    C.outT = nc.dram_tensor("outT", [D, T], F32, kind="ExternalOutput").ap()
    C.xS = nc.dram_tensor("xS", [D, T], F32).ap()
    skind = "ExternalOutput" if DEBUG.get("dump") else "Internal"
    C.yS = nc.dram_tensor("yS", [48, 128, T], BF16, kind=skind).ap()
    C.qS = nc.dram_tensor("qS", [16, 128, T], BF16, kind=skind).ap()
    C.kS = nc.dram_tensor("kS", [16, 128, T], BF16, kind=skind).ap()
    C.vS = nc.dram_tensor("vS", [32, 128, T], BF16, kind=skind).ap()
    C.sgS = nc.dram_tensor("sgS", [32, 128, T], BF16, kind=skind).ap()
    C.baS = nc.dram_tensor("baS", [128, T], F32, kind=skind).ap()

    with ExitStack() as st:
        arena_t = st.enter_context(nc.sbuf_tensor("arena", [128, ARENA_BYTES // 4], F32))
        C.A = A = Arena(arena_t)
        C.psum = [st.enter_context(nc.psum_tensor(f"ps{i}", [128, 512], F32)) for i in range(8)]
        C.P = P = Prog(nc)

        C.identb = A.view(C_IDB, [128], BF16)
        C.identf = A.view(C_IDF, [128], F32)
        C.onesm = A.view(C_ONE, [128], F32)
        C.vecs = A.view(C_VEC, [NVEC], F32)
        C.rstd = A.view(C_RSTD, [512], F32)
        C.sm = A.view(C_SM, [16], F32)
        C.redge = A.view(C_MRSTD, [64], F32)
        P.dma("sp", C.identb, C.cbf_in, [], ["c0"])
        P.dma("sp", C.identf, C.cmat_in[:, 0:128], [], ["c1"])
        P.dma("sp", C.onesm, C.cmat_in[:, 128:256], [], ["c2"])
        P.dma("sp", C.vecs, C.vecs_in, [], ["c3"])
        P.dma("sp", C.redge, C.redge_in, [], ["c4"])
        C.ones1 = A.view(C_ONE1, [128], F32)
        C.masks = A.view(C_MASK, [4, 128], F32)
        P.dma("sp", C.ones1, C.cmat_in[:, 256:384], [], ["c5"])
        P.dma("sp", C.masks, C.masks_in.rearrange("p (a b) -> p a b", a=4), [], ["c6"])
        P.barrier()

        xsrc = C.xT_in
        for li in layers:
            if DEBUG.get("dn_only"):
                dn_core(C, li)
                continue
            phase_norm(C, xsrc, C.vecs[:, V_NORM + li * 16: V_NORM + li * 16 + 16])
            phase_memkv(C, li)
            P.barrier()
            if li % 2 == 0:
                phase_pool(C, li)
            else:
                phase_dn(C, li)
            P.barrier()
            phase_out(C, li, xsrc)
            P.barrier()
            xsrc = C.xS
        if final:
            phase_final(C, xsrc)
        else:
            for oc in range(16):
                P.dma("sp", C.outT[oc * 128:(oc + 1) * 128, :], xsrc[oc * 128:(oc + 1) * 128, :], reads=[], writes=[("o", oc)])
        P.barrier()
        P.emit()
    return nc


def rsqrt_eps(C, out, in_, reads, wkey):
    P = C.P
    P.act(out, in_, AF.Ln, reads=list(reads), writes=[wkey], bias=C.vecs[:, V_EPS:V_EPS + 1], scale=1.0)
    P.act(out, out, AF.Exp, reads=[wkey], writes=[wkey], scale=-0.5)


def phase_norm(C, src, wcol):
    P, A = C.P, C.A
    hT = A.view(O_HT, [16, T], BF16)
    xt = A.view(O_R1, [16, 512], F32)
    sq = [A.view(O_R2 + i * 2048, [512], F32) for i in range(2)]
    srcv = src.rearrange("(kc p) t -> p kc t", p=128)
    ps = C.psum[6]
    for tb in range(NTB):
        P.dma("sp", xt, srcv[:, :, tb * 512:(tb + 1) * 512], reads=[], writes=["xt"])
        for kc in range(16):
            P.act(sq[kc % 2], xt[:, kc, :], AF.Square, reads=["xt"], writes=[("sq", kc % 2)])
            P.add("pe", (lambda kc=kc: lambda e: e.matmul(ps[:, :], C.onesm, sq[kc % 2],
                                                           start=(kc == 0), stop=(kc == 15)))(),
                  reads=[("sq", kc % 2)], writes=[("ps", 6)] if kc in (0, 15) else [])
        rsqrt_eps(C, C.rstd, ps[:, :], [("ps", 6)], "rstd")
        for kc in range(16):
            eng = "dve"
            P.stt(eng, hT[:, kc, tb * 512:(tb + 1) * 512], xt[:, kc, :], wcol[:, kc:kc + 1], C.rstd,
                  ALU.mult, ALU.mult, reads=["xt", "rstd"], writes=[("hT", tb, kc)])


def phase_memkv(C, li):
    P, A = C.P, C.A
    memf = A.view(O_R2 + 4096, [16, NMEM], F32)
    sq = [A.view(O_R2 + 4096 + 16384 + i * 1024, [NMEM], F32) for i in range(2)]
    mrstd = A.view(O_R2 + 4096 + 16384 + 2048, [NMEM], F32)
    kT = A.view(O_KV, [16, NMEM], BF16)
    v = A.view(O_KV + 8192, [2, D], BF16)
    memn = A.view(O_KV + 16384, [16, NMEM], BF16)
    wcol = C.vecs[:, V_MNORM + li * 16: V_MNORM + li * 16 + 16]
    ps = C.psum[7]
    P.dma("sp", memf, C.memT_in.rearrange("(kc p) m -> p kc m", p=128), reads=[], writes=["memf"])
    for kc in range(16):
        P.act(sq[kc % 2], memf[:, kc, :], AF.Square, reads=["memf"], writes=[("msq", kc % 2)])
        P.add("pe", (lambda kc=kc: lambda e: e.matmul(ps[:, 0:NMEM], C.onesm, sq[kc % 2],
                                                       start=(kc == 0), stop=(kc == 15)))(),
              reads=[("msq", kc % 2)], writes=[("ps", 7)] if kc in (0, 15) else [])
    rsqrt_eps(C, mrstd, ps[:, 0:NMEM], [("ps", 7)], "mrstd")
    for kc in range(16):
        P.stt("dve", memn[:, kc, :], memf[:, kc, :], wcol[:, kc:kc + 1], mrstd, ALU.mult, ALU.mult,
              reads=["memf", "mrstd"], writes=[("memn", kc)])
    memn_keys = [("memn", kc) for kc in range(16)]
    W = WStream(P, A, O_WB)
    rot = Rot([4, 5])
    for c in range(16):
        pair = c // 2
        wkey, wv = W.get(("kvk", li, pair),
                         C.w_kvk[li, pair * 2:pair * 2 + 2].rearrange("n p f -> p n f"), [2, 16, 128])
        b = rot.next()
        pst = C.psum[b]
        P.mmgroup(pst[:, 0:NMEM], [(wv[:, c % 2, kc, :], memn[:, kc, :]) for kc in range(16)],
                  reads=[wkey] + memn_keys, writes=[("ps", b)])
        P.copy("act" if c % 2 else "dve", kT[:, c, :], pst[:, 0:NMEM], reads=[("ps", b)], writes=[("kT", c)])
    for blk in range(8):
        wkey, wv = W.get(("kvv", li, blk), C.w_kvv[li, blk], [16, 256])
        for mc in range(2):
            b = rot.next()
            pst = C.psum[b]
            P.mmgroup(pst[:, 0:256], [(memn[:, kc, mc * 128:(mc + 1) * 128], wv[:, kc, :]) for kc in range(16)],
                      reads=[wkey] + memn_keys, writes=[("ps", b)])
            P.copy("act" if mc else "dve", v[:, mc, blk * 256:(blk + 1) * 256], pst[:, 0:256],
                   reads=[("ps", b)], writes=[("v", mc, blk)])


def gemm_cols(C, W, wsrc_fn, col, rot, epilogue, hT):
    P = C.P
    pair = col // 2
    src = wsrc_fn("src", pair)
    wkey, wv = W.get(wsrc_fn("key", pair), src, [int(src.shape[1]), 16, 128])
    for tb in range(NTB):
        b = rot.next()
        pst = C.psum[b]
        P.mmgroup(pst[:, :], [(wv[:, col % 2, kc, :], hT[:, kc, tb * 512:(tb + 1) * 512]) for kc in range(16)],
                  reads=[wkey], writes=[("ps", b)])
        epilogue(tb, pst, ("ps", b))


def gate_chunk(C, W, wsrc_fn, col, rot, hT, sg, sgkey):
    P = C.P

    def epi(tb, pst, pkey):
        P.act(sg[:, tb * 512:(tb + 1) * 512], pst[:, :], AF.Silu, reads=[pkey], writes=[(sgkey, tb)])
    gemm_cols(C, W, wsrc_fn, col, rot, epi, hT)


def cross_attn(C, W, wsrc_fn, rot, hT, xq_col0, gate_col0, li):
    P, A = C.P, C.A
    kT = A.view(O_KV, [16, NMEM], BF16)
    v = A.view(O_KV + 8192, [2, D], BF16)
    xqT = A.view(O_R2, [4, T], BF16)
    pT = A.view(O_R2 + 16384, [2, T], BF16)
    p32 = [A.view(O_R2 + 24576 + i * 1024, [NMEM], F32) for i in range(2)]
    pbf = [A.view(O_R2 + 26624 + i * 512, [NMEM], BF16) for i in range(2)]
    sgb = [A.view(O_SG + i * 4096, [T], BF16) for i in range(2)]
    yst = [A.view(O_SG + 8192 + i * 4096, [T], BF16) for i in range(2)]
    scale = float(512 ** -0.5)
    sm = C.sm
    cnt = 0
    for hd in range(4):
        for dc in range(4):
            def epi(tb, pst, pkey, dc=dc):
                P.copy("dve" if tb % 2 else "act", xqT[:, dc, tb * 512:(tb + 1) * 512], pst[:, :],
                       reads=[pkey], writes=[("xqT", dc, tb)])
            gemm_cols(C, W, wsrc_fn, xq_col0 + hd * 4 + dc, rot, epi, hT)
        for stl in range(16):
            tb = stl // 4
            i2 = stl % 2
            pst = C.psum[6]
            P.mmgroup(pst[:, 0:NMEM], [(xqT[:, dc, stl * 128:(stl + 1) * 128], kT[:, hd * 4 + dc, :])
                                       for dc in range(4)],
                      reads=[("xqT", dc, tb) for dc in range(4)], writes=[("ps", 6)])
            P.add("dve", lambda e, pst=pst: e.reduce_max(out=sm[:, 0:1], in_=pst[:, 0:NMEM], axis=AX.X),
                  reads=[("ps", 6)], writes=["sm0"])
            P.ts("dve", sm[:, 1:2], sm[:, 0:1], -scale, None, ALU.mult, None, reads=["sm0"], writes=["sm1"])
            P.act(p32[i2], pst[:, 0:NMEM], AF.Exp, reads=[("ps", 6), "sm1"], writes=[("p32", i2)],
                  bias=sm[:, 1:2], scale=scale)
            P.add("dve", lambda e, i2=i2: e.reduce_sum(out=sm[:, 2:3], in_=p32[i2], axis=AX.X),
                  reads=[("p32", i2)], writes=["sm2"])
            P.add("dve", lambda e: e.reciprocal(out=sm[:, 3:4], in_=sm[:, 2:3]), reads=["sm2"], writes=["sm3"])
            P.ts("dve", pbf[i2], p32[i2], sm[:, 3:4], None, ALU.mult, None, reads=[("p32", i2), "sm3"],
                 writes=[("pbf", i2)])
            pt = C.psum[7][:, 0:128].bitcast(BF16)
            for mc in range(2):
                P.transpose(pt[:, mc * 128:(mc + 1) * 128], pbf[i2][:, mc * 128:(mc + 1) * 128], C.identb,
                            reads=[("pbf", i2)], writes=[("ps", 7)])
            P.copy("act", pT[:, :, stl * 128:(stl + 1) * 128], pt.rearrange("p (a b) -> p a b", a=2),
                   reads=[("ps", 7)], writes=[("pT", stl)])
        for dc in range(4):
            j = 32 + hd * 4 + dc
            sg = sgb[cnt % 2]
            ys = yst[cnt % 2]
            gate_chunk(C, W, wsrc_fn, gate_col0 + j, rot, hT, sg, ("sg", cnt % 2))
            for tb in range(NTB):
                b = rot.next()
                pst = C.psum[b]
                P.mmgroup(pst[:, :], [(v[:, mc, hd * 512 + dc * 128: hd * 512 + (dc + 1) * 128],
                                       pT[:, mc, tb * 512:(tb + 1) * 512]) for mc in range(2)],
                          reads=[("pT", s_) for s_ in range(tb * 4, tb * 4 + 4)], writes=[("ps", b)])
                P.tt("dve", ys[:, tb * 512:(tb + 1) * 512], pst[:, :], sg[:, tb * 512:(tb + 1) * 512], ALU.mult,
                     reads=[("ps", b), (("sg", cnt % 2), tb)], writes=[("yst", cnt % 2, tb)])
            P.dma("sp", C.yS[j], ys, reads=[("yst", cnt % 2, tb) for tb in range(NTB)], writes=[("yS", j)])
            cnt += 1


def phase_pool(C, li):
    P, A = C.P, C.A
    j_ = li // 2
    hT = A.view(O_HT, [16, T], BF16)
    pg = A.view(O_R1, [8, T], BF16)
    LP = T + 32
    ubs = [A.view(O_R2 + i * LP * 4, [LP], F32) for i in range(2)]
    sA = A.view(O_R2 + 2 * LP * 4, [LP], F32)
    sB = A.view(O_R2 + 3 * LP * 4, [LP], F32)
    assert 4 * LP * 4 <= 33792
    sgb = [A.view(O_SG + i * 4096, [T], BF16) for i in range(2)]
    yst = [A.view(O_SG + 8192 + i * 4096, [T], BF16) for i in range(2)]
    W = WStream(P, A, O_WB)
    rot = Rot([0, 1, 2, 3, 4, 5])

    def wsrc(kind, pair):
        if kind == "key":
            return ("pw_in", li, pair)
        return C.pw_in[j_, pair * 2:pair * 2 + 2].rearrange("n p f -> p n f")

    for i in range(2):
        P.memset("pool", ubs[i][:, 0:16], 0.0, writes=[("ub_padl", i)])
        P.memset("pool", ubs[i][:, 16 + T:LP], 0.0, writes=[("ub_padr", i)])
    cnt = 0
    ucnt = 0
    for g in range(4):
        w = POOL_WINDOWS[g]
        half = w // 2
        for cc in range(8):
            ui = ucnt % 2
            ucnt += 1
            ub = ubs[ui]

            def epi(tb, pst, pkey, ub=ub, ui=ui):
                P.copy("act", ub[:, 16 + tb * 512:16 + (tb + 1) * 512], pst[:, :],
                       reads=[pkey], writes=[("ub", ui, tb)])
            gemm_cols(C, W, wsrc, g * 8 + cc, rot, epi, hT)
            ubk = [("ub", ui, tb) for tb in range(NTB)] + [("ub_padl", ui), ("ub_padr", ui)]
            P.tt("dve", sA[:, 0:LP - 1], ub[:, 0:LP - 1], ub[:, 1:LP], ALU.add, reads=ubk, writes=["sA"])
            cur, curk, n, sh = sA, "sA", LP - 1, 2
            oth, othk = sB, "sB"
            while sh < w:
                P.tt("dve" if sh == 2 else "pool", oth[:, 0:n - sh], cur[:, 0:n - sh], cur[:, sh:n], ALU.add,
                     reads=[curk], writes=[othk])
                cur, curk, oth, othk = oth, othk, cur, curk
                n -= sh
                sh *= 2
            off = 16 - half
            P.stt("dve", pg[:, cc, :], cur[:, off:off + T], 1.0 / w, ub[:, 16:16 + T], ALU.mult, ALU.subtract,
                  reads=[curk] + ubk, writes=[("pg", cc)])
            nl = half
            nr = half - 1
            re = C.redge[:, g * 16:(g + 1) * 16]
            P.tt("dve", C.sm[:, 4:4 + nl], cur[:, off:off + nl], re[:, 0:nl], ALU.mult, reads=[curk], writes=["edl"])
            P.tt("dve", pg[:, cc, 0:nl], C.sm[:, 4:4 + nl], ub[:, 16:16 + nl], ALU.subtract,
                 reads=["edl"] + ubk, writes=[("pg", cc)])
            if nr > 0:
                P.tt("dve", C.sm[:, 4:4 + nr], cur[:, off + T - nr:off + T], re[:, 8:8 + nr], ALU.mult,
                     reads=[curk], writes=["edl"])
                P.tt("dve", pg[:, cc, T - nr:T], C.sm[:, 4:4 + nr], ub[:, 16 + T - nr:16 + T], ALU.subtract,
                     reads=["edl"] + ubk, writes=[("pg", cc)])
        pgk = [("pg", cc) for cc in range(8)]
        for oc in range(8):
            j = g * 8 + oc
            sg = sgb[cnt % 2]
            ys = yst[cnt % 2]
            gate_chunk(C, W, wsrc, 48 + j, rot, hT, sg, ("sg", cnt % 2))
            wkey, wv = W.get(("pw_g", li, g, oc // 4),
                             C.pw_g[j_, g * 8 + (oc // 4) * 4: g * 8 + (oc // 4) * 4 + 4].rearrange("n p f -> p n f"),
                             [4, 8, 128])
            scol = C.vecs[:, V_PSCALE + j_ * 32 + j: V_PSCALE + j_ * 32 + j + 1]
            for tb in range(NTB):
                b = rot.next()
                pst = C.psum[b]
                P.mmgroup(pst[:, :], [(wv[:, oc % 4, kc, :], pg[:, kc, tb * 512:(tb + 1) * 512]) for kc in range(8)],
                          reads=[wkey] + pgk, writes=[("ps", b)])
                P.stt("dve", ys[:, tb * 512:(tb + 1) * 512], pst[:, :], scol, sg[:, tb * 512:(tb + 1) * 512],
                      ALU.mult, ALU.mult, reads=[("ps", b), (("sg", cnt % 2), tb)], writes=[("yst", cnt % 2, tb)])
            P.dma("sp", C.yS[j], ys, reads=[("yst", cnt % 2, tb) for tb in range(NTB)], writes=[("yS", j)])
            cnt += 1
    P.barrier()
    cross_attn(C, W, wsrc, rot, hT, 32, 48, li)


def phase_dn(C, li):
    P, A = C.P, C.A
    j_ = li // 2
    hT = A.view(O_HT, [16, T], BF16)
    LC = T + 4
    cbs = [A.view(O_R2 + i * LC * 4, [LC], F32) for i in range(2)]
    accs = [A.view(O_R2 + 2 * LC * 4 + i * T * 4, [T], F32) for i in range(2)]
    assert 2 * LC * 4 + 2 * T * 4 <= 33792
    r32s = [A.view(O_R1, [T], F32), A.view(O_R1 + 24576, [T], F32)]
    stb = [A.view(O_R1 + 8192 + i * 4096, [T], BF16) for i in range(2)]
    baf = A.view(O_R1 + 16384, [T], F32)
    sgb = [A.view(O_SG + i * 4096, [T], BF16) for i in range(2)]
    W = WStream(P, A, O_WB)
    rot = Rot([0, 1, 2, 3, 4, 5])

    def wsrc(kind, pair):
        if kind == "key":
            return ("dw_in", li, pair)
        n = 2 if pair * 2 + 2 <= 129 else 1
        return C.dw_in[j_, pair * 2:pair * 2 + n].rearrange("n p f -> p n f")

    for i in range(2):
        P.memset("pool", cbs[i][:, 0:2], 0.0, writes=[("cb_padl", i)])
        P.memset("pool", cbs[i][:, 2 + T:LC], 0.0, writes=[("cb_padr", i)])
    cnt = 0
    pending_tails = []
    for c in range(64):
        ci = c % 2
        cb = cbs[ci]

        def epi(tb, pst, pkey, cb=cb, ci=ci):
            P.copy("act" if tb % 2 else "dve", cb[:, 2 + tb * 512:2 + (tb + 1) * 512], pst[:, :],
                   reads=[pkey], writes=[("cb", ci, tb)])
        gemm_cols(C, W, wsrc, c, rot, epi, hT)
        while pending_tails:
            for (eng_, fn_, r_, w_, d_) in pending_tails.pop(0):
                P.add(eng_, fn_, r_, w_, d_)
        cbk = [("cb", ci, tb) for tb in range(NTB)] + [("cb_padl", ci), ("cb_padr", ci)]
        wc = C.vecs[:, V_CONV + j_ * 320 + c * 5: V_CONV + j_ * 320 + c * 5 + 5]
        acc = accs[ci]
        r32 = r32s[ci]
        ak = ("acc", ci)
        P.add("act", lambda e, cb=cb, wc=wc, acc=acc: e.mul(out=acc, in_=cb[:, 0:T], mul=wc[:, 0:1]),
              reads=cbk, writes=[ak])
        for k in range(1, 5):
            P.stt("dve", acc, cb[:, k:k + T], wc[:, k:k + 1], acc, ALU.mult, ALU.add, reads=cbk + [ak], writes=[ak])
        st_ = stb[cnt % 2]
        stk = ("stb", cnt % 2)
        cnt += 1
        if c >= 32:
            P.act(st_, acc, AF.Silu, reads=[ak], writes=[stk])
            P.dma("sp", C.vS[c - 32], st_, reads=[stk], writes=[("vS", c - 32)])
        else:
            r32k = [("r32", ci, tb) for tb in range(NTB)]
            P.act(acc, acc, AF.Silu, reads=[ak], writes=[ak])
            P.act(r32, acc, AF.Square, reads=[ak], writes=r32k)
            P.rec = []
            for tb in range(NTB):
                b = rot.next()
                pst = C.psum[b]
                P.add("pe", lambda e, pst=pst, tb=tb, r32=r32: e.matmul(pst[:, :], C.ones1,
                                                                       r32[:, tb * 512:(tb + 1) * 512],
                                                                       start=True, stop=True),
                      reads=[("r32", ci, tb)], writes=[("ps", b)])
                P.act(r32[:, tb * 512:(tb + 1) * 512], pst[:, :], AF.Ln, reads=[("ps", b)], writes=[("r32", ci, tb)],
                      bias=C.vecs[:, V_EPS:V_EPS + 1], scale=1.0)
            P.act(r32, r32, AF.Exp, reads=r32k, writes=r32k, scale=-0.5)
            qs = float(128 ** -0.5) if c < 16 else 1.0
            P.stt("dve", st_, acc, qs, r32, ALU.mult, ALU.mult, reads=[ak] + r32k, writes=[stk])
            dst = C.qS[c] if c < 16 else C.kS[c - 16]
            P.dma("sp", dst, st_, reads=[stk], writes=[("qk", c)])
            tail, P.rec = P.rec, None
            pending_tails.append(tail)
    for j in range(32):
        if j == 1:
            while pending_tails:
                for (eng_, fn_, r_, w_, d_) in pending_tails.pop(0):
                    P.add(eng_, fn_, r_, w_, d_)
        sg = sgb[j % 2]
        gate_chunk(C, W, wsrc, 80 + j, rot, hT, sg, ("sg", j % 2))
        P.dma("sp", C.sgS[j], sg, reads=[(("sg", j % 2), tb) for tb in range(NTB)], writes=[("sgS", j)])
    def epi_ba(tb, pst, pkey):
        P.copy("act", baf[:, tb * 512:(tb + 1) * 512], pst[:, :], reads=[pkey], writes=[("baf", tb)])
    gemm_cols(C, W, wsrc, 128, rot, epi_ba, hT)
    P.dma("sp", C.baS, baf, reads=[("baf", tb) for tb in range(NTB)], writes=["baS"])
    P.barrier()
    cross_attn(C, W, wsrc, rot, hT, 64, 80, li)
    P.barrier()
    dn_core(C, li)


def phase_out(C, li, xsrc):
    P, A = C.P, C.A
    yblk = A.view(0, [48, 512], BF16)
    xt = [A.view(49152 + i * 2048, [512], F32) for i in range(2)]
    xo = [A.view(49152 + 4096 + i * 2048, [512], F32) for i in range(2)]
    WB = 65536
    W = WStream(P, A, WB, nslots=2, slot_bytes=12288)
    rot = Rot([0, 1, 2, 3])
    xs = xsrc.rearrange("(oc p) t -> oc p t", p=128)
    xd = C.xS.rearrange("(oc p) t -> oc p t", p=128)
    cnt = 0
    for tb in range(NTB):
        P.dma("sp", yblk, C.yS[:, :, tb * 512:(tb + 1) * 512].rearrange("c p t -> p c t"), reads=[], writes=["yblk"])
        for oc in range(16):
            i2 = cnt % 2
            cnt += 1
            wkey, wv = W.get(("w_out", li, oc, tb), C.w_out[li, oc], [48, 128])
            P.dma("sp", xt[i2], xs[oc][:, tb * 512:(tb + 1) * 512], reads=[("xS", oc, tb)], writes=[("xt", i2)])
            b = rot.next()
            pst = C.psum[b]
            P.mmgroup(pst[:, :], [(wv[:, kc, :], yblk[:, kc, :]) for kc in range(48)],
                      reads=[wkey, "yblk"], writes=[("ps", b)])
            P.tt("dve", xo[i2], pst[:, :], xt[i2], ALU.add, reads=[("ps", b), ("xt", i2)], writes=[("xo", i2)])
            P.dma("sp", xd[oc][:, tb * 512:(tb + 1) * 512], xo[i2], reads=[("xo", i2)], writes=[("xS", oc, tb)])


def phase_final(C, xsrc):
    P, A = C.P, C.A
    xt = A.view(O_R1, [16, 512], F32)
    ot = A.view(0, [16, 512], F32)
    sq = [A.view(O_R2 + i * 2048, [512], F32) for i in range(2)]
    wcol = C.vecs[:, V_FIN:V_FIN + 16]
    srcv = xsrc.rearrange("(kc p) t -> p kc t", p=128)
    dstv = C.outT.rearrange("(kc p) t -> p kc t", p=128)
    ps = C.psum[6]
    for tb in range(NTB):
        P.dma("sp", xt, srcv[:, :, tb * 512:(tb + 1) * 512], reads=[], writes=["xt"])
        for kc in range(16):
            P.act(sq[kc % 2], xt[:, kc, :], AF.Square, reads=["xt"], writes=[("sq", kc % 2)])
            P.add("pe", (lambda kc=kc: lambda e: e.matmul(ps[:, :], C.onesm, sq[kc % 2],
                                                           start=(kc == 0), stop=(kc == 15)))(),
                  reads=[("sq", kc % 2)], writes=[("ps", 6)] if kc in (0, 15) else [])
        rsqrt_eps(C, C.rstd, ps[:, :], [("ps", 6)], "rstd")
        for kc in range(16):
            P.stt("dve", ot[:, kc, :], xt[:, kc, :], wcol[:, kc:kc + 1], C.rstd,
                  ALU.mult, ALU.mult, reads=["xt", "rstd"], writes=[("ot", kc)])
        P.dma("sp", dstv[:, :, tb * 512:(tb + 1) * 512], ot, reads=[("ot", kc) for kc in range(16)],
              writes=[("out", tb)])


def _blk(w, kc, nb):
    K, N = w.shape
    return np.ascontiguousarray(w.reshape(K // 128, 128, N // nb, nb).transpose(2, 1, 0, 3)).reshape(
        N // nb, 128, (K // 128) * nb)


def _col(v):
    return np.ascontiguousarray(v.reshape(-1, 128).T)


def prep_shared(inp):
    import ml_dtypes
    f = np.float32
    vecs = np.zeros((128, NVEC), f)
    for i in range(DEPTH):
        vecs[:, V_NORM + i * 16:V_NORM + (i + 1) * 16] = _col(inp["norm_w"][i])
        vecs[:, V_MNORM + i * 16:V_MNORM + (i + 1) * 16] = _col(inp["mem_norm_w"][i])
    vecs[:, V_FIN:V_FIN + 16] = _col(inp["final_norm_w"])
    for j in range(2):
        vecs[:, V_PSCALE + j * 32:V_PSCALE + (j + 1) * 32] = _col(inp["pool_scale"][j])
        cw = inp["dn_conv_w"][j]
        vecs[:, V_CONV + j * 320:V_CONV + (j + 1) * 320] = np.ascontiguousarray(
            cw.reshape(5, 64, 128).transpose(2, 1, 0)).reshape(128, 320)
        vecs[:, V_DNNORM + j] = inp["dn_norm_w"][j]
    vecs[:, V_EPS] = EPS
    vecs[:, V_ONE] = 1.0
    cmat = np.zeros((128, 3 * 128), f)
    cmat[:, 0:128] = np.eye(128, dtype=f)
    cmat[:, 128:256] = 1.0 / D
    cmat[:, 256:384] = 1.0
    cbf = np.eye(128, dtype=f).astype(ml_dtypes.bfloat16)
    redge = np.zeros((128, 64), f)
    for g, w in enumerate(POOL_WINDOWS):
        half = w // 2
        for t in range(half):
            redge[:, g * 16 + t] = 1.0 / (t + half)
        nr = half - 1
        for i in range(nr):
            t = T - nr + i
            redge[:, g * 16 + 8 + i] = 1.0 / (T - t + half)
    for j in range(2):
        dtb = inp["dn_dt_bias"][j]
        alog = inp["dn_a_log"][j]
        vecs[0:32, V_DTB + j] = dtb[1]
        vecs[32:64, V_DTB + j] = dtb[0]
        vecs[0:32, V_ALOG + j] = alog[1]
        vecs[32:64, V_ALOG + j] = alog[0]
    cmask = np.ones((128, T), f)
    cmask[:, ::64] = 0.0
    ii = np.arange(128)[:, None]
    jj = np.arange(128)[None, :]
    same = (ii // 64) == (jj // 64)
    masks = np.concatenate([(same & (jj < ii)), (same & (jj > ii)), (same & (jj <= ii)), (same & (jj >= ii))],
                           axis=1).astype(f)
    sh = {"vecs": vecs, "cmat": cmat, "cbf": cbf, "redge": redge, "cmask": cmask, "masks": masks}
    wkv = inp["w_kv_mem"]
    sh["w_kvk"] = np.stack([_blk(wkv[i][:, :2048], 16, 128) for i in range(DEPTH)])
    sh["w_kvv"] = np.stack([_blk(wkv[i][:, 2048:], 16, 256) for i in range(DEPTH)])
    sh["w_out"] = np.stack([_blk(inp["w_out"][i], 48, 128) for i in range(DEPTH)])
    sh["pw_in"] = np.stack([_blk(inp["pool_w_in"][j], 16, 128) for j in range(2)])
    sh["pw_g"] = np.stack([np.concatenate([_blk(inp["pool_w_group"][j][g], 8, 128) for g in range(4)])
                           for j in range(2)])
    dws = []
    for j in range(2):
        w = inp["dn_w_in"][j]
        ba = w[:, 16384:]
        w2 = np.concatenate([w[:, :16384], ba[:, 96:128], ba[:, 64:96], ba[:, 0:32], ba[:, 32:64]], axis=1)
        dws.append(_blk(w2, 16, 128))
    sh["dw_in"] = np.stack(dws)
    return sh


_NC_CACHE = {}


def kernel(**inp):
    inp = {k: np.asarray(v) for k, v in inp.items()}
    sh = prep_shared(inp)
    if "full" not in _NC_CACHE:
        _NC_CACHE["full"] = build_program()
    nc = _NC_CACHE["full"]
    in_maps = []
    for b in range(8):
        m = dict(sh)
        m["xT"] = np.ascontiguousarray(inp["x"][b].T)
        m["memT"] = np.ascontiguousarray(inp["mem"][b].T)
        in_maps.append(m)
    res = run_bass_kernel_spmd(nc, in_maps, core_ids=list(range(8)))
    out = np.stack([np.ascontiguousarray(r["outT"].T) for r in res.results])
    return out.astype(np.float32)


def dn_core(C, li):
    P, A = C.P, C.A
    j_ = li // 2
    off = [0]

    def alloc(shape, dt):
        n = int(np.prod(shape)) * (4 if dt == F32 else 2)
        n = (n + 31) // 32 * 32
        o = off[0]
        off[0] += n
        return A.view(o, shape, dt)

    BG = alloc([T], F32)
    TM1 = alloc([16, 128], F32)
    TMB = alloc([16, 64], F32)
    TMK = alloc([16, 64], F32)
    GTOT = alloc([32], F32)
    base = off[0]
    BAf = alloc([T], F32)
    G0 = alloc([T], F32)
    PF = alloc([T], F32)
    B2 = alloc([T], F32)
    cmask = alloc([T], F32)
    TMGD = alloc([16, 64], F32)
    assert off[0] <= O_CONST
    dtb = C.vecs[0:64, V_DTB + j_:V_DTB + j_ + 1]
    alog = C.vecs[0:64, V_ALOG + j_:V_ALOG + j_ + 1]
    one = C.vecs[0:64, V_ONE:V_ONE + 1]
    sm = C.sm
    P.dma("sp", BAf, C.baS, reads=[], writes=["BAf"])
    P.dma("sp", cmask, C.cmask_in, reads=[], writes=["cmask"])
    P.act(BG[64:128, :], BAf[64:128, :], AF.Sigmoid, reads=["BAf"], writes=["BGb"])
    P.act(sm[0:64, 12:13], alog, AF.Exp, reads=[], writes=["nA"])
    P.ts("dve", sm[0:64, 13:14], sm[0:64, 12:13], -1.0, None, ALU.mult, None, reads=["nA"], writes=["nA2"])
    P.ts("dve", B2[0:64, :], BAf[0:64, :], dtb, None, ALU.add, None, reads=["BAf"], writes=["B2"])
    P.ts("dve", PF[0:64, :], B2[0:64, :], -1.0, None, ALU.mult, None, reads=["B2"], writes=["PF"])
    P.tt("dve", PF[0:64, :], PF[0:64, :], B2[0:64, :], ALU.max, reads=["PF", "B2"], writes=["PF"])
    P.act(PF[0:64, :], PF[0:64, :], AF.Exp, reads=["PF"], writes=["PF"], scale=-1.0)
    P.act(PF[0:64, :], PF[0:64, :], AF.Ln, reads=["PF"], writes=["PF"], bias=one, scale=1.0)
    P.ts("dve", G0[0:64, :], B2[0:64, :], 0.0, None, ALU.max, None, reads=["B2"], writes=["G0"])
    P.tt("dve", G0[0:64, :], G0[0:64, :], PF[0:64, :], ALU.add, reads=["G0", "PF"], writes=["G0"])
    P.ts("dve", G0[0:64, :], G0[0:64, :], sm[0:64, 13:14], None, ALU.mult, None, reads=["G0", "nA2"], writes=["G0"])
    P.add("dve", lambda e: e.tensor_tensor_scan(out=PF[0:64, :], data0=cmask[0:64, :], data1=G0[0:64, :],
                                                 initial=0.0, op0=ALU.mult, op1=ALU.add),
          reads=["G0", "cmask", "PF"], writes=["PF"])
    PF3 = PF.rearrange("p (c k) -> p c k", k=64)
    BG3 = BG.rearrange("p (c k) -> p c k", k=64)
    B23 = B2.rearrange("p (c k) -> p c k", k=64)
    G03 = G0.rearrange("p (c k) -> p c k", k=64)
    P.memset("dve", GTOT, 0.0, writes=["GTOT"])
    P.copy("dve", GTOT[0:64, :], PF3[0:64, :, 63], reads=["PF", "GTOT"], writes=["GTOT"])
    P.copy("dve", BG[32:64, :], PF[32:64, :], reads=["PF"], writes=["BGf"])

    def gbc(r0, r1):
        return GTOT[r0:r1, :, None].broadcast_to([r1 - r0, 32, 64])
    P.tt("dve", B23[0:32], gbc(0, 32), PF3[0:32], ALU.subtract, reads=["GTOT", "PF"], writes=["B2"])
    P.tt("dve", BG3[0:32], B23[0:32], G03[0:32], ALU.add, reads=["B2", "G0"], writes=["BGr"])
    P.tt("dve", B23[0:64], gbc(0, 64), BG3[0:64], ALU.subtract, reads=["GTOT", "BGr", "BGf", "B2"], writes=["B2"])
    bgk = ["BGb", "BGf", "BGr"]
    for q4 in range(4):
        pst = C.psum[q4 % 2]
        for k in range(4):
            tau = q4 * 4 + k
            P.transpose(pst[:, k * 128:(k + 1) * 128], BG[:, tau * 128:(tau + 1) * 128], C.identf,
                        reads=bgk, writes=[("ps", q4 % 2)])
        P.copy("dve", TM1[:, q4 * 4:(q4 + 1) * 4, :], pst[:, :].rearrange("p (a b) -> p a b", a=4),
               reads=[("ps", q4 % 2)], writes=[("TM1", q4)])
    for q8 in range(2):
        pst = C.psum[2 + q8]
        for k in range(8):
            tau = q8 * 8 + k
            P.transpose(pst[:, k * 64:(k + 1) * 64], B2[0:64, tau * 128:(tau + 1) * 128], C.identf[0:64, 0:64],
                        reads=["B2"], writes=[("ps", 2 + q8)])
        P.copy("dve", TMGD[:, q8 * 8:(q8 + 1) * 8, :], pst[:, :].rearrange("p (a b) -> p a b", a=8),
               reads=[("ps", 2 + q8)], writes=[("TMGD", q8)])
    tm1k = [("TM1", q) for q in range(4)]
    P.act(TMK, TMGD, AF.Exp, reads=[("TMGD", 0), ("TMGD", 1)], writes=["TMK"])
    P.act(TMB, TM1[:, :, 0:64], AF.Exp, reads=tm1k, writes=["TMB"])
    P.tt("dve", TMB[:, :, 0:32], TMB[:, :, 0:32], TM1[:, :, 96:128], ALU.mult, reads=["TMB"] + tm1k, writes=["TMB"])
    P.tt("dve", TMB[:, :, 32:64], TMB[:, :, 32:64], TM1[:, :, 64:96], ALU.mult, reads=["TMB"] + tm1k, writes=["TMB"])
    P.barrier()

    off[0] = base
    qT = alloc([T], BF16)
    kT = alloc([T], BF16)
    Ktm = alloc([16, 128], BF16)
    vT = alloc([T], BF16)
    Vtm = alloc([16, 128], BF16)
    nGMs2 = [[alloc([4, 128], F32) for _ in range(2)] for _ in range(2)]
    KQMt2 = [alloc([4, 128], F32) for _ in range(2)]
    SCR = []
    for d in range(2):
        scr = Ctx()
        scr.t1 = alloc([4, 128], F32)
        scr.t2 = alloc([4, 128], F32)
        scr.tmp = alloc([4, 128], F32)
        scr.egr = alloc([512], F32)
        scr.Qb = [alloc([4, 128], BF16) for _ in range(2)]
        scr.Pb = [alloc([4, 128], BF16) for _ in range(2)]
        scr.Rb = [alloc([4, 128], BF16) for _ in range(2)]
        scr.bV = alloc([4, 128], BF16)
        scr.Kp = alloc([4, 128], BF16)
        SCR.append(scr)
    U = [alloc([16, 128], F32) for _ in range(2)]
    WT = [alloc([T], BF16) for _ in range(2)]
    QgT = [alloc([T], BF16) for _ in range(2)]
    Kd = [alloc([16, 128], BF16) for _ in range(2)]
    AT = [alloc([16, 128], BF16) for _ in range(2)]
    S = [alloc([128], F32) for _ in range(2)]
    Sbf = [alloc([128], BF16) for _ in range(2)]
    egt = [[alloc([32], F32) for _ in range(2)] for _ in range(2)]
    vnew = [alloc([128], BF16) for _ in range(2)]
    sel = [alloc([128], F32) for _ in range(4)]
    nsel = [alloc([128], F32) for _ in range(2)]
    oT = [alloc([T], F32) for _ in range(2)]
    sg = alloc([T], BF16)
    yst = alloc([T], BF16)
    osq = [alloc([512], F32) for _ in range(2)]
    rs = [alloc([512], F32) for _ in range(2)]
    assert off[0] <= O_CONST, off[0]
    Ms = [C.masks[:, 0, :], C.masks[:, 1, :]]
    Mt = [C.masks[:, 2, :], C.masks[:, 3, :]]

    def b4(ap):
        return ap[:, None, :].broadcast_to([128, 4, 128])

    def p4(pst):
        return pst[:, :].rearrange("p (a b) -> p a b", a=4)
    nwcol = C.vecs[:, V_DNNORM + j_:V_DNNORM + j_ + 1]
    rot = Rot([3, 4, 5, 6, 7])
    tm1k = []

    def head_setup(hv):
        h = hv // 2
        if hv % 2 == 0:
            P.dma("sp", qT, C.qS[h], reads=[], writes=["qT"])
            P.dma("sp", kT, C.kS[h], reads=[], writes=["kT"])
            yield
            for q8 in range(2):
                b = rot.next()
                ptb = C.psum[b][:, :].bitcast(BF16)
                for k in range(8):
                    tau = q8 * 8 + k
                    P.transpose(ptb[:, k * 128:(k + 1) * 128], kT[:, tau * 128:(tau + 1) * 128], C.identb,
                                reads=["kT"], writes=[("ps", b)])
                P.copy("act", Ktm[:, q8 * 8:(q8 + 1) * 8, :], ptb.rearrange("p (a b) -> p a b", a=8),
                       reads=[("ps", b)], writes=[("Ktm", q8)])
                yield
        P.dma("sp", vT, C.vS[hv], reads=[], writes=["vT"])
        yield
        for q8 in range(2):
            b = rot.next()
            ptb = C.psum[b][:, :].bitcast(BF16)
            for k in range(8):
                tau = q8 * 8 + k
                P.transpose(ptb[:, k * 128:(k + 1) * 128], vT[:, tau * 128:(tau + 1) * 128], C.identb,
                            reads=["vT"], writes=[("ps", b)])
            P.copy("act", Vtm[:, q8 * 8:(q8 + 1) * 8, :], ptb.rearrange("p (a b) -> p a b", a=8),
                   reads=[("ps", b)], writes=[("Vtm", q8)])
            yield
        rows = [32 + hv, hv, 64 + hv, 96 + hv]
        for i, r in enumerate(rows):
            P.ts(DEBUG.get("pe1", "dve"), sel[i], C.ones1, C.identf[:, r:r + 1], None, ALU.mult, None, reads=[], writes=[("sel", i)])
        for d in range(2):
            P.ts(DEBUG.get("pe1", "dve"), nsel[d], sel[d], -1.0, None, ALU.mult, None, reads=[("sel", d)], writes=[("nsel", d)])
        yield
        for d in range(2):
            b = rot.next()
            pst = C.psum[b]
            P.add("pe", lambda e, pst=pst, d=d: e.matmul(pst[:, 0:32], sel[d], GTOT, start=True, stop=True),
                  reads=[("sel", d)], writes=[("ps", b)])
            P.act(egt[hv % 2][d], pst[:, 0:32], AF.Exp, reads=[("ps", b)], writes=[("egt", hv % 2, d)])
            yield

    urot = [Rot([3, 4, 7]), Rot([5, 6])]

    def tg_shared(hv, tg, st):
        nGMs = nGMs2[st]
        rot = urot[st]
        b0, b1 = rot.next(), rot.next()
        psG, psKQ = C.psum[b0], C.psum[b1]

        def fg(e):
            ins = None
            for k in range(4):
                sl = slice((tg * 4 + k) * 128, (tg * 4 + k + 1) * 128)
                ins = e.matmul(psG[:, k * 128:(k + 1) * 128], kT[:, sl], kT[:, sl], start=True, stop=True)
            return ins

        def fkq(e):
            ins = None
            for k in range(4):
                sl = slice((tg * 4 + k) * 128, (tg * 4 + k + 1) * 128)
                ins = e.matmul(psKQ[:, k * 128:(k + 1) * 128], kT[:, sl], qT[:, sl], start=True, stop=True)
            return ins
        P.add("pe", fg, reads=["kT"], writes=[("ps", b0)])
        P.add("pe", fkq, reads=["kT", "qT"], writes=[("ps", b1)])
        yield
        for d in range(2):
            P.stt("dve", nGMs[d], p4(psG), -1.0, b4(Ms[d]), ALU.mult, ALU.mult, reads=[("ps", b0)],
                  writes=[("nGMs", st, d)])
            yield
        P.tt("dve", KQMt2[st], p4(psKQ), b4(Mt[1 - st]), ALU.mult, reads=[("ps", b1)], writes=[("KQMt", st)])
        yield

    def pre_unit(hv, tg, d):
        scr = SCR[d]
        nGMs = nGMs2[d]
        KQMt_d = KQMt2[d]
        t1, t2, tmp, egr, Qb, Pb, Rb, bV, Kp = scr.t1, scr.t2, scr.tmp, scr.egr, scr.Qb, scr.Pb, scr.Rb, scr.bV, scr.Kp
        sk = lambda n, *a: (n, d) + a
        tsl = slice(tg * 512, (tg + 1) * 512)
        cgc = [32 + hv, hv][d]
        cbe = [64 + hv, 96 + hv][d]
        rot = urot[d]
        bD = rot.next()
        psD = C.psum[bD]

        def fdiff(e):
            ins = None
            for k in range(4):
                sl = slice((tg * 4 + k) * 128, (tg * 4 + k + 1) * 128)
                e.matmul(psD[:, k * 128:(k + 1) * 128], sel[d], BG[:, sl], start=True, stop=False)
                ins = e.matmul(psD[:, k * 128:(k + 1) * 128], BG[:, sl], nsel[d], start=False, stop=True)
            return ins
        P.add("pe", fdiff, reads=[("sel", d), ("nsel", d)], writes=[("ps", bD)])
        yield
        P.ts("dve", t1, p4(psD), 0.0, None, ALU.max, None, reads=[("ps", bD)], writes=[sk("t1")])
        P.ts("dve", t2, p4(psD), 0.0, None, ALU.min, None, reads=[("ps", bD)], writes=[sk("t2")])
        bRg = rot.next()
        psRg = C.psum[bRg]
        P.add("pe", lambda e: e.matmul(psRg[:, :], sel[d], BG[:, tsl], start=True, stop=True),
              reads=[("sel", d)], writes=[("ps", bRg)])
        yield
        P.act(t1, t1, AF.Exp, reads=[sk("t1")], writes=[sk("t1")], scale=-1.0)
        P.act(t2, t2, AF.Exp, reads=[sk("t2")], writes=[sk("t2")])
        P.act(egr, psRg[:, :], AF.Exp, reads=[("ps", bRg)], writes=[sk("egr")])
        bRb = rot.next()
        psRb = C.psum[bRb]
        P.add("pe", lambda e: e.matmul(psRb[:, :], sel[2 + d], BG[:, tsl], start=True, stop=True),
              reads=[("sel", 2 + d)], writes=[("ps", bRb)])
        yield
        for k in range(4):
            tau = tg * 4 + k
            bei = TM1[:, tau, cbe:cbe + 1]
            P.stt("dve", Qb[0][:, k, :], t1[:, k, :], bei, nGMs[d][:, k, :], ALU.mult, ALU.mult,
                  reads=[sk("t1"), ("nGMs", d, d)], writes=[sk("Q", 0, k)])
        yield
        P.tt("dve", tmp, t2, p4(psRb), ALU.mult, reads=[sk("t2"), ("ps", bRb)], writes=[sk("tmp")])
        P.tt("dve", Pb[0], tmp, nGMs[1 - d], ALU.mult, reads=[sk("tmp"), ("nGMs", d, 1 - d)], writes=[sk("P", 0)])
        yield
        P.tt("dve", Rb[0], Pb[0], b4(C.identf), ALU.add, reads=[sk("P", 0)], writes=[sk("R", 0)])
        P.tt("dve", AT[d][:, tg * 4:(tg + 1) * 4, :], t2, KQMt_d, ALU.mult, reads=[sk("t2"), ("KQMt", d)],
             writes=[("AT", d, tg)])
        yield
        P.tt("dve", QgT[d][:, tsl], qT[:, tsl], egr, ALU.mult, reads=["qT", sk("egr")], writes=[("QgT", d, tg)])
        for k in range(4):
            tau = tg * 4 + k
            bei = TM1[:, tau, cbe:cbe + 1]
            if DEBUG.get("pe2", "dve") == "pool":
                bc = lambda ap: ap.broadcast_to([128, 128])
                P.tt("pool", bV[:, k, :], Vtm[:, tau, :], bc(bei), ALU.mult, reads=[("Vtm", tau // 8)],
                     writes=[sk("bV", k)])
                P.tt("pool", Kp[:, k, :], Ktm[:, tau, :], bc(TMB[:, tau, cgc:cgc + 1]), ALU.mult,
                     reads=[("Ktm", tau // 8)], writes=[sk("Kp", k)])
                P.tt("pool", Kd[d][:, tau, :], Ktm[:, tau, :], bc(TMK[:, tau, cgc:cgc + 1]), ALU.mult,
                     reads=[("Ktm", tau // 8)], writes=[("Kd", d, tau)])
            else:
                if DEBUG.get("nomul"):
                    P.ts("dve", bV[:, k, :], Vtm[:, tau, :], bei, None, ALU.mult, None, reads=[("Vtm", tau // 8)],
                         writes=[sk("bV", k)])
                    P.ts("dve", Kp[:, k, :], Ktm[:, tau, :], TMB[:, tau, cgc:cgc + 1], None, ALU.mult, None,
                         reads=[("Ktm", tau // 8)], writes=[sk("Kp", k)])
                    P.ts("dve", Kd[d][:, tau, :], Ktm[:, tau, :], TMK[:, tau, cgc:cgc + 1], None, ALU.mult, None,
                         reads=[("Ktm", tau // 8)], writes=[("Kd", d, tau)])
                    continue
                P.add("act", lambda e, k=k, tau=tau, bei=bei: e.mul(out=bV[:, k, :], in_=Vtm[:, tau, :], mul=bei),
                      reads=[("Vtm", tau // 8)], writes=[sk("bV", k)])
                P.add("act", lambda e, k=k, tau=tau: e.mul(out=Kp[:, k, :], in_=Ktm[:, tau, :],
                                                           mul=TMB[:, tau, cgc:cgc + 1]),
                      reads=[("Ktm", tau // 8)], writes=[sk("Kp", k)])
                P.add("act", lambda e, tau=tau: e.mul(out=Kd[d][:, tau, :], in_=Ktm[:, tau, :],
                                                      mul=TMK[:, tau, cgc:cgc + 1]),
                      reads=[("Ktm", tau // 8)], writes=[("Kd", d, tau)])
        yield
        qk = [sk("Q", 0, k) for k in range(4)]
        cur = 0
        rc = 0
        for m in range(1, 6):
            nxt = 1 - cur
            bA = rot.next()
            psA = C.psum[bA]

            def fq(e, psA=psA, cur=cur):
                ins = None
                for k in range(4):
                    ins = e.matmul(psA[:, k * 128:(k + 1) * 128], Pb[cur][:, k, :], Qb[cur][:, k, :],
                                   start=True, stop=True)
                return ins
            P.add("pe", fq, reads=qk + [sk("P", cur)], writes=[("ps", bA)])
            if m < 5:
                bB = rot.next()
                psB = C.psum[bB]

                def fp(e, psB=psB, cur=cur):
                    ins = None
                    for k in range(4):
                        ins = e.matmul(psB[:, k * 128:(k + 1) * 128], Qb[cur][:, k, :], Pb[cur][:, k, :],
                                       start=True, stop=True)
                    return ins
                P.add("pe", fp, reads=qk + [sk("P", cur)], writes=[("ps", bB)])
            P.copy("act", Qb[nxt], p4(psA), reads=[("ps", bA)], writes=[sk("Q", nxt)])
            if m < 5:
                P.copy("act", Pb[nxt], p4(psB), reads=[("ps", bB)], writes=[sk("P", nxt)])
            yield
            bC = rot.next()
            psC = C.psum[bC]

            def fr(e, psC=psC, nxt=nxt, rc=rc):
                ins = None
                for k in range(4):
                    e.matmul(psC[:, k * 128:(k + 1) * 128], Qb[nxt][:, k, :], Rb[rc][:, k, :], start=True, stop=False)
                    ins = e.matmul(psC[:, k * 128:(k + 1) * 128], C.identb, Rb[rc][:, k, :], start=False, stop=True)
                return ins
            P.add("pe", fr, reads=[sk("Q", nxt), sk("R", rc)], writes=[("ps", bC)])
            P.copy("act", Rb[1 - rc], p4(psC), reads=[("ps", bC)], writes=[sk("R", 1 - rc)])
            yield
            rc = 1 - rc
            cur = nxt
            qk = [sk("Q", cur)]
        TT = Rb[rc]
        ttk = sk("R", rc)
        bU = rot.next()
        psU = C.psum[bU]

        def fu(e):
            ins = None
            for k in range(4):
                ins = e.matmul(psU[:, k * 128:(k + 1) * 128], TT[:, k, :], bV[:, k, :], start=True, stop=True)
            return ins
        P.add("pe", fu, reads=[ttk] + [sk("bV", k) for k in range(4)], writes=[("ps", bU)])
        P.copy("act", U[d][:, tg * 4:(tg + 1) * 4, :], p4(psU), reads=[("ps", bU)], writes=[("U", d, tg)])
        yield
        bW = rot.next()
        psW = C.psum[bW]

        def fw(e):
            ins = None
            for k in range(4):
                ins = e.matmul(psW[:, k * 128:(k + 1) * 128], Kp[:, k, :], TT[:, k, :], start=True, stop=True)
            return ins
        P.add("pe", fw, reads=[ttk] + [sk("Kp", k) for k in range(4)], writes=[("ps", bW)])
        P.copy("act", WT[d][:, tsl], psW[:, :], reads=[("ps", bW)], writes=[("WT", d, tg)])
        yield

    def seq_group(hv, u, dirs=(0, 1)):
        eg = egt[hv % 2]
        if u == 0:
            for d in dirs:
                P.memset("dve", S[d], 0.0, writes=[("S", d)])
                P.memset("dve", Sbf[d], 0.0, writes=[("Sbf", d)])
        for st in range(8):
            step = u * 8 + st
            for d in dirs:
                c = step if d == 0 else 31 - step
                tau, hf = c // 2, c % 2
                tg = tau // 4
                r0 = 64 * hf
                bq = 1 + d
                psq = C.psum[bq]
                pso = C.psum[0]
                P.add("pe", lambda e, d=d, tau=tau, psq=psq: e.matmul(
                    psq[:, 0:128], WT[d][:, tau * 128:(tau + 1) * 128], Sbf[d], start=True, stop=True),
                    reads=[("WT", d, tg), ("Sbf", d)], writes=[("ps", bq)])
                P.tt("dve", vnew[d][r0:r0 + 64, :], U[d][r0:r0 + 64, tau, :], psq[r0:r0 + 64, 0:128], ALU.subtract,
                     reads=[("U", d, tg), ("ps", bq)], writes=[("vnew", d)])
                cs = d * 256 + (c % 4) * 64

                def fo(e, d=d, c=c, tau=tau, r0=r0, cs=cs, psq=psq):
                    e.matmul(pso[:, cs:cs + 64], Sbf[d], QgT[d][:, c * 64:(c + 1) * 64], start=True, stop=False)
                    e.matmul(pso[:, cs:cs + 64], vnew[d][r0:r0 + 64, :], AT[d][r0:r0 + 64, tau, r0:r0 + 64],
                             start=False, stop=True)
                    return e.matmul(psq[:, 128:256], Kd[d][r0:r0 + 64, tau, :], vnew[d][r0:r0 + 64, :],
                                    start=True, stop=True)
                P.add("pe", fo, reads=[("Sbf", d), ("QgT", d, tg), ("vnew", d), ("AT", d, tg), ("Kd", d, tau)],
                      writes=[("ps", 0), ("ps", bq)])
                P.stt("dve", Sbf[d], S[d], eg[d][:, c:c + 1], psq[:, 128:256], ALU.mult, ALU.add,
                      reads=[("S", d), ("egt", hv % 2, d), ("ps", bq)], writes=[("Sbf", d)])
                P.stt("dve", S[d], S[d], eg[d][:, c:c + 1], psq[:, 128:256], ALU.mult, ALU.add,
                      reads=[("S", d), ("egt", hv % 2, d), ("ps", bq)], writes=[("S", d)])
                last = (c % 4 == 3) if d == 0 else (c % 4 == 0)
                if last:
                    g4 = c // 4
                    P.copy("act", oT[d][:, g4 * 256:(g4 + 1) * 256], pso[:, d * 256:(d + 1) * 256],
                           reads=[("ps", 0)], writes=[("oT", d, g4)])
                yield

    def finalize(hv):
        P.dma("sp", sg, C.sgS[hv], reads=[], writes=["sg"])
        for tb in range(NTB):
            tsl = slice(tb * 512, (tb + 1) * 512)
            i2 = tb % 2
            otk = [("oT", d, g4) for d in range(2) for g4 in (2 * tb, 2 * tb + 1)]
            P.tt("dve", oT[0][:, tsl], oT[0][:, tsl], oT[1][:, tsl], ALU.add, reads=otk, writes=[("o", tb)])
            P.act(osq[i2], oT[0][:, tsl], AF.Square, reads=[("o", tb)], writes=[("osq", i2)])
            b = 1 + (tb % 2)
            pst = C.psum[b]
            P.add("pe", lambda e, pst=pst, i2=i2: e.matmul(pst[:, :], C.ones1, osq[i2], start=True, stop=True),
                  reads=[("osq", i2)], writes=[("ps", b)])
            P.act(rs[i2], pst[:, :], AF.Ln, reads=[("ps", b)], writes=[("rs", i2)],
                  bias=C.vecs[:, V_EPS:V_EPS + 1], scale=1.0 / 128)
            yield
            P.act(rs[i2], rs[i2], AF.Exp, reads=[("rs", i2)], writes=[("rs", i2)], scale=-0.5)
            P.stt("dve", osq[i2], oT[0][:, tsl], nwcol, rs[i2], ALU.mult, ALU.mult,
                  reads=[("o", tb), ("rs", i2), ("osq", i2)], writes=[("osq", i2)])
            P.tt("dve", yst[:, tsl], osq[i2], sg[:, tsl], ALU.mult, reads=[("osq", i2), "sg"], writes=[("yst", tb)])
            yield
        P.dma("sp", C.yS[hv], yst, reads=[("yst", tb) for tb in range(NTB)], writes=[("yS", hv)])
        yield

    def record(gen):
        P.rec = []
        for _ in gen:
            pass
        out = P.rec
        P.rec = None
        return out

    def merge(*lists):
        lists = [l for l in lists if l]
        pos = [0] * len(lists)
        out = []
        while True:
            best, bf = -1, 2.0
            for i, l in enumerate(lists):
                if pos[i] < len(l):
                    f = pos[i] / len(l)
                    if f < bf:
                        best, bf = i, f
            if best < 0:
                break
            out.append(lists[best][pos[best]])
            pos[best] += 1
        return out

    def pre_ops(hv, u):
        a = record(tg_shared(hv, u, 0)) + record(pre_unit(hv, u, 0))
        b_ = record(tg_shared(hv, 3 - u, 1)) + record(pre_unit(hv, 3 - u, 1))
        ops = merge(a, b_)
        if u == 0:
            ops = record(head_setup(hv)) + ops
        return ops

    def seq_ops(hv, u):
        ops = merge(record(seq_group(hv, u, (0,))), record(seq_group(hv, u, (1,))))
        if u == 3:
            ops = ops + record(finalize(hv))
        return ops

    def play(ops):
        for (eng, fn, reads, writes, dma) in ops:
            P.add(eng, fn, reads, writes, dma)

    groups = [(hv, u) for hv in range(DEBUG.get('nhv', 32)) for u in range(4)]
    play(pre_ops(0, 0))
    for gi, (hv, u) in enumerate(groups):
        so = seq_ops(hv, u)
        po = pre_ops(*groups[gi + 1]) if gi + 1 < len(groups) else []
        play(merge(so, po))
```

```python
import numpy as np
from contextlib import ExitStack
import concourse.bass as bass
import concourse.mybir as mybir
from concourse.bass_utils import run_bass_kernel_spmd

F32 = mybir.dt.float32
BF16 = mybir.dt.bfloat16
AF = mybir.ActivationFunctionType
ALU = mybir.AluOpType
AX = mybir.AxisListType

D = 2048
T = 2048
NMEM = 256
DEPTH = 4
EPS = 1e-6
NTB = T // 512
KC = D // 128
POOL_WINDOWS = (2, 4, 8, 16)
DEBUG = {}


class _Op:
    __slots__ = ("eng", "fn", "deps", "is_dma", "has_dep", "sig", "idx")

    def __init__(self, eng, fn, is_dma):
        self.eng = eng
        self.fn = fn
        self.deps = set()
        self.is_dma = is_dma
        self.has_dep = False
        self.sig = 0
        self.idx = 0


class Prog:
    CE = ("pe", "dve", "act", "pool")
    QE = ("sp", "act", "pool")
    ALLE = ("pe", "dve", "act", "pool", "sp")
    NSLOT = 8

    def __init__(self, nc):
        self.nc = nc
        self.stream = {e: [] for e in self.ALLE}
        self.last_w = {}
        self.rd_c = {}
        self.rd_d = {}
        self.ndma = {q: 0 for q in self.QE}
        self.dmas_since_barrier = []

    rec = None

    def add(self, eng, fn, reads=(), writes=(), dma=False):
        if self.rec is not None:
            self.rec.append((eng, fn, tuple(reads), tuple(writes), dma))
            return None
        op = _Op(eng, fn, dma)
        deps = op.deps
        for r in reads:
            w = self.last_w.get(r)
            if w is not None:
                deps.add(w)
        for r in writes:
            w = self.last_w.get(r)
            if w is not None:
                deps.add(w)
            for o in self.rd_c.get(r, {}).values():
                deps.add(o)
            for o in self.rd_d.get(r, ()):
                deps.add(o)
        if eng == "pe":
            for d_ in [d_ for d_ in deps if d_.eng == "pe" and not d_.is_dma]:
                deps.discard(d_)
        for r in writes:
            self.last_w[r] = op
            self.rd_c[r] = {}
            self.rd_d[r] = []
        for r in reads:
            if dma:
                self.rd_d.setdefault(r, []).append(op)
            else:
                self.rd_c.setdefault(r, {})[eng] = op
        if dma:
            op.idx = self.ndma[eng]
            self.ndma[eng] += 1
            self.dmas_since_barrier.append(op)
        self.stream[eng].append(op)
        return op

    def dma(self, q, out, in_, reads, writes):
        return self.add(q, lambda e: e.dma_start(out=out, in_=in_), reads, writes, dma=True)

    def act(self, out, in_, func, reads, writes, **kw):
        return self.add("act", lambda e: e.activation(out=out, in_=in_, func=func, **kw), reads, writes)

    def tt(self, eng, out, in0, in1, op, reads, writes):
        return self.add(eng, lambda e: e.tensor_tensor(out=out, in0=in0, in1=in1, op=op), reads, writes)

    def ts(self, eng, out, in0, s1, s2, op0, op1, reads, writes):
        if op1 is None:
            return self.add(eng, lambda e: e.tensor_scalar(out=out, in0=in0, scalar1=s1, scalar2=None, op0=op0),
                            reads, writes)
        return self.add(eng, lambda e: e.tensor_scalar(out=out, in0=in0, scalar1=s1, scalar2=s2, op0=op0, op1=op1),
                        reads, writes)

    def stt(self, eng, out, in0, scalar, in1, op0, op1, reads, writes):
        return self.add(eng, lambda e: e.scalar_tensor_tensor(out=out, in0=in0, scalar=scalar, in1=in1,
                                                              op0=op0, op1=op1), reads, writes)

    def copy(self, eng, out, in_, reads, writes):
        if eng == "act":
            return self.add("act", lambda e: e.copy(out=out, in_=in_), reads, writes)
        return self.add(eng, lambda e: e.tensor_copy(out=out, in_=in_), reads, writes)

    def memset(self, eng, ap, val, writes):
        return self.add(eng, lambda e: e.memset(ap, val), (), writes)

    def mmgroup(self, out, pairs, reads, writes):
        n = len(pairs)

        def fn(e):
            ins = None
            for i, (l, r) in enumerate(pairs):
                ins = e.matmul(out, l, r, start=(i == 0), stop=(i == n - 1))
            return ins
        return self.add("pe", fn, reads, writes)

    def transpose(self, out, in_, ident, reads, writes):
        return self.add("pe", lambda e: e.transpose(out, in_, ident), reads, writes)

    def barrier(self):
        tails = [self.stream[e][-1] for e in self.CE if self.stream[e] and not self.stream[e][-1].is_dma]
        for e in self.CE:
            for o in reversed(self.stream[e]):
                if not o.is_dma:
                    tails.append(o)
                    break
        b = _Op("sp", lambda e: e.nop(), False)
        b.deps = set(tails) | set(self.dmas_since_barrier)
        self.stream["sp"].append(b)
        self.dmas_since_barrier = []
        for e in self.CE:
            o = _Op(e, lambda en: en.nop(), False)
            o.deps = {b}
            self.stream[e].append(o)
        self.last_w = {}
        self.rd_c = {}
        self.rd_d = {}
        return b

    def emit(self):
        nc = self.nc
        for ops in self.stream.values():
            for op in ops:
                for d_ in op.deps:
                    d_.has_dep = True
        for e, ops in self.stream.items():
            c = 0
            for op in ops:
                if (not op.is_dma) and op.has_dep:
                    c += 1
                    op.sig = c
        NS = self.NSLOT
        with ExitStack() as st:
            sems = {e: st.enter_context(nc.semaphore(f"s_{e}")) for e in self.ALLE}
            dsems = {q: [st.enter_context(nc.semaphore(f"d_{q}{i}")) for i in range(NS)] for q in self.QE}
            block = st.enter_context(nc.Block())

            def run(ename, eng):
                waited = {e: 0 for e in self.ALLE}
                waited_d = {}
                for op in self.stream[ename]:
                    need = {}
                    needd = {}
                    for d_ in op.deps:
                        if d_.is_dma:
                            k = (d_.eng, d_.idx % NS)
                            rnd = d_.idx // NS + 1
                            if waited_d.get(k, 0) < rnd:
                                needd[k] = max(needd.get(k, 0), rnd)
                        else:
                            if waited[d_.eng] < d_.sig:
                                need[d_.eng] = max(need.get(d_.eng, 0), d_.sig)
                    if op.is_dma and op.idx >= NS:
                        k = (ename, op.idx % NS)
                        rnd = op.idx // NS
                        if waited_d.get(k, 0) < rnd:
                            needd[k] = max(needd.get(k, 0), rnd)
                    for e2, v in need.items():
                        eng.wait_ge(sems[e2], v)
                        waited[e2] = v
                    for (q, slot), r in needd.items():
                        eng.wait_ge(dsems[q][slot], 16 * r)
                        waited_d[(q, slot)] = r
                    ins = op.fn(eng)
                    if op.is_dma:
                        ins.then_inc(dsems[ename][op.idx % NS], 16)
                    elif op.has_dep:
                        ins.then_inc(sems[ename], 1)

            @block.tensor
            def _(e):
                run("pe", e)

            @block.vector
            def _(e):
                run("dve", e)

            @block.scalar
            def _(e):
                run("act", e)

            @block.gpsimd
            def _(e):
                run("pool", e)

            @block.sync
            def _(e):
                run("sp", e)


class Arena:
    def __init__(self, t):
        self.t = t

    def view(self, off, shape, dt):
        n = int(np.prod(shape))
        esz = 4 if dt == F32 else 2
        assert off % 4 == 0 and (n * esz) % 4 == 0
        ap = self.t[:, off // 4:(off + n * esz) // 4]
        if dt != F32:
            ap = ap.bitcast(dt)
        if len(shape) == 2:
            ap = ap.rearrange("p (a b) -> p a b", a=shape[0])
        elif len(shape) == 3:
            ap = ap.rearrange("p (a b c) -> p a b c", a=shape[0], b=shape[1])
        return ap


ARENA_BYTES = 207 * 1024
O_HT = 0
O_WB = 65536
O_R1 = O_WB + 3 * 8192
O_R2 = O_R1 + 32768
O_SG = O_R2 + 33792
O_KV = O_SG + 16384
O_CONST = O_KV + 24576
C_IDB = O_CONST
C_IDF = C_IDB + 256
C_ONE = C_IDF + 512
C_VEC = C_ONE + 512
NVEC = 864
C_RSTD = C_VEC + 4 * 864
C_MRSTD = C_RSTD + 2048
C_SM = C_MRSTD + 1024
C_ONE1 = C_SM + 64
C_MASK = C_ONE1 + 512
C_END = C_MASK + 2048
assert NVEC <= 864 and C_END <= ARENA_BYTES, (NVEC, C_END)

V_NORM = 0
V_MNORM = 64
V_FIN = 128
V_PSCALE = 144
V_CONV = 208
V_DNNORM = 848
V_EPS = 850
V_ONE = 851
V_DTB = 852
V_ALOG = 854


class Ctx:
    pass


class WStream:
    def __init__(self, P, A, base, nslots=3, slot_bytes=8192):
        self.P, self.A, self.base, self.n, self.sb = P, A, base, nslots, slot_bytes
        self.i = 0
        self.cache = {}
        self.owner = [None] * nslots

    def get(self, key, src, shape):
        if key in self.cache:
            return self.cache[key]
        s = self.i % self.n
        self.i += 1
        if self.owner[s] is not None:
            del self.cache[self.owner[s]]
        self.owner[s] = key
        view = self.A.view(self.base + s * self.sb, shape, BF16)
        fshape = list(src.shape[1:])
        flat = self.A.view(self.base + s * self.sb, fshape, BF16)
        assert int(np.prod(fshape)) == int(np.prod(shape)), (fshape, shape)
        self.P.dma("pool", flat, src, reads=[], writes=[("wb", self.base, s)])
        self.cache[key] = (("wb", self.base, s), view)
        return self.cache[key]


class Rot:
    def __init__(self, items):
        self.items = items
        self.i = 0

    def next(self):
        x = self.items[self.i % len(self.items)]
        self.i += 1
        return x


def build_program(layers=(0, 1, 2, 3), final=True, dbg_out=None):
    nc = bass.Bass("TRN2", target_bir_lowering=False)
    C = Ctx()
    C.nc = nc

    def din(name, shape, dt=F32):
        return nc.dram_tensor(name, shape, dt, kind="ExternalInput").ap()

    C.xT_in = din("xT", [D, T])
    C.memT_in = din("memT", [D, NMEM])
    C.vecs_in = din("vecs", [128, NVEC])
    C.cmat_in = din("cmat", [128, 3 * 128])
    C.cbf_in = din("cbf", [128, 128], BF16)
    C.redge_in = din("redge", [128, 64])
    C.cmask_in = din("cmask", [128, T])
    C.masks_in = din("masks", [128, 4 * 128])
    C.w_kvk = din("w_kvk", [DEPTH, 16, 128, 16 * 128])
    C.w_kvv = din("w_kvv", [DEPTH, 8, 128, 16 * 256])
    C.w_out = din("w_out", [DEPTH, 16, 128, 48 * 128])
    C.pw_in = din("pw_in", [2, 96, 128, 16 * 128])
    C.pw_g = din("pw_g", [2, 32, 128, 8 * 128])
    C.dw_in = din("dw_in", [2, 129, 128, 16 * 128])
    C.outT = nc.dram_tensor("outT", [D, T], F32, kind="ExternalOutput").ap()
    C.xS = nc.dram_tensor("xS", [D, T], F32).ap()
    skind = "ExternalOutput" if DEBUG.get("dump") else "Internal"
    C.yS = nc.dram_tensor("yS", [48, 128, T], BF16, kind=skind).ap()
    C.qS = nc.dram_tensor("qS", [16, 128, T], BF16, kind=skind).ap()
    C.kS = nc.dram_tensor("kS", [16, 128, T], BF16, kind=skind).ap()
    C.vS = nc.dram_tensor("vS", [32, 128, T], BF16, kind=skind).ap()
    C.sgS = nc.dram_tensor("sgS", [32, 128, T], BF16, kind=skind).ap()
    C.baS = nc.dram_tensor("baS", [128, T], F32, kind=skind).ap()

    with ExitStack() as st:
        arena_t = st.enter_context(nc.sbuf_tensor("arena", [128, ARENA_BYTES // 4], F32))
        C.A = A = Arena(arena_t)
        C.psum = [st.enter_context(nc.psum_tensor(f"ps{i}", [128, 512], F32)) for i in range(8)]
        C.P = P = Prog(nc)

        C.identb = A.view(C_IDB, [128], BF16)
        C.identf = A.view(C_IDF, [128], F32)
        C.onesm = A.view(C_ONE, [128], F32)
        C.vecs = A.view(C_VEC, [NVEC], F32)
        C.rstd = A.view(C_RSTD, [512], F32)
        C.sm = A.view(C_SM, [16], F32)
        C.redge = A.view(C_MRSTD, [64], F32)
        P.dma("sp", C.identb, C.cbf_in, [], ["c0"])
        P.dma("sp", C.identf, C.cmat_in[:, 0:128], [], ["c1"])
        P.dma("sp", C.onesm, C.cmat_in[:, 128:256], [], ["c2"])
        P.dma("sp", C.vecs, C.vecs_in, [], ["c3"])
        P.dma("sp", C.redge, C.redge_in, [], ["c4"])
        C.ones1 = A.view(C_ONE1, [128], F32)
        C.masks = A.view(C_MASK, [4, 128], F32)
        P.dma("sp", C.ones1, C.cmat_in[:, 256:384], [], ["c5"])
        P.dma("sp", C.masks, C.masks_in.rearrange("p (a b) -> p a b", a=4), [], ["c6"])
        P.barrier()

        xsrc = C.xT_in
        for li in layers:
            if DEBUG.get("dn_only"):
                dn_core(C, li)
                continue
            phase_norm(C, xsrc, C.vecs[:, V_NORM + li * 16: V_NORM + li * 16 + 16])
            phase_memkv(C, li)
            P.barrier()
            if li % 2 == 0:
                phase_pool(C, li)
            else:
                phase_dn(C, li)
            P.barrier()
            phase_out(C, li, xsrc)
            P.barrier()
            xsrc = C.xS
        if final:
            phase_final(C, xsrc)
        else:
            for oc in range(16):
                P.dma("sp", C.outT[oc * 128:(oc + 1) * 128, :], xsrc[oc * 128:(oc + 1) * 128, :], reads=[], writes=[("o", oc)])
        P.barrier()
        P.emit()
    return nc


def rsqrt_eps(C, out, in_, reads, wkey):
    P = C.P
    P.act(out, in_, AF.Ln, reads=list(reads), writes=[wkey], bias=C.vecs[:, V_EPS:V_EPS + 1], scale=1.0)
    P.act(out, out, AF.Exp, reads=[wkey], writes=[wkey], scale=-0.5)


def phase_norm(C, src, wcol):
    P, A = C.P, C.A
    hT = A.view(O_HT, [16, T], BF16)
    xt = A.view(O_R1, [16, 512], F32)
    sq = [A.view(O_R2 + i * 2048, [512], F32) for i in range(2)]
    srcv = src.rearrange("(kc p) t -> p kc t", p=128)
    ps = C.psum[6]
    for tb in range(NTB):
        P.dma("sp", xt, srcv[:, :, tb * 512:(tb + 1) * 512], reads=[], writes=["xt"])
        for kc in range(16):
            P.act(sq[kc % 2], xt[:, kc, :], AF.Square, reads=["xt"], writes=[("sq", kc % 2)])
            P.add("pe", (lambda kc=kc: lambda e: e.matmul(ps[:, :], C.onesm, sq[kc % 2],
                                                           start=(kc == 0), stop=(kc == 15)))(),
                  reads=[("sq", kc % 2)], writes=[("ps", 6)] if kc in (0, 15) else [])
        rsqrt_eps(C, C.rstd, ps[:, :], [("ps", 6)], "rstd")
        for kc in range(16):
            eng = "dve"
            P.stt(eng, hT[:, kc, tb * 512:(tb + 1) * 512], xt[:, kc, :], wcol[:, kc:kc + 1], C.rstd,
                  ALU.mult, ALU.mult, reads=["xt", "rstd"], writes=[("hT", tb, kc)])


def phase_memkv(C, li):
    P, A = C.P, C.A
    memf = A.view(O_R2 + 4096, [16, NMEM], F32)
    sq = [A.view(O_R2 + 4096 + 16384 + i * 1024, [NMEM], F32) for i in range(2)]
    mrstd = A.view(O_R2 + 4096 + 16384 + 2048, [NMEM], F32)
    kT = A.view(O_KV, [16, NMEM], BF16)
    v = A.view(O_KV + 8192, [2, D], BF16)
    memn = A.view(O_KV + 16384, [16, NMEM], BF16)
    wcol = C.vecs[:, V_MNORM + li * 16: V_MNORM + li * 16 + 16]
    ps = C.psum[7]
    P.dma("sp", memf, C.memT_in.rearrange("(kc p) m -> p kc m", p=128), reads=[], writes=["memf"])
    for kc in range(16):
        P.act(sq[kc % 2], memf[:, kc, :], AF.Square, reads=["memf"], writes=[("msq", kc % 2)])
        P.add("pe", (lambda kc=kc: lambda e: e.matmul(ps[:, 0:NMEM], C.onesm, sq[kc % 2],
                                                       start=(kc == 0), stop=(kc == 15)))(),
              reads=[("msq", kc % 2)], writes=[("ps", 7)] if kc in (0, 15) else [])
    rsqrt_eps(C, mrstd, ps[:, 0:NMEM], [("ps", 7)], "mrstd")
    for kc in range(16):
        P.stt("dve", memn[:, kc, :], memf[:, kc, :], wcol[:, kc:kc + 1], mrstd, ALU.mult, ALU.mult,
              reads=["memf", "mrstd"], writes=[("memn", kc)])
    memn_keys = [("memn", kc) for kc in range(16)]
    W = WStream(P, A, O_WB)
    rot = Rot([4, 5])
    for c in range(16):
        pair = c // 2
        wkey, wv = W.get(("kvk", li, pair),
                         C.w_kvk[li, pair * 2:pair * 2 + 2].rearrange("n p f -> p n f"), [2, 16, 128])
        b = rot.next()
        pst = C.psum[b]
        P.mmgroup(pst[:, 0:NMEM], [(wv[:, c % 2, kc, :], memn[:, kc, :]) for kc in range(16)],
                  reads=[wkey] + memn_keys, writes=[("ps", b)])
        P.copy("act" if c % 2 else "dve", kT[:, c, :], pst[:, 0:NMEM], reads=[("ps", b)], writes=[("kT", c)])
    for blk in range(8):
        wkey, wv = W.get(("kvv", li, blk), C.w_kvv[li, blk], [16, 256])
        for mc in range(2):
            b = rot.next()
            pst = C.psum[b]
            P.mmgroup(pst[:, 0:256], [(memn[:, kc, mc * 128:(mc + 1) * 128], wv[:, kc, :]) for kc in range(16)],
                      reads=[wkey] + memn_keys, writes=[("ps", b)])
            P.copy("act" if mc else "dve", v[:, mc, blk * 256:(blk + 1) * 256], pst[:, 0:256],
                   reads=[("ps", b)], writes=[("v", mc, blk)])


def gemm_cols(C, W, wsrc_fn, col, rot, epilogue, hT):
    P = C.P
    pair = col // 2
    src = wsrc_fn("src", pair)
    wkey, wv = W.get(wsrc_fn("key", pair), src, [int(src.shape[1]), 16, 128])
    for tb in range(NTB):
        b = rot.next()
        pst = C.psum[b]
        P.mmgroup(pst[:, :], [(wv[:, col % 2, kc, :], hT[:, kc, tb * 512:(tb + 1) * 512]) for kc in range(16)],
                  reads=[wkey], writes=[("ps", b)])
        epilogue(tb, pst, ("ps", b))


def gate_chunk(C, W, wsrc_fn, col, rot, hT, sg, sgkey):
    P = C.P

    def epi(tb, pst, pkey):
        P.act(sg[:, tb * 512:(tb + 1) * 512], pst[:, :], AF.Silu, reads=[pkey], writes=[(sgkey, tb)])
    gemm_cols(C, W, wsrc_fn, col, rot, epi, hT)


def cross_attn(C, W, wsrc_fn, rot, hT, xq_col0, gate_col0, li):
    P, A = C.P, C.A
    kT = A.view(O_KV, [16, NMEM], BF16)
    v = A.view(O_KV + 8192, [2, D], BF16)
    xqT = A.view(O_R2, [4, T], BF16)
    pT = A.view(O_R2 + 16384, [2, T], BF16)
    p32 = [A.view(O_R2 + 24576 + i * 1024, [NMEM], F32) for i in range(2)]
    pbf = [A.view(O_R2 + 26624 + i * 512, [NMEM], BF16) for i in range(2)]
    sgb = [A.view(O_SG + i * 4096, [T], BF16) for i in range(2)]
    yst = [A.view(O_SG + 8192 + i * 4096, [T], BF16) for i in range(2)]
    scale = float(512 ** -0.5)
    sm = C.sm
    cnt = 0
    rot = Rot([0, 1, 2, 3, 4])
    for hd in range(4):
        for dc in range(4):
            def epi(tb, pst, pkey, dc=dc):
                P.copy("dve" if tb % 2 else "act", xqT[:, dc, tb * 512:(tb + 1) * 512], pst[:, :],
                       reads=[pkey], writes=[("xqT", dc, tb)])
            gemm_cols(C, W, wsrc_fn, xq_col0 + hd * 4 + dc, rot, epi, hT)
        def scores(stl):
            bs = 5 + (stl % 2)
            pst_ = C.psum[bs]
            P.mmgroup(pst_[:, 0:NMEM], [(xqT[:, dc, stl * 128:(stl + 1) * 128], kT[:, hd * 4 + dc, :])
                                        for dc in range(4)],
                      reads=[("xqT", dc, stl // 4) for dc in range(4)], writes=[("ps", bs)])
        scores(0)
        for stl in range(16):
            tb = stl // 4
            i2 = stl % 2
            bs = 5 + (stl % 2)
            pst = C.psum[bs]
            P.add("dve", lambda e, pst=pst: e.reduce_max(out=sm[:, 0:1], in_=pst[:, 0:NMEM], axis=AX.X),
                  reads=[("ps", bs)], writes=["sm0"])
            P.ts("dve", sm[:, 1:2], sm[:, 0:1], -scale, None, ALU.mult, None, reads=["sm0"], writes=["sm1"])
            P.act(p32[i2], pst[:, 0:NMEM], AF.Exp, reads=[("ps", bs), "sm1"], writes=[("p32", i2)],
                  bias=sm[:, 1:2], scale=scale)
            if stl + 1 < 16:
                scores(stl + 1)
            P.add("dve", lambda e, i2=i2: e.reduce_sum(out=sm[:, 2:3], in_=p32[i2], axis=AX.X),
                  reads=[("p32", i2)], writes=["sm2"])
            P.add("dve", lambda e: e.reciprocal(out=sm[:, 3:4], in_=sm[:, 2:3]), reads=["sm2"], writes=["sm3"])
            P.ts("dve", pbf[i2], p32[i2], sm[:, 3:4], None, ALU.mult, None, reads=[("p32", i2), "sm3"],
                 writes=[("pbf", i2)])
            pt = C.psum[7][:, 0:128].bitcast(BF16)
            for mc in range(2):
                P.transpose(pt[:, mc * 128:(mc + 1) * 128], pbf[i2][:, mc * 128:(mc + 1) * 128], C.identb,
                            reads=[("pbf", i2)], writes=[("ps", 7)])
            P.copy("act", pT[:, :, stl * 128:(stl + 1) * 128], pt.rearrange("p (a b) -> p a b", a=2),
                   reads=[("ps", 7)], writes=[("pT", stl)])
        for dc in range(4):
            j = 32 + hd * 4 + dc
            sg = sgb[cnt % 2]
            ys = yst[cnt % 2]
            gate_chunk(C, W, wsrc_fn, gate_col0 + j, rot, hT, sg, ("sg", cnt % 2))
            for tb in range(NTB):
                b = rot.next()
                pst = C.psum[b]
                P.mmgroup(pst[:, :], [(v[:, mc, hd * 512 + dc * 128: hd * 512 + (dc + 1) * 128],
                                       pT[:, mc, tb * 512:(tb + 1) * 512]) for mc in range(2)],
                          reads=[("pT", s_) for s_ in range(tb * 4, tb * 4 + 4)], writes=[("ps", b)])
                P.tt("dve", ys[:, tb * 512:(tb + 1) * 512], pst[:, :], sg[:, tb * 512:(tb + 1) * 512], ALU.mult,
                     reads=[("ps", b), (("sg", cnt % 2), tb)], writes=[("yst", cnt % 2, tb)])
            P.dma("sp", C.yS[j], ys, reads=[("yst", cnt % 2, tb) for tb in range(NTB)], writes=[("yS", j)])
            cnt += 1


def phase_pool(C, li):
    P, A = C.P, C.A
    j_ = li // 2
    hT = A.view(O_HT, [16, T], BF16)
    pg = A.view(O_R1, [8, T], BF16)
    LP = T + 32
    ubs = [A.view(O_R2 + i * LP * 4, [LP], F32) for i in range(2)]
    sA = A.view(O_R2 + 2 * LP * 4, [LP], F32)
    sB = A.view(O_R2 + 3 * LP * 4, [LP], F32)
    assert 4 * LP * 4 <= 33792
    sgb = [A.view(O_SG + i * 4096, [T], BF16) for i in range(2)]
    yst = [A.view(O_SG + 8192 + i * 4096, [T], BF16) for i in range(2)]
    W = WStream(P, A, O_WB)
    rot = Rot([0, 1, 2, 3, 4, 5])

    def wsrc(kind, pair):
        if kind == "key":
            return ("pw_in", li, pair)
        return C.pw_in[j_, pair * 2:pair * 2 + 2].rearrange("n p f -> p n f")

    for i in range(2):
        P.memset("pool", ubs[i][:, 0:16], 0.0, writes=[("ub_padl", i)])
        P.memset("pool", ubs[i][:, 16 + T:LP], 0.0, writes=[("ub_padr", i)])
    cnt = 0
    ucnt = 0
    for g in range(4):
        w = POOL_WINDOWS[g]
        half = w // 2
        for cc in range(8):
            ui = ucnt % 2
            ucnt += 1
            ub = ubs[ui]

            def epi(tb, pst, pkey, ub=ub, ui=ui):
                P.copy("act", ub[:, 16 + tb * 512:16 + (tb + 1) * 512], pst[:, :],
                       reads=[pkey], writes=[("ub", ui, tb)])
            gemm_cols(C, W, wsrc, g * 8 + cc, rot, epi, hT)
            ubk = [("ub", ui, tb) for tb in range(NTB)] + [("ub_padl", ui), ("ub_padr", ui)]
            P.tt("dve", sA[:, 0:LP - 1], ub[:, 0:LP - 1], ub[:, 1:LP], ALU.add, reads=ubk, writes=["sA"])
            cur, curk, n, sh = sA, "sA", LP - 1, 2
            oth, othk = sB, "sB"
            while sh < w:
                P.tt("dve" if sh == 2 else "pool", oth[:, 0:n - sh], cur[:, 0:n - sh], cur[:, sh:n], ALU.add,
                     reads=[curk], writes=[othk])
                cur, curk, oth, othk = oth, othk, cur, curk
                n -= sh
                sh *= 2
            off = 16 - half
            P.stt("dve", pg[:, cc, :], cur[:, off:off + T], 1.0 / w, ub[:, 16:16 + T], ALU.mult, ALU.subtract,
                  reads=[curk] + ubk, writes=[("pg", cc)])
            nl = half
            nr = half - 1
            re = C.redge[:, g * 16:(g + 1) * 16]
            P.tt("dve", C.sm[:, 4:4 + nl], cur[:, off:off + nl], re[:, 0:nl], ALU.mult, reads=[curk], writes=["edl"])
            P.tt("dve", pg[:, cc, 0:nl], C.sm[:, 4:4 + nl], ub[:, 16:16 + nl], ALU.subtract,
                 reads=["edl"] + ubk, writes=[("pg", cc)])
            if nr > 0:
                P.tt("dve", C.sm[:, 4:4 + nr], cur[:, off + T - nr:off + T], re[:, 8:8 + nr], ALU.mult,
                     reads=[curk], writes=["edl"])
                P.tt("dve", pg[:, cc, T - nr:T], C.sm[:, 4:4 + nr], ub[:, 16 + T - nr:16 + T], ALU.subtract,
                     reads=["edl"] + ubk, writes=[("pg", cc)])
        pgk = [("pg", cc) for cc in range(8)]
        for oc in range(8):
            j = g * 8 + oc
            sg = sgb[cnt % 2]
            ys = yst[cnt % 2]
            gate_chunk(C, W, wsrc, 48 + j, rot, hT, sg, ("sg", cnt % 2))
            wkey, wv = W.get(("pw_g", li, g, oc // 4),
                             C.pw_g[j_, g * 8 + (oc // 4) * 4: g * 8 + (oc // 4) * 4 + 4].rearrange("n p f -> p n f"),
                             [4, 8, 128])
            scol = C.vecs[:, V_PSCALE + j_ * 32 + j: V_PSCALE + j_ * 32 + j + 1]
            for tb in range(NTB):
                b = rot.next()
                pst = C.psum[b]
                P.mmgroup(pst[:, :], [(wv[:, oc % 4, kc, :], pg[:, kc, tb * 512:(tb + 1) * 512]) for kc in range(8)],
                          reads=[wkey] + pgk, writes=[("ps", b)])
                P.stt("dve", ys[:, tb * 512:(tb + 1) * 512], pst[:, :], scol, sg[:, tb * 512:(tb + 1) * 512],
                      ALU.mult, ALU.mult, reads=[("ps", b), (("sg", cnt % 2), tb)], writes=[("yst", cnt % 2, tb)])
            P.dma("sp", C.yS[j], ys, reads=[("yst", cnt % 2, tb) for tb in range(NTB)], writes=[("yS", j)])
            cnt += 1
    P.barrier()
    cross_attn(C, W, wsrc, rot, hT, 32, 48, li)


def phase_dn(C, li):
    P, A = C.P, C.A
    j_ = li // 2
    hT = A.view(O_HT, [16, T], BF16)
    LC = T + 4
    cbs = [A.view(O_R2 + i * LC * 4, [LC], F32) for i in range(2)]
    accs = [A.view(O_R2 + 2 * LC * 4 + i * T * 4, [T], F32) for i in range(2)]
    assert 2 * LC * 4 + 2 * T * 4 <= 33792
    r32s = [A.view(O_R1, [T], F32), A.view(O_R1 + 24576, [T], F32)]
    stb = [A.view(O_R1 + 8192 + i * 4096, [T], BF16) for i in range(2)]
    baf = A.view(O_R1 + 16384, [T], F32)
    sgb = [A.view(O_SG + i * 4096, [T], BF16) for i in range(2)]
    W = WStream(P, A, O_WB)
    rot = Rot([0, 1, 2, 3, 4, 5])

    def wsrc(kind, pair):
        if kind == "key":
            return ("dw_in", li, pair)
        n = 2 if pair * 2 + 2 <= 129 else 1
        return C.dw_in[j_, pair * 2:pair * 2 + n].rearrange("n p f -> p n f")

    for i in range(2):
        P.memset("pool", cbs[i][:, 0:2], 0.0, writes=[("cb_padl", i)])
        P.memset("pool", cbs[i][:, 2 + T:LC], 0.0, writes=[("cb_padr", i)])
    cnt = 0
    pending_tails = []
    for c in range(64):
        ci = c % 2
        cb = cbs[ci]

        def epi(tb, pst, pkey, cb=cb, ci=ci):
            P.copy("act" if tb % 2 else "dve", cb[:, 2 + tb * 512:2 + (tb + 1) * 512], pst[:, :],
                   reads=[pkey], writes=[("cb", ci, tb)])
        gemm_cols(C, W, wsrc, c, rot, epi, hT)
        while pending_tails:
            for (eng_, fn_, r_, w_, d_) in pending_tails.pop(0):
                P.add(eng_, fn_, r_, w_, d_)
        cbk = [("cb", ci, tb) for tb in range(NTB)] + [("cb_padl", ci), ("cb_padr", ci)]
        wc = C.vecs[:, V_CONV + j_ * 320 + c * 5: V_CONV + j_ * 320 + c * 5 + 5]
        acc = accs[ci]
        r32 = r32s[ci]
        ak = ("acc", ci)
        P.add("act", lambda e, cb=cb, wc=wc, acc=acc: e.mul(out=acc, in_=cb[:, 0:T], mul=wc[:, 0:1]),
              reads=cbk, writes=[ak])
        for k in range(1, 5):
            P.stt("dve", acc, cb[:, k:k + T], wc[:, k:k + 1], acc, ALU.mult, ALU.add, reads=cbk + [ak], writes=[ak])
        st_ = stb[cnt % 2]
        stk = ("stb", cnt % 2)
        cnt += 1
        if c >= 32:
            P.act(st_, acc, AF.Silu, reads=[ak], writes=[stk])
            P.dma("sp", C.vS[c - 32], st_, reads=[stk], writes=[("vS", c - 32)])
        else:
            r32k = [("r32", ci, tb) for tb in range(NTB)]
            P.act(acc, acc, AF.Silu, reads=[ak], writes=[ak])
            P.act(r32, acc, AF.Square, reads=[ak], writes=r32k)
            P.rec = []
            for tb in range(NTB):
                b = rot.next()
                pst = C.psum[b]
                P.add("pe", lambda e, pst=pst, tb=tb, r32=r32: e.matmul(pst[:, :], C.ones1,
                                                                       r32[:, tb * 512:(tb + 1) * 512],
                                                                       start=True, stop=True),
                      reads=[("r32", ci, tb)], writes=[("ps", b)])
                P.act(r32[:, tb * 512:(tb + 1) * 512], pst[:, :], AF.Ln, reads=[("ps", b)], writes=[("r32", ci, tb)],
                      bias=C.vecs[:, V_EPS:V_EPS + 1], scale=1.0)
            P.act(r32, r32, AF.Exp, reads=r32k, writes=r32k, scale=-0.5)
            qs = float(128 ** -0.5) if c < 16 else 1.0
            P.stt("dve", st_, acc, qs, r32, ALU.mult, ALU.mult, reads=[ak] + r32k, writes=[stk])
            dst = C.qS[c] if c < 16 else C.kS[c - 16]
            P.dma("sp", dst, st_, reads=[stk], writes=[("qk", c)])
            tail, P.rec = P.rec, None
            pending_tails.append(tail)
    for j in range(32):
        if j == 1:
            while pending_tails:
                for (eng_, fn_, r_, w_, d_) in pending_tails.pop(0):
                    P.add(eng_, fn_, r_, w_, d_)
        sg = sgb[j % 2]
        gate_chunk(C, W, wsrc, 80 + j, rot, hT, sg, ("sg", j % 2))
        P.dma("sp", C.sgS[j], sg, reads=[(("sg", j % 2), tb) for tb in range(NTB)], writes=[("sgS", j)])
    def epi_ba(tb, pst, pkey):
        P.copy("act", baf[:, tb * 512:(tb + 1) * 512], pst[:, :], reads=[pkey], writes=[("baf", tb)])
    gemm_cols(C, W, wsrc, 128, rot, epi_ba, hT)
    P.dma("sp", C.baS, baf, reads=[("baf", tb) for tb in range(NTB)], writes=["baS"])
    P.barrier()
    cross_attn(C, W, wsrc, rot, hT, 64, 80, li)
    P.barrier()
    dn_core(C, li)


def phase_out(C, li, xsrc):
    P, A = C.P, C.A
    yblk = A.view(0, [48, 512], BF16)
    xt = [A.view(49152 + i * 2048, [512], F32) for i in range(2)]
    xo = [A.view(49152 + 4096 + i * 2048, [512], F32) for i in range(2)]
    WB = 65536
    W = WStream(P, A, WB, nslots=2, slot_bytes=12288)
    rot = Rot([0, 1, 2, 3])
    xs = xsrc.rearrange("(oc p) t -> oc p t", p=128)
    xd = C.xS.rearrange("(oc p) t -> oc p t", p=128)
    cnt = 0
    for tb in range(NTB):
        P.dma("sp", yblk, C.yS[:, :, tb * 512:(tb + 1) * 512].rearrange("c p t -> p c t"), reads=[], writes=["yblk"])
        for oc in range(16):
            i2 = cnt % 2
            cnt += 1
            wkey, wv = W.get(("w_out", li, oc, tb), C.w_out[li, oc], [48, 128])
            P.dma("sp", xt[i2], xs[oc][:, tb * 512:(tb + 1) * 512], reads=[("xS", oc, tb)], writes=[("xt", i2)])
            b = rot.next()
            pst = C.psum[b]
            P.mmgroup(pst[:, :], [(wv[:, kc, :], yblk[:, kc, :]) for kc in range(48)],
                      reads=[wkey, "yblk"], writes=[("ps", b)])
            P.tt("dve", xo[i2], pst[:, :], xt[i2], ALU.add, reads=[("ps", b), ("xt", i2)], writes=[("xo", i2)])
            P.dma("sp", xd[oc][:, tb * 512:(tb + 1) * 512], xo[i2], reads=[("xo", i2)], writes=[("xS", oc, tb)])


def phase_final(C, xsrc):
    P, A = C.P, C.A
    xt = A.view(O_R1, [16, 512], F32)
    ot = A.view(0, [16, 512], F32)
    sq = [A.view(O_R2 + i * 2048, [512], F32) for i in range(2)]
    wcol = C.vecs[:, V_FIN:V_FIN + 16]
    srcv = xsrc.rearrange("(kc p) t -> p kc t", p=128)
    dstv = C.outT.rearrange("(kc p) t -> p kc t", p=128)
    ps = C.psum[6]
    for tb in range(NTB):
        P.dma("sp", xt, srcv[:, :, tb * 512:(tb + 1) * 512], reads=[], writes=["xt"])
        for kc in range(16):
            P.act(sq[kc % 2], xt[:, kc, :], AF.Square, reads=["xt"], writes=[("sq", kc % 2)])
            P.add("pe", (lambda kc=kc: lambda e: e.matmul(ps[:, :], C.onesm, sq[kc % 2],
                                                           start=(kc == 0), stop=(kc == 15)))(),
                  reads=[("sq", kc % 2)], writes=[("ps", 6)] if kc in (0, 15) else [])
        rsqrt_eps(C, C.rstd, ps[:, :], [("ps", 6)], "rstd")
        for kc in range(16):
            P.stt("dve", ot[:, kc, :], xt[:, kc, :], wcol[:, kc:kc + 1], C.rstd,
                  ALU.mult, ALU.mult, reads=["xt", "rstd"], writes=[("ot", kc)])
        P.dma("sp", dstv[:, :, tb * 512:(tb + 1) * 512], ot, reads=[("ot", kc) for kc in range(16)],
              writes=[("out", tb)])


def _blk(w, kc, nb):
    K, N = w.shape
    return np.ascontiguousarray(w.reshape(K // 128, 128, N // nb, nb).transpose(2, 1, 0, 3)).reshape(
        N // nb, 128, (K // 128) * nb)


def _col(v):
    return np.ascontiguousarray(v.reshape(-1, 128).T)


def prep_shared(inp):
    import ml_dtypes
    f = np.float32
    vecs = np.zeros((128, NVEC), f)
    for i in range(DEPTH):
        vecs[:, V_NORM + i * 16:V_NORM + (i + 1) * 16] = _col(inp["norm_w"][i])
        vecs[:, V_MNORM + i * 16:V_MNORM + (i + 1) * 16] = _col(inp["mem_norm_w"][i])
    vecs[:, V_FIN:V_FIN + 16] = _col(inp["final_norm_w"])
    for j in range(2):
        vecs[:, V_PSCALE + j * 32:V_PSCALE + (j + 1) * 32] = _col(inp["pool_scale"][j])
        cw = inp["dn_conv_w"][j]
        vecs[:, V_CONV + j * 320:V_CONV + (j + 1) * 320] = np.ascontiguousarray(
            cw.reshape(5, 64, 128).transpose(2, 1, 0)).reshape(128, 320)
        vecs[:, V_DNNORM + j] = inp["dn_norm_w"][j]
    vecs[:, V_EPS] = EPS
    vecs[:, V_ONE] = 1.0
    cmat = np.zeros((128, 3 * 128), f)
    cmat[:, 0:128] = np.eye(128, dtype=f)
    cmat[:, 128:256] = 1.0 / D
    cmat[:, 256:384] = 1.0
    cbf = np.eye(128, dtype=f).astype(ml_dtypes.bfloat16)
    redge = np.zeros((128, 64), f)
    for g, w in enumerate(POOL_WINDOWS):
        half = w // 2
        for t in range(half):
            redge[:, g * 16 + t] = 1.0 / (t + half)
        nr = half - 1
        for i in range(nr):
            t = T - nr + i
            redge[:, g * 16 + 8 + i] = 1.0 / (T - t + half)
    for j in range(2):
        dtb = inp["dn_dt_bias"][j]
        alog = inp["dn_a_log"][j]
        vecs[0:32, V_DTB + j] = dtb[1]
        vecs[32:64, V_DTB + j] = dtb[0]
        vecs[0:32, V_ALOG + j] = alog[1]
        vecs[32:64, V_ALOG + j] = alog[0]
    cmask = np.ones((128, T), f)
    cmask[:, ::64] = 0.0
    ii = np.arange(128)[:, None]
    jj = np.arange(128)[None, :]
    same = (ii // 64) == (jj // 64)
    masks = np.concatenate([(same & (jj < ii)), (same & (jj > ii)), (same & (jj <= ii)), (same & (jj >= ii))],
                           axis=1).astype(f)
    sh = {"vecs": vecs, "cmat": cmat, "cbf": cbf, "redge": redge, "cmask": cmask, "masks": masks}
    wkv = inp["w_kv_mem"]
    sh["w_kvk"] = np.stack([_blk(wkv[i][:, :2048], 16, 128) for i in range(DEPTH)])
    sh["w_kvv"] = np.stack([_blk(wkv[i][:, 2048:], 16, 256) for i in range(DEPTH)])
    sh["w_out"] = np.stack([_blk(inp["w_out"][i], 48, 128) for i in range(DEPTH)])
    sh["pw_in"] = np.stack([_blk(inp["pool_w_in"][j], 16, 128) for j in range(2)])
    sh["pw_g"] = np.stack([np.concatenate([_blk(inp["pool_w_group"][j][g], 8, 128) for g in range(4)])
                           for j in range(2)])
    dws = []
    for j in range(2):
        w = inp["dn_w_in"][j]
        ba = w[:, 16384:]
        w2 = np.concatenate([w[:, :16384], ba[:, 96:128], ba[:, 64:96], ba[:, 0:32], ba[:, 32:64]], axis=1)
        dws.append(_blk(w2, 16, 128))
    sh["dw_in"] = np.stack(dws)
    return sh


_NC_CACHE = {}


def kernel(**inp):
    inp = {k: np.asarray(v) for k, v in inp.items()}
    sh = prep_shared(inp)
    if "full" not in _NC_CACHE:
        _NC_CACHE["full"] = build_program()
    nc = _NC_CACHE["full"]
    in_maps = []
    for b in range(8):
        m = dict(sh)
        m["xT"] = np.ascontiguousarray(inp["x"][b].T)
        m["memT"] = np.ascontiguousarray(inp["mem"][b].T)
        in_maps.append(m)
    res = run_bass_kernel_spmd(nc, in_maps, core_ids=list(range(8)))
    out = np.stack([np.ascontiguousarray(r["outT"].T) for r in res.results])
    return out.astype(np.float32)


def dn_core(C, li):
    P, A = C.P, C.A
    j_ = li // 2
    off = [0]

    def alloc(shape, dt):
        n = int(np.prod(shape)) * (4 if dt == F32 else 2)
        n = (n + 31) // 32 * 32
        o = off[0]
        off[0] += n
        return A.view(o, shape, dt)

    BG = alloc([T], F32)
    TM1 = alloc([16, 128], F32)
    TMB = alloc([16, 64], F32)
    TMK = alloc([16, 64], F32)
    GTOT = alloc([32], F32)
    base = off[0]
    BAf = alloc([T], F32)
    G0 = alloc([T], F32)
    PF = alloc([T], F32)
    B2 = alloc([T], F32)
    cmask = alloc([T], F32)
    TMGD = alloc([16, 64], F32)
    assert off[0] <= O_CONST
    dtb = C.vecs[0:64, V_DTB + j_:V_DTB + j_ + 1]
    alog = C.vecs[0:64, V_ALOG + j_:V_ALOG + j_ + 1]
    one = C.vecs[0:64, V_ONE:V_ONE + 1]
    sm = C.sm
    P.dma("sp", BAf, C.baS, reads=[], writes=["BAf"])
    P.dma("sp", cmask, C.cmask_in, reads=[], writes=["cmask"])
    P.act(BG[64:128, :], BAf[64:128, :], AF.Sigmoid, reads=["BAf"], writes=["BGb"])
    P.act(sm[0:64, 12:13], alog, AF.Exp, reads=[], writes=["nA"])
    P.ts("dve", sm[0:64, 13:14], sm[0:64, 12:13], -1.0, None, ALU.mult, None, reads=["nA"], writes=["nA2"])
    P.ts("dve", B2[0:64, :], BAf[0:64, :], dtb, None, ALU.add, None, reads=["BAf"], writes=["B2"])
    P.ts("dve", PF[0:64, :], B2[0:64, :], -1.0, None, ALU.mult, None, reads=["B2"], writes=["PF"])
    P.tt("dve", PF[0:64, :], PF[0:64, :], B2[0:64, :], ALU.max, reads=["PF", "B2"], writes=["PF"])
    P.act(PF[0:64, :], PF[0:64, :], AF.Exp, reads=["PF"], writes=["PF"], scale=-1.0)
    P.act(PF[0:64, :], PF[0:64, :], AF.Ln, reads=["PF"], writes=["PF"], bias=one, scale=1.0)
    P.ts("dve", G0[0:64, :], B2[0:64, :], 0.0, None, ALU.max, None, reads=["B2"], writes=["G0"])
    P.tt("dve", G0[0:64, :], G0[0:64, :], PF[0:64, :], ALU.add, reads=["G0", "PF"], writes=["G0"])
    P.ts("dve", G0[0:64, :], G0[0:64, :], sm[0:64, 13:14], None, ALU.mult, None, reads=["G0", "nA2"], writes=["G0"])
    P.add("dve", lambda e: e.tensor_tensor_scan(out=PF[0:64, :], data0=cmask[0:64, :], data1=G0[0:64, :],
                                                 initial=0.0, op0=ALU.mult, op1=ALU.add),
          reads=["G0", "cmask", "PF"], writes=["PF"])
    PF3 = PF.rearrange("p (c k) -> p c k", k=64)
    BG3 = BG.rearrange("p (c k) -> p c k", k=64)
    B23 = B2.rearrange("p (c k) -> p c k", k=64)
    G03 = G0.rearrange("p (c k) -> p c k", k=64)
    P.memset("dve", GTOT, 0.0, writes=["GTOT"])
    P.copy("dve", GTOT[0:64, :], PF3[0:64, :, 63], reads=["PF", "GTOT"], writes=["GTOT"])
    P.copy("dve", BG[32:64, :], PF[32:64, :], reads=["PF"], writes=["BGf"])

    def gbc(r0, r1):
        return GTOT[r0:r1, :, None].broadcast_to([r1 - r0, 32, 64])
    P.tt("dve", B23[0:32], gbc(0, 32), PF3[0:32], ALU.subtract, reads=["GTOT", "PF"], writes=["B2"])
    P.tt("dve", BG3[0:32], B23[0:32], G03[0:32], ALU.add, reads=["B2", "G0"], writes=["BGr"])
    P.tt("dve", B23[0:64], gbc(0, 64), BG3[0:64], ALU.subtract, reads=["GTOT", "BGr", "BGf", "B2"], writes=["B2"])
    bgk = ["BGb", "BGf", "BGr"]
    for q4 in range(4):
        pst = C.psum[q4 % 2]
        for k in range(4):
            tau = q4 * 4 + k
            P.transpose(pst[:, k * 128:(k + 1) * 128], BG[:, tau * 128:(tau + 1) * 128], C.identf,
                        reads=bgk, writes=[("ps", q4 % 2)])
        P.copy("dve", TM1[:, q4 * 4:(q4 + 1) * 4, :], pst[:, :].rearrange("p (a b) -> p a b", a=4),
               reads=[("ps", q4 % 2)], writes=[("TM1", q4)])
    for q8 in range(2):
        pst = C.psum[2 + q8]
        for k in range(8):
            tau = q8 * 8 + k
            P.transpose(pst[:, k * 64:(k + 1) * 64], B2[0:64, tau * 128:(tau + 1) * 128], C.identf[0:64, 0:64],
                        reads=["B2"], writes=[("ps", 2 + q8)])
        P.copy("dve", TMGD[:, q8 * 8:(q8 + 1) * 8, :], pst[:, :].rearrange("p (a b) -> p a b", a=8),
               reads=[("ps", 2 + q8)], writes=[("TMGD", q8)])
    tm1k = [("TM1", q) for q in range(4)]
    P.act(TMK, TMGD, AF.Exp, reads=[("TMGD", 0), ("TMGD", 1)], writes=["TMK"])
    P.act(TMB, TM1[:, :, 0:64], AF.Exp, reads=tm1k, writes=["TMB"])
    P.tt("dve", TMB[:, :, 0:32], TMB[:, :, 0:32], TM1[:, :, 96:128], ALU.mult, reads=["TMB"] + tm1k, writes=["TMB"])
    P.tt("dve", TMB[:, :, 32:64], TMB[:, :, 32:64], TM1[:, :, 64:96], ALU.mult, reads=["TMB"] + tm1k, writes=["TMB"])
    P.barrier()

    off[0] = base
    qT = alloc([T], BF16)
    kT = alloc([T], BF16)
    Ktm = alloc([16, 128], BF16)
    vT = alloc([T], BF16)
    Vtm = alloc([16, 128], BF16)
    nGMs2 = [[alloc([4, 128], F32) for _ in range(2)] for _ in range(2)]
    KQMt2 = [alloc([4, 128], F32) for _ in range(2)]
    SCR = []
    for d in range(2):
        scr = Ctx()
        scr.t1 = alloc([4, 128], F32)
        scr.t2 = alloc([4, 128], F32)
        scr.tmp = alloc([4, 128], F32)
        scr.egr = alloc([512], F32)
        scr.Qb = [alloc([4, 128], BF16) for _ in range(2)]
        scr.Pb = [alloc([4, 128], BF16) for _ in range(2)]
        scr.Rb = [alloc([4, 128], BF16) for _ in range(2)]
        scr.bV = alloc([4, 128], BF16)
        scr.Kp = alloc([4, 128], BF16)
        SCR.append(scr)
    U = [alloc([16, 128], F32) for _ in range(2)]
    WT = [alloc([T], BF16) for _ in range(2)]
    QgT = [alloc([T], BF16) for _ in range(2)]
    Kd = [alloc([16, 128], BF16) for _ in range(2)]
    AT = [alloc([16, 128], BF16) for _ in range(2)]
    S = [alloc([128], F32) for _ in range(2)]
    Sbf = [alloc([128], BF16) for _ in range(2)]
    egt = [[alloc([32], F32) for _ in range(2)] for _ in range(2)]
    vnew = [alloc([128], BF16) for _ in range(2)]
    sel = [alloc([128], F32) for _ in range(4)]
    nsel = [alloc([128], F32) for _ in range(2)]
    oT = [alloc([T], F32) for _ in range(2)]
    sg = alloc([T], BF16)
    yst = alloc([T], BF16)
    osq = [alloc([512], F32) for _ in range(2)]
    rs = [alloc([512], F32) for _ in range(2)]
    assert off[0] <= O_CONST, off[0]
    Ms = [C.masks[:, 0, :], C.masks[:, 1, :]]
    Mt = [C.masks[:, 2, :], C.masks[:, 3, :]]

    def b4(ap):
        return ap[:, None, :].broadcast_to([128, 4, 128])

    def p4(pst):
        return pst[:, :].rearrange("p (a b) -> p a b", a=4)
    nwcol = C.vecs[:, V_DNNORM + j_:V_DNNORM + j_ + 1]
    rot = Rot([3, 4, 5, 6, 7])
    tm1k = []

    def head_setup(hv):
        h = hv // 2
        if hv % 2 == 0:
            P.dma("sp", qT, C.qS[h], reads=[], writes=["qT"])
            P.dma("sp", kT, C.kS[h], reads=[], writes=["kT"])
            yield
            for q8 in range(2):
                b = rot.next()
                ptb = C.psum[b][:, :].bitcast(BF16)
                for k in range(8):
                    tau = q8 * 8 + k
                    P.transpose(ptb[:, k * 128:(k + 1) * 128], kT[:, tau * 128:(tau + 1) * 128], C.identb,
                                reads=["kT"], writes=[("ps", b)])
                P.copy("act", Ktm[:, q8 * 8:(q8 + 1) * 8, :], ptb.rearrange("p (a b) -> p a b", a=8),
                       reads=[("ps", b)], writes=[("Ktm", q8)])
                yield
        P.dma("sp", vT, C.vS[hv], reads=[], writes=["vT"])
        yield
        for q8 in range(2):
            b = rot.next()
            ptb = C.psum[b][:, :].bitcast(BF16)
            for k in range(8):
                tau = q8 * 8 + k
                P.transpose(ptb[:, k * 128:(k + 1) * 128], vT[:, tau * 128:(tau + 1) * 128], C.identb,
                            reads=["vT"], writes=[("ps", b)])
            P.copy("act", Vtm[:, q8 * 8:(q8 + 1) * 8, :], ptb.rearrange("p (a b) -> p a b", a=8),
                   reads=[("ps", b)], writes=[("Vtm", q8)])
            yield
        rows = [32 + hv, hv, 64 + hv, 96 + hv]
        for i, r in enumerate(rows):
            P.ts(DEBUG.get("pe1", "dve"), sel[i], C.ones1, C.identf[:, r:r + 1], None, ALU.mult, None, reads=[], writes=[("sel", i)])
        for d in range(2):
            P.ts(DEBUG.get("pe1", "dve"), nsel[d], sel[d], -1.0, None, ALU.mult, None, reads=[("sel", d)], writes=[("nsel", d)])
        yield
        for d in range(2):
            b = rot.next()
            pst = C.psum[b]
            P.add("pe", lambda e, pst=pst, d=d: e.matmul(pst[:, 0:32], sel[d], GTOT, start=True, stop=True),
                  reads=[("sel", d)], writes=[("ps", b)])
            P.act(egt[hv % 2][d], pst[:, 0:32], AF.Exp, reads=[("ps", b)], writes=[("egt", hv % 2, d)])
            yield

    urot = [Rot([3, 4, 7]), Rot([5, 6])]

    def tg_shared(hv, tg, st):
        nGMs = nGMs2[st]
        rot = urot[st]
        b0, b1 = rot.next(), rot.next()
        psG, psKQ = C.psum[b0], C.psum[b1]

        def fg(e):
            ins = None
            for k in range(4):
                sl = slice((tg * 4 + k) * 128, (tg * 4 + k + 1) * 128)
                ins = e.matmul(psG[:, k * 128:(k + 1) * 128], kT[:, sl], kT[:, sl], start=True, stop=True)
            return ins

        def fkq(e):
            ins = None
            for k in range(4):
                sl = slice((tg * 4 + k) * 128, (tg * 4 + k + 1) * 128)
                ins = e.matmul(psKQ[:, k * 128:(k + 1) * 128], kT[:, sl], qT[:, sl], start=True, stop=True)
            return ins
        P.add("pe", fg, reads=["kT"], writes=[("ps", b0)])
        P.add("pe", fkq, reads=["kT", "qT"], writes=[("ps", b1)])
        yield
        for d in range(2):
            P.stt("dve", nGMs[d], p4(psG), -1.0, b4(Ms[d]), ALU.mult, ALU.mult, reads=[("ps", b0)],
                  writes=[("nGMs", st, d)])
            yield
        P.tt("dve", KQMt2[st], p4(psKQ), b4(Mt[1 - st]), ALU.mult, reads=[("ps", b1)], writes=[("KQMt", st)])
        yield

    def pre_unit(hv, tg, d):
        scr = SCR[d]
        nGMs = nGMs2[d]
        KQMt_d = KQMt2[d]
        t1, t2, tmp, egr, Qb, Pb, Rb, bV, Kp = scr.t1, scr.t2, scr.tmp, scr.egr, scr.Qb, scr.Pb, scr.Rb, scr.bV, scr.Kp
        sk = lambda n, *a: (n, d) + a
        tsl = slice(tg * 512, (tg + 1) * 512)
        cgc = [32 + hv, hv][d]
        cbe = [64 + hv, 96 + hv][d]
        rot = urot[d]
        bD = rot.next()
        psD = C.psum[bD]

        def fdiff(e):
            ins = None
            for k in range(4):
                sl = slice((tg * 4 + k) * 128, (tg * 4 + k + 1) * 128)
                e.matmul(psD[:, k * 128:(k + 1) * 128], sel[d], BG[:, sl], start=True, stop=False)
                ins = e.matmul(psD[:, k * 128:(k + 1) * 128], BG[:, sl], nsel[d], start=False, stop=True)
            return ins
        P.add("pe", fdiff, reads=[("sel", d), ("nsel", d)], writes=[("ps", bD)])
        yield
        P.ts("dve", t1, p4(psD), 0.0, None, ALU.max, None, reads=[("ps", bD)], writes=[sk("t1")])
        P.ts("dve", t2, p4(psD), 0.0, None, ALU.min, None, reads=[("ps", bD)], writes=[sk("t2")])
        bRg = rot.next()
        psRg = C.psum[bRg]
        P.add("pe", lambda e: e.matmul(psRg[:, :], sel[d], BG[:, tsl], start=True, stop=True),
              reads=[("sel", d)], writes=[("ps", bRg)])
        yield
        P.act(t1, t1, AF.Exp, reads=[sk("t1")], writes=[sk("t1")], scale=-1.0)
        P.act(t2, t2, AF.Exp, reads=[sk("t2")], writes=[sk("t2")])
        P.act(egr, psRg[:, :], AF.Exp, reads=[("ps", bRg)], writes=[sk("egr")])
        bRb = rot.next()
        psRb = C.psum[bRb]
        P.add("pe", lambda e: e.matmul(psRb[:, :], sel[2 + d], BG[:, tsl], start=True, stop=True),
              reads=[("sel", 2 + d)], writes=[("ps", bRb)])
        yield
        for k in range(4):
            tau = tg * 4 + k
            bei = TM1[:, tau, cbe:cbe + 1]
            P.stt("dve", Qb[0][:, k, :], t1[:, k, :], bei, nGMs[d][:, k, :], ALU.mult, ALU.mult,
                  reads=[sk("t1"), ("nGMs", d, d)], writes=[sk("Q", 0, k)])
        yield
        P.tt("dve", tmp, t2, p4(psRb), ALU.mult, reads=[sk("t2"), ("ps", bRb)], writes=[sk("tmp")])
        P.tt("dve", Pb[0], tmp, nGMs[1 - d], ALU.mult, reads=[sk("tmp"), ("nGMs", d, 1 - d)], writes=[sk("P", 0)])
        yield
        P.tt("dve", Rb[0], Pb[0], b4(C.identf), ALU.add, reads=[sk("P", 0)], writes=[sk("R", 0)])
        P.tt("dve", AT[d][:, tg * 4:(tg + 1) * 4, :], t2, KQMt_d, ALU.mult, reads=[sk("t2"), ("KQMt", d)],
             writes=[("AT", d, tg)])
        yield
        P.tt("dve", QgT[d][:, tsl], qT[:, tsl], egr, ALU.mult, reads=["qT", sk("egr")], writes=[("QgT", d, tg)])
        for k in range(4):
            tau = tg * 4 + k
            bei = TM1[:, tau, cbe:cbe + 1]
            if DEBUG.get("pe2", "dve") == "pool":
                bc = lambda ap: ap.broadcast_to([128, 128])
                P.tt("pool", bV[:, k, :], Vtm[:, tau, :], bc(bei), ALU.mult, reads=[("Vtm", tau // 8)],
                     writes=[sk("bV", k)])
                P.tt("pool", Kp[:, k, :], Ktm[:, tau, :], bc(TMB[:, tau, cgc:cgc + 1]), ALU.mult,
                     reads=[("Ktm", tau // 8)], writes=[sk("Kp", k)])
                P.tt("pool", Kd[d][:, tau, :], Ktm[:, tau, :], bc(TMK[:, tau, cgc:cgc + 1]), ALU.mult,
                     reads=[("Ktm", tau // 8)], writes=[("Kd", d, tau)])
            else:
                if DEBUG.get("nomul"):
                    P.ts("dve", bV[:, k, :], Vtm[:, tau, :], bei, None, ALU.mult, None, reads=[("Vtm", tau // 8)],
                         writes=[sk("bV", k)])
                    P.ts("dve", Kp[:, k, :], Ktm[:, tau, :], TMB[:, tau, cgc:cgc + 1], None, ALU.mult, None,
                         reads=[("Ktm", tau // 8)], writes=[sk("Kp", k)])
                    P.ts("dve", Kd[d][:, tau, :], Ktm[:, tau, :], TMK[:, tau, cgc:cgc + 1], None, ALU.mult, None,
                         reads=[("Ktm", tau // 8)], writes=[("Kd", d, tau)])
                    continue
                P.add("act", lambda e, k=k, tau=tau, bei=bei: e.mul(out=bV[:, k, :], in_=Vtm[:, tau, :], mul=bei),
                      reads=[("Vtm", tau // 8)], writes=[sk("bV", k)])
                P.add("act", lambda e, k=k, tau=tau: e.mul(out=Kp[:, k, :], in_=Ktm[:, tau, :],
                                                           mul=TMB[:, tau, cgc:cgc + 1]),
                      reads=[("Ktm", tau // 8)], writes=[sk("Kp", k)])
                P.add("act", lambda e, tau=tau: e.mul(out=Kd[d][:, tau, :], in_=Ktm[:, tau, :],
                                                      mul=TMK[:, tau, cgc:cgc + 1]),
                      reads=[("Ktm", tau // 8)], writes=[("Kd", d, tau)])
        yield
        qk = [sk("Q", 0, k) for k in range(4)]
        cur = 0
        rc = 0
        for m in range(1, 6):
            nxt = 1 - cur
            bA = rot.next()
            psA = C.psum[bA]

            def fq(e, psA=psA, cur=cur):
                ins = None
                for k in range(4):
                    ins = e.matmul(psA[:, k * 128:(k + 1) * 128], Pb[cur][:, k, :], Qb[cur][:, k, :],
                                   start=True, stop=True)
                return ins
            P.add("pe", fq, reads=qk + [sk("P", cur)], writes=[("ps", bA)])
            if m < 5:
                bB = rot.next()
                psB = C.psum[bB]

                def fp(e, psB=psB, cur=cur):
                    ins = None
                    for k in range(4):
                        ins = e.matmul(psB[:, k * 128:(k + 1) * 128], Qb[cur][:, k, :], Pb[cur][:, k, :],
                                       start=True, stop=True)
                    return ins
                P.add("pe", fp, reads=qk + [sk("P", cur)], writes=[("ps", bB)])
            P.copy("act", Qb[nxt], p4(psA), reads=[("ps", bA)], writes=[sk("Q", nxt)])
            if m < 5:
                P.copy("act", Pb[nxt], p4(psB), reads=[("ps", bB)], writes=[sk("P", nxt)])
            yield
            bC = rot.next()
            psC = C.psum[bC]

            def fr(e, psC=psC, nxt=nxt, rc=rc):
                ins = None
                for k in range(4):
                    e.matmul(psC[:, k * 128:(k + 1) * 128], Qb[nxt][:, k, :], Rb[rc][:, k, :], start=True, stop=False)
                    ins = e.matmul(psC[:, k * 128:(k + 1) * 128], C.identb, Rb[rc][:, k, :], start=False, stop=True)
                return ins
            P.add("pe", fr, reads=[sk("Q", nxt), sk("R", rc)], writes=[("ps", bC)])
            P.copy("act", Rb[1 - rc], p4(psC), reads=[("ps", bC)], writes=[sk("R", 1 - rc)])
            yield
            rc = 1 - rc
            cur = nxt
            qk = [sk("Q", cur)]
        TT = Rb[rc]
        ttk = sk("R", rc)
        bU = rot.next()
        psU = C.psum[bU]

        def fu(e):
            ins = None
            for k in range(4):
                ins = e.matmul(psU[:, k * 128:(k + 1) * 128], TT[:, k, :], bV[:, k, :], start=True, stop=True)
            return ins
        P.add("pe", fu, reads=[ttk] + [sk("bV", k) for k in range(4)], writes=[("ps", bU)])
        P.copy("act", U[d][:, tg * 4:(tg + 1) * 4, :], p4(psU), reads=[("ps", bU)], writes=[("U", d, tg)])
        yield
        bW = rot.next()
        psW = C.psum[bW]

        def fw(e):
            ins = None
            for k in range(4):
                ins = e.matmul(psW[:, k * 128:(k + 1) * 128], Kp[:, k, :], TT[:, k, :], start=True, stop=True)
            return ins
        P.add("pe", fw, reads=[ttk] + [sk("Kp", k) for k in range(4)], writes=[("ps", bW)])
        P.copy("act", WT[d][:, tsl], psW[:, :], reads=[("ps", bW)], writes=[("WT", d, tg)])
        yield

    def seq_group(hv, u, dirs=(0, 1)):
        eg = egt[hv % 2]
        if u == 0:
            for d in dirs:
                P.memset("dve", S[d], 0.0, writes=[("S", d)])
                P.memset("dve", Sbf[d], 0.0, writes=[("Sbf", d)])
        for st in range(8):
            step = u * 8 + st
            for d in dirs:
                c = step if d == 0 else 31 - step
                tau, hf = c // 2, c % 2
                tg = tau // 4
                r0 = 64 * hf
                bq = 1 + d
                psq = C.psum[bq]
                pso = C.psum[0]
                P.add("pe", lambda e, d=d, tau=tau, psq=psq: e.matmul(
                    psq[:, 0:128], WT[d][:, tau * 128:(tau + 1) * 128], Sbf[d], start=True, stop=True),
                    reads=[("WT", d, tg), ("Sbf", d)], writes=[("ps", bq)])
                P.tt("dve", vnew[d][r0:r0 + 64, :], U[d][r0:r0 + 64, tau, :], psq[r0:r0 + 64, 0:128], ALU.subtract,
                     reads=[("U", d, tg), ("ps", bq)], writes=[("vnew", d)])
                cs = d * 256 + (c % 4) * 64

                def fo(e, d=d, c=c, tau=tau, r0=r0, cs=cs, psq=psq):
                    e.matmul(pso[:, cs:cs + 64], Sbf[d], QgT[d][:, c * 64:(c + 1) * 64], start=True, stop=False)
                    e.matmul(pso[:, cs:cs + 64], vnew[d][r0:r0 + 64, :], AT[d][r0:r0 + 64, tau, r0:r0 + 64],
                             start=False, stop=True)
                    return e.matmul(psq[:, 128:256], Kd[d][r0:r0 + 64, tau, :], vnew[d][r0:r0 + 64, :],
                                    start=True, stop=True)
                P.add("pe", fo, reads=[("Sbf", d), ("QgT", d, tg), ("vnew", d), ("AT", d, tg), ("Kd", d, tau)],
                      writes=[("ps", 0), ("ps", bq)])
                P.stt("dve", Sbf[d], S[d], eg[d][:, c:c + 1], psq[:, 128:256], ALU.mult, ALU.add,
                      reads=[("S", d), ("egt", hv % 2, d), ("ps", bq)], writes=[("Sbf", d)])
                P.stt("dve", S[d], S[d], eg[d][:, c:c + 1], psq[:, 128:256], ALU.mult, ALU.add,
                      reads=[("S", d), ("egt", hv % 2, d), ("ps", bq)], writes=[("S", d)])
                last = (c % 4 == 3) if d == 0 else (c % 4 == 0)
                if last:
                    g4 = c // 4
                    P.copy("act", oT[d][:, g4 * 256:(g4 + 1) * 256], pso[:, d * 256:(d + 1) * 256],
                           reads=[("ps", 0)], writes=[("oT", d, g4)])
                yield

    def finalize(hv):
        P.dma("sp", sg, C.sgS[hv], reads=[], writes=["sg"])
        for tb in range(NTB):
            tsl = slice(tb * 512, (tb + 1) * 512)
            i2 = tb % 2
            otk = [("oT", d, g4) for d in range(2) for g4 in (2 * tb, 2 * tb + 1)]
            P.tt("dve", oT[0][:, tsl], oT[0][:, tsl], oT[1][:, tsl], ALU.add, reads=otk, writes=[("o", tb)])
            P.act(osq[i2], oT[0][:, tsl], AF.Square, reads=[("o", tb)], writes=[("osq", i2)])
            b = 1 + (tb % 2)
            pst = C.psum[b]
            P.add("pe", lambda e, pst=pst, i2=i2: e.matmul(pst[:, :], C.ones1, osq[i2], start=True, stop=True),
                  reads=[("osq", i2)], writes=[("ps", b)])
            P.act(rs[i2], pst[:, :], AF.Ln, reads=[("ps", b)], writes=[("rs", i2)],
                  bias=C.vecs[:, V_EPS:V_EPS + 1], scale=1.0 / 128)
            yield
            P.act(rs[i2], rs[i2], AF.Exp, reads=[("rs", i2)], writes=[("rs", i2)], scale=-0.5)
            P.stt("dve", osq[i2], oT[0][:, tsl], nwcol, rs[i2], ALU.mult, ALU.mult,
                  reads=[("o", tb), ("rs", i2), ("osq", i2)], writes=[("osq", i2)])
            P.tt("dve", yst[:, tsl], osq[i2], sg[:, tsl], ALU.mult, reads=[("osq", i2), "sg"], writes=[("yst", tb)])
            yield
        P.dma("sp", C.yS[hv], yst, reads=[("yst", tb) for tb in range(NTB)], writes=[("yS", hv)])
        yield

    def record(gen):
        P.rec = []
        for _ in gen:
            pass
        out = P.rec
        P.rec = None
        return out

    def merge(*lists):
        lists = [l for l in lists if l]
        pos = [0] * len(lists)
        out = []
        while True:
            best, bf = -1, 2.0
            for i, l in enumerate(lists):
                if pos[i] < len(l):
                    f = pos[i] / len(l)
                    if f < bf:
                        best, bf = i, f
            if best < 0:
                break
            out.append(lists[best][pos[best]])
            pos[best] += 1
        return out

    def pre_ops(hv, u):
        a = record(tg_shared(hv, u, 0)) + record(pre_unit(hv, u, 0))
        b_ = record(tg_shared(hv, 3 - u, 1)) + record(pre_unit(hv, 3 - u, 1))
        ops = merge(a, b_)
        if u == 0:
            ops = record(head_setup(hv)) + ops
        return ops

    def seq_ops(hv, u):
        ops = merge(record(seq_group(hv, u, (0,))), record(seq_group(hv, u, (1,))))
        if u == 3:
            ops = ops + record(finalize(hv))
        return ops

    def play(ops):
        for (eng, fn, reads, writes, dma) in ops:
            P.add(eng, fn, reads, writes, dma)

    groups = [(hv, u) for hv in range(DEBUG.get('nhv', 32)) for u in range(4)]
    play(pre_ops(0, 0))
    for gi, (hv, u) in enumerate(groups):
        so = seq_ops(hv, u)
        po = pre_ops(*groups[gi + 1]) if gi + 1 < len(groups) else []
        play(merge(so, po))
```

```python
import numpy as np
from contextlib import ExitStack
import concourse.bass as bass
import concourse.mybir as mybir
from concourse.bass_utils import run_bass_kernel_spmd

F32 = mybir.dt.float32
BF16 = mybir.dt.bfloat16
AF = mybir.ActivationFunctionType
ALU = mybir.AluOpType
AX = mybir.AxisListType

D = 2048
T = 2048
NMEM = 256
DEPTH = 4
EPS = 1e-6
NTB = T // 512
KC = D // 128
POOL_WINDOWS = (2, 4, 8, 16)
DEBUG = {}


class _Op:
    __slots__ = ("eng", "fn", "deps", "is_dma", "has_dep", "sig", "idx")

    def __init__(self, eng, fn, is_dma):
        self.eng = eng
        self.fn = fn
        self.deps = set()
        self.is_dma = is_dma
        self.has_dep = False
        self.sig = 0
        self.idx = 0


class Prog:
    CE = ("pe", "dve", "act", "pool")
    QE = ("sp", "act", "pool")
    ALLE = ("pe", "dve", "act", "pool", "sp")
    NSLOT = 8

    def __init__(self, nc):
        self.nc = nc
        self.stream = {e: [] for e in self.ALLE}
        self.last_w = {}
        self.rd_c = {}
        self.rd_d = {}
        self.ndma = {q: 0 for q in self.QE}
        self.dmas_since_barrier = []

    rec = None

    def add(self, eng, fn, reads=(), writes=(), dma=False):
        if self.rec is not None:
            self.rec.append((eng, fn, tuple(reads), tuple(writes), dma))
            return None
        op = _Op(eng, fn, dma)
        deps = op.deps
        for r in reads:
            w = self.last_w.get(r)
            if w is not None:
                deps.add(w)
        for r in writes:
            w = self.last_w.get(r)
            if w is not None:
                deps.add(w)
            for o in self.rd_c.get(r, {}).values():
                deps.add(o)
            for o in self.rd_d.get(r, ()):
                deps.add(o)
        if eng == "pe":
            for d_ in [d_ for d_ in deps if d_.eng == "pe" and not d_.is_dma]:
                deps.discard(d_)
        for r in writes:
            self.last_w[r] = op
            self.rd_c[r] = {}
            self.rd_d[r] = []
        for r in reads:
            if dma:
                self.rd_d.setdefault(r, []).append(op)
            else:
                self.rd_c.setdefault(r, {})[eng] = op
        if dma:
            op.idx = self.ndma[eng]
            self.ndma[eng] += 1
            self.dmas_since_barrier.append(op)
        self.stream[eng].append(op)
        return op

    def dma(self, q, out, in_, reads, writes):
        return self.add(q, lambda e: e.dma_start(out=out, in_=in_), reads, writes, dma=True)

    def act(self, out, in_, func, reads, writes, **kw):
        return self.add("act", lambda e: e.activation(out=out, in_=in_, func=func, **kw), reads, writes)

    def tt(self, eng, out, in0, in1, op, reads, writes):
        return self.add(eng, lambda e: e.tensor_tensor(out=out, in0=in0, in1=in1, op=op), reads, writes)

    def ts(self, eng, out, in0, s1, s2, op0, op1, reads, writes):
        if op1 is None:
            return self.add(eng, lambda e: e.tensor_scalar(out=out, in0=in0, scalar1=s1, scalar2=None, op0=op0),
                            reads, writes)
        return self.add(eng, lambda e: e.tensor_scalar(out=out, in0=in0, scalar1=s1, scalar2=s2, op0=op0, op1=op1),
                        reads, writes)

    def stt(self, eng, out, in0, scalar, in1, op0, op1, reads, writes):
        return self.add(eng, lambda e: e.scalar_tensor_tensor(out=out, in0=in0, scalar=scalar, in1=in1,
                                                              op0=op0, op1=op1), reads, writes)

    def copy(self, eng, out, in_, reads, writes):
        if eng == "act":
            return self.add("act", lambda e: e.copy(out=out, in_=in_), reads, writes)
        return self.add(eng, lambda e: e.tensor_copy(out=out, in_=in_), reads, writes)

    def memset(self, eng, ap, val, writes):
        return self.add(eng, lambda e: e.memset(ap, val), (), writes)

    def mmgroup(self, out, pairs, reads, writes):
        n = len(pairs)

        def fn(e):
            ins = None
            for i, (l, r) in enumerate(pairs):
                ins = e.matmul(out, l, r, start=(i == 0), stop=(i == n - 1))
            return ins
        return self.add("pe", fn, reads, writes)

    def transpose(self, out, in_, ident, reads, writes):
        return self.add("pe", lambda e: e.transpose(out, in_, ident), reads, writes)

    def barrier(self):
        tails = [self.stream[e][-1] for e in self.CE if self.stream[e] and not self.stream[e][-1].is_dma]
        for e in self.CE:
            for o in reversed(self.stream[e]):
                if not o.is_dma:
                    tails.append(o)
                    break
        b = _Op("sp", lambda e: e.nop(), False)
        b.deps = set(tails) | set(self.dmas_since_barrier)
        self.stream["sp"].append(b)
        self.dmas_since_barrier = []
        for e in self.CE:
            o = _Op(e, lambda en: en.nop(), False)
            o.deps = {b}
            self.stream[e].append(o)
        self.last_w = {}
        self.rd_c = {}
        self.rd_d = {}
        return b

    def emit(self):
        nc = self.nc
        for ops in self.stream.values():
            for op in ops:
                for d_ in op.deps:
                    d_.has_dep = True
        for e, ops in self.stream.items():
            c = 0
            for op in ops:
                if (not op.is_dma) and op.has_dep:
                    c += 1
                    op.sig = c
        NS = self.NSLOT
        with ExitStack() as st:
            sems = {e: st.enter_context(nc.semaphore(f"s_{e}")) for e in self.ALLE}
            dsems = {q: [st.enter_context(nc.semaphore(f"d_{q}{i}")) for i in range(NS)] for q in self.QE}
            block = st.enter_context(nc.Block())

            def run(ename, eng):
                waited = {e: 0 for e in self.ALLE}
                waited_d = {}
                for op in self.stream[ename]:
                    need = {}
                    needd = {}
                    for d_ in op.deps:
                        if d_.is_dma:
                            k = (d_.eng, d_.idx % NS)
                            rnd = d_.idx // NS + 1
                            if waited_d.get(k, 0) < rnd:
                                needd[k] = max(needd.get(k, 0), rnd)
                        else:
                            if waited[d_.eng] < d_.sig:
                                need[d_.eng] = max(need.get(d_.eng, 0), d_.sig)
                    if op.is_dma and op.idx >= NS:
                        k = (ename, op.idx % NS)
                        rnd = op.idx // NS
                        if waited_d.get(k, 0) < rnd:
                            needd[k] = max(needd.get(k, 0), rnd)
                    for e2, v in need.items():
                        eng.wait_ge(sems[e2], v)
                        waited[e2] = v
                    for (q, slot), r in needd.items():
                        eng.wait_ge(dsems[q][slot], 16 * r)
                        waited_d[(q, slot)] = r
                    ins = op.fn(eng)
                    if op.is_dma:
                        ins.then_inc(dsems[ename][op.idx % NS], 16)
                    elif op.has_dep:
                        ins.then_inc(sems[ename], 1)

            @block.tensor
            def _(e):
                run("pe", e)

            @block.vector
            def _(e):
                run("dve", e)

            @block.scalar
            def _(e):
                run("act", e)

            @block.gpsimd
            def _(e):
                run("pool", e)

            @block.sync
            def _(e):
                run("sp", e)


class Arena:
    def __init__(self, t):
        self.t = t

    def view(self, off, shape, dt):
        n = int(np.prod(shape))
        esz = 4 if dt == F32 else 2
        assert off % 4 == 0 and (n * esz) % 4 == 0
        ap = self.t[:, off // 4:(off + n * esz) // 4]
        if dt != F32:
            ap = ap.bitcast(dt)
        if len(shape) == 2:
            ap = ap.rearrange("p (a b) -> p a b", a=shape[0])
        elif len(shape) == 3:
            ap = ap.rearrange("p (a b c) -> p a b c", a=shape[0], b=shape[1])
        return ap


ARENA_BYTES = 207 * 1024
O_HT = 0
O_WB = 65536
O_R1 = O_WB + 3 * 8192
O_R2 = O_R1 + 32768
O_SG = O_R2 + 33792
O_KV = O_SG + 16384
O_CONST = O_KV + 24576
C_IDB = O_CONST
C_IDF = C_IDB + 256
C_ONE = C_IDF + 512
C_VEC = C_ONE + 512
NVEC = 864
C_RSTD = C_VEC + 4 * 864
C_MRSTD = C_RSTD + 2048
C_SM = C_MRSTD + 1024
C_ONE1 = C_SM + 64
C_MASK = C_ONE1 + 512
C_END = C_MASK + 2048
assert NVEC <= 864 and C_END <= ARENA_BYTES, (NVEC, C_END)

V_NORM = 0
V_MNORM = 64
V_FIN = 128
V_PSCALE = 144
V_CONV = 208
V_DNNORM = 848
V_EPS = 850
V_ONE = 851
V_DTB = 852
V_ALOG = 854


class Ctx:
    pass


class WStream:
    def __init__(self, P, A, base, nslots=3, slot_bytes=8192):
        self.P, self.A, self.base, self.n, self.sb = P, A, base, nslots, slot_bytes
        self.i = 0
        self.cache = {}
        self.owner = [None] * nslots

    def get(self, key, src, shape):
        if key in self.cache:
            return self.cache[key]
        s = self.i % self.n
        self.i += 1
        if self.owner[s] is not None:
            del self.cache[self.owner[s]]
        self.owner[s] = key
        view = self.A.view(self.base + s * self.sb, shape, BF16)
        fshape = list(src.shape[1:])
        flat = self.A.view(self.base + s * self.sb, fshape, BF16)
        assert int(np.prod(fshape)) == int(np.prod(shape)), (fshape, shape)
        self.P.dma("pool", flat, src, reads=[], writes=[("wb", self.base, s)])
        self.cache[key] = (("wb", self.base, s), view)
        return self.cache[key]


class Rot:
    def __init__(self, items):
        self.items = items
        self.i = 0

    def next(self):
        x = self.items[self.i % len(self.items)]
        self.i += 1
        return x


def build_program(layers=(0, 1, 2, 3), final=True, dbg_out=None):
    nc = bass.Bass("TRN2", target_bir_lowering=False)
    C = Ctx()
    C.nc = nc

    def din(name, shape, dt=F32):
        return nc.dram_tensor(name, shape, dt, kind="ExternalInput").ap()

    C.xT_in = din("xT", [D, T])
    C.memT_in = din("memT", [D, NMEM])
    C.vecs_in = din("vecs", [128, NVEC])
    C.cmat_in = din("cmat", [128, 3 * 128])
    C.cbf_in = din("cbf", [128, 128], BF16)
    C.redge_in = din("redge", [128, 64])
    C.cmask_in = din("cmask", [128, T])
    C.masks_in = din("masks", [128, 4 * 128])
    C.w_kvk = din("w_kvk", [DEPTH, 16, 128, 16 * 128])
    C.w_kvv = din("w_kvv", [DEPTH, 8, 128, 16 * 256])
    C.w_out = din("w_out", [DEPTH, 16, 128, 48 * 128])
    C.pw_in = din("pw_in", [2, 96, 128, 16 * 128])
    C.pw_g = din("pw_g", [2, 32, 128, 8 * 128])
    C.dw_in = din("dw_in", [2, 129, 128, 16 * 128])
    C.outT = nc.dram_tensor("outT", [D, T], F32, kind="ExternalOutput").ap()
    C.xS = nc.dram_tensor("xS", [D, T], F32).ap()
    skind = "ExternalOutput" if DEBUG.get("dump") else "Internal"
    C.yS = nc.dram_tensor("yS", [48, 128, T], BF16, kind=skind).ap()
    C.qS = nc.dram_tensor("qS", [16, 128, T], BF16, kind=skind).ap()
    C.kS = nc.dram_tensor("kS", [16, 128, T], BF16, kind=skind).ap()
    C.vS = nc.dram_tensor("vS", [32, 128, T], BF16, kind=skind).ap()
    C.sgS = nc.dram_tensor("sgS", [32, 128, T], BF16, kind=skind).ap()
    C.baS = nc.dram_tensor("baS", [128, T], F32, kind=skind).ap()

    with ExitStack() as st:
        arena_t = st.enter_context(nc.sbuf_tensor("arena", [128, ARENA_BYTES // 4], F32))
        C.A = A = Arena(arena_t)
        C.psum = [st.enter_context(nc.psum_tensor(f"ps{i}", [128, 512], F32)) for i in range(8)]
        C.P = P = Prog(nc)

        C.identb = A.view(C_IDB, [128], BF16)
        C.identf = A.view(C_IDF, [128], F32)
        C.onesm = A.view(C_ONE, [128], F32)
        C.vecs = A.view(C_VEC, [NVEC], F32)
        C.rstd = A.view(C_RSTD, [512], F32)
        C.sm = A.view(C_SM, [16], F32)
        C.redge = A.view(C_MRSTD, [64], F32)
        P.dma("sp", C.identb, C.cbf_in, [], ["c0"])
        P.dma("sp", C.identf, C.cmat_in[:, 0:128], [], ["c1"])
        P.dma("sp", C.onesm, C.cmat_in[:, 128:256], [], ["c2"])
        P.dma("sp", C.vecs, C.vecs_in, [], ["c3"])
        P.dma("sp", C.redge, C.redge_in, [], ["c4"])
        C.ones1 = A.view(C_ONE1, [128], F32)
        C.masks = A.view(C_MASK, [4, 128], F32)
        P.dma("sp", C.ones1, C.cmat_in[:, 256:384], [], ["c5"])
        P.dma("sp", C.masks, C.masks_in.rearrange("p (a b) -> p a b", a=4), [], ["c6"])
        P.barrier()

        xsrc = C.xT_in
        for li in layers:
            if DEBUG.get("dn_only"):
                dn_core(C, li)
                continue
            phase_norm(C, xsrc, C.vecs[:, V_NORM + li * 16: V_NORM + li * 16 + 16])
            phase_memkv(C, li)
            P.barrier()
            if li % 2 == 0:
                phase_pool(C, li)
            else:
                phase_dn(C, li)
            P.barrier()
            phase_out(C, li, xsrc)
            P.barrier()
            xsrc = C.xS
        if final:
            phase_final(C, xsrc)
        else:
            for oc in range(16):
                P.dma("sp", C.outT[oc * 128:(oc + 1) * 128, :], xsrc[oc * 128:(oc + 1) * 128, :], reads=[], writes=[("o", oc)])
        P.barrier()
        P.emit()
    return nc


def rsqrt_eps(C, out, in_, reads, wkey):
    P = C.P
    P.act(out, in_, AF.Ln, reads=list(reads), writes=[wkey], bias=C.vecs[:, V_EPS:V_EPS + 1], scale=1.0)
    P.act(out, out, AF.Exp, reads=[wkey], writes=[wkey], scale=-0.5)


def phase_norm(C, src, wcol):
    P, A = C.P, C.A
    hT = A.view(O_HT, [16, T], BF16)
    xt = A.view(O_R1, [16, 512], F32)
    sq = [A.view(O_R2 + i * 2048, [512], F32) for i in range(2)]
    srcv = src.rearrange("(kc p) t -> p kc t", p=128)
    ps = C.psum[6]
    for tb in range(NTB):
        P.dma("sp", xt, srcv[:, :, tb * 512:(tb + 1) * 512], reads=[], writes=["xt"])
        for kc in range(16):
            P.act(sq[kc % 2], xt[:, kc, :], AF.Square, reads=["xt"], writes=[("sq", kc % 2)])
            P.add("pe", (lambda kc=kc: lambda e: e.matmul(ps[:, :], C.onesm, sq[kc % 2],
                                                           start=(kc == 0), stop=(kc == 15)))(),
                  reads=[("sq", kc % 2)], writes=[("ps", 6)] if kc in (0, 15) else [])
        rsqrt_eps(C, C.rstd, ps[:, :], [("ps", 6)], "rstd")
        for kc in range(16):
            eng = "dve"
            P.stt(eng, hT[:, kc, tb * 512:(tb + 1) * 512], xt[:, kc, :], wcol[:, kc:kc + 1], C.rstd,
                  ALU.mult, ALU.mult, reads=["xt", "rstd"], writes=[("hT", tb, kc)])


def phase_memkv(C, li):
    P, A = C.P, C.A
    memf = A.view(O_R2 + 4096, [16, NMEM], F32)
    sq = [A.view(O_R2 + 4096 + 16384 + i * 1024, [NMEM], F32) for i in range(2)]
    mrstd = A.view(O_R2 + 4096 + 16384 + 2048, [NMEM], F32)
    kT = A.view(O_KV, [16, NMEM], BF16)
    v = A.view(O_KV + 8192, [2, D], BF16)
    memn = A.view(O_KV + 16384, [16, NMEM], BF16)
    wcol = C.vecs[:, V_MNORM + li * 16: V_MNORM + li * 16 + 16]
    ps = C.psum[7]
    P.dma("sp", memf, C.memT_in.rearrange("(kc p) m -> p kc m", p=128), reads=[], writes=["memf"])
    for kc in range(16):
        P.act(sq[kc % 2], memf[:, kc, :], AF.Square, reads=["memf"], writes=[("msq", kc % 2)])
        P.add("pe", (lambda kc=kc: lambda e: e.matmul(ps[:, 0:NMEM], C.onesm, sq[kc % 2],
                                                       start=(kc == 0), stop=(kc == 15)))(),
              reads=[("msq", kc % 2)], writes=[("ps", 7)] if kc in (0, 15) else [])
    rsqrt_eps(C, mrstd, ps[:, 0:NMEM], [("ps", 7)], "mrstd")
    for kc in range(16):
        P.stt("dve", memn[:, kc, :], memf[:, kc, :], wcol[:, kc:kc + 1], mrstd, ALU.mult, ALU.mult,
              reads=["memf", "mrstd"], writes=[("memn", kc)])
    memn_keys = [("memn", kc) for kc in range(16)]
    W = WStream(P, A, O_WB)
    rot = Rot([4, 5])
    for c in range(16):
        pair = c // 2
        wkey, wv = W.get(("kvk", li, pair),
                         C.w_kvk[li, pair * 2:pair * 2 + 2].rearrange("n p f -> p n f"), [2, 16, 128])
        b = rot.next()
        pst = C.psum[b]
        P.mmgroup(pst[:, 0:NMEM], [(wv[:, c % 2, kc, :], memn[:, kc, :]) for kc in range(16)],
                  reads=[wkey] + memn_keys, writes=[("ps", b)])
        P.copy("act" if c % 2 else "dve", kT[:, c, :], pst[:, 0:NMEM], reads=[("ps", b)], writes=[("kT", c)])
    for blk in range(8):
        wkey, wv = W.get(("kvv", li, blk), C.w_kvv[li, blk], [16, 256])
        for mc in range(2):
            b = rot.next()
            pst = C.psum[b]
            P.mmgroup(pst[:, 0:256], [(memn[:, kc, mc * 128:(mc + 1) * 128], wv[:, kc, :]) for kc in range(16)],
                      reads=[wkey] + memn_keys, writes=[("ps", b)])
            P.copy("act" if mc else "dve", v[:, mc, blk * 256:(blk + 1) * 256], pst[:, 0:256],
                   reads=[("ps", b)], writes=[("v", mc, blk)])


def gemm_cols(C, W, wsrc_fn, col, rot, epilogue, hT):
    P = C.P
    pair = col // 2
    src = wsrc_fn("src", pair)
    wkey, wv = W.get(wsrc_fn("key", pair), src, [int(src.shape[1]), 16, 128])
    for tb in range(NTB):
        b = rot.next()
        pst = C.psum[b]
        P.mmgroup(pst[:, :], [(wv[:, col % 2, kc, :], hT[:, kc, tb * 512:(tb + 1) * 512]) for kc in range(16)],
                  reads=[wkey], writes=[("ps", b)])
        epilogue(tb, pst, ("ps", b))


def gate_chunk(C, W, wsrc_fn, col, rot, hT, sg, sgkey):
    P = C.P

    def epi(tb, pst, pkey):
        P.act(sg[:, tb * 512:(tb + 1) * 512], pst[:, :], AF.Silu, reads=[pkey], writes=[(sgkey, tb)])
    gemm_cols(C, W, wsrc_fn, col, rot, epi, hT)


def cross_attn(C, W, wsrc_fn, rot, hT, xq_col0, gate_col0, li):
    P, A = C.P, C.A
    kT = A.view(O_KV, [16, NMEM], BF16)
    v = A.view(O_KV + 8192, [2, D], BF16)
    xqT = A.view(O_R2, [4, T], BF16)
    pT = A.view(O_R2 + 16384, [2, T], BF16)
    p32 = [A.view(O_R2 + 24576 + i * 1024, [NMEM], F32) for i in range(2)]
    pbf = [A.view(O_R2 + 26624 + i * 512, [NMEM], BF16) for i in range(2)]
    sgb = [A.view(O_SG + i * 4096, [T], BF16) for i in range(2)]
    yst = [A.view(O_SG + 8192 + i * 4096, [T], BF16) for i in range(2)]
    scale = float(512 ** -0.5)
    sm = C.sm
    cnt = 0
    rot = Rot([0, 1, 2, 3, 4])
    for hd in range(4):
        for dc in range(4):
            def epi(tb, pst, pkey, dc=dc):
                P.copy("dve" if tb % 2 else "act", xqT[:, dc, tb * 512:(tb + 1) * 512], pst[:, :],
                       reads=[pkey], writes=[("xqT", dc, tb)])
            gemm_cols(C, W, wsrc_fn, xq_col0 + hd * 4 + dc, rot, epi, hT)
        def scores(stl):
            bs = 5 + (stl % 2)
            pst_ = C.psum[bs]
            P.mmgroup(pst_[:, 0:NMEM], [(xqT[:, dc, stl * 128:(stl + 1) * 128], kT[:, hd * 4 + dc, :])
                                        for dc in range(4)],
                      reads=[("xqT", dc, stl // 4) for dc in range(4)], writes=[("ps", bs)])
        scores(0)
        for stl in range(16):
            tb = stl // 4
            i2 = stl % 2
            bs = 5 + (stl % 2)
            pst = C.psum[bs]
            P.add("dve", lambda e, pst=pst: e.reduce_max(out=sm[:, 0:1], in_=pst[:, 0:NMEM], axis=AX.X),
                  reads=[("ps", bs)], writes=["sm0"])
            P.ts("dve", sm[:, 1:2], sm[:, 0:1], -scale, None, ALU.mult, None, reads=["sm0"], writes=["sm1"])
            P.act(p32[i2], pst[:, 0:NMEM], AF.Exp, reads=[("ps", bs), "sm1"], writes=[("p32", i2)],
                  bias=sm[:, 1:2], scale=scale)
            if stl + 1 < 16:
                scores(stl + 1)
            P.add("dve", lambda e, i2=i2: e.reduce_sum(out=sm[:, 2:3], in_=p32[i2], axis=AX.X),
                  reads=[("p32", i2)], writes=["sm2"])
            P.add("dve", lambda e: e.reciprocal(out=sm[:, 3:4], in_=sm[:, 2:3]), reads=["sm2"], writes=["sm3"])
            P.ts("dve", pbf[i2], p32[i2], sm[:, 3:4], None, ALU.mult, None, reads=[("p32", i2), "sm3"],
                 writes=[("pbf", i2)])
            pt = C.psum[7][:, 0:128].bitcast(BF16)
            for mc in range(2):
                P.transpose(pt[:, mc * 128:(mc + 1) * 128], pbf[i2][:, mc * 128:(mc + 1) * 128], C.identb,
                            reads=[("pbf", i2)], writes=[("ps", 7)])
            P.copy("act", pT[:, :, stl * 128:(stl + 1) * 128], pt.rearrange("p (a b) -> p a b", a=2),
                   reads=[("ps", 7)], writes=[("pT", stl)])
        for dc in range(4):
            j = 32 + hd * 4 + dc
            sg = sgb[cnt % 2]
            ys = yst[cnt % 2]
            gate_chunk(C, W, wsrc_fn, gate_col0 + j, rot, hT, sg, ("sg", cnt % 2))
            for tb in range(NTB):
                b = rot.next()
                pst = C.psum[b]
                P.mmgroup(pst[:, :], [(v[:, mc, hd * 512 + dc * 128: hd * 512 + (dc + 1) * 128],
                                       pT[:, mc, tb * 512:(tb + 1) * 512]) for mc in range(2)],
                          reads=[("pT", s_) for s_ in range(tb * 4, tb * 4 + 4)], writes=[("ps", b)])
                P.tt("dve", ys[:, tb * 512:(tb + 1) * 512], pst[:, :], sg[:, tb * 512:(tb + 1) * 512], ALU.mult,
                     reads=[("ps", b), (("sg", cnt % 2), tb)], writes=[("yst", cnt % 2, tb)])
            P.dma("sp", C.yS[j], ys, reads=[("yst", cnt % 2, tb) for tb in range(NTB)], writes=[("yS", j)])
            cnt += 1


def phase_pool(C, li):
    P, A = C.P, C.A
    j_ = li // 2
    hT = A.view(O_HT, [16, T], BF16)
    pg = A.view(O_R1, [8, T], BF16)
    LP = T + 32
    ubs = [A.view(O_R2 + i * LP * 4, [LP], F32) for i in range(2)]
    sA = A.view(O_R2 + 2 * LP * 4, [LP], F32)
    sB = A.view(O_R2 + 3 * LP * 4, [LP], F32)
    assert 4 * LP * 4 <= 33792
    sgb = [A.view(O_SG + i * 4096, [T], BF16) for i in range(2)]
    yst = [A.view(O_SG + 8192 + i * 4096, [T], BF16) for i in range(2)]
    W = WStream(P, A, O_WB)
    rot = Rot([0, 1, 2, 3, 4, 5])

    def wsrc(kind, pair):
        if kind == "key":
            return ("pw_in", li, pair)
        return C.pw_in[j_, pair * 2:pair * 2 + 2].rearrange("n p f -> p n f")

    for i in range(2):
        P.memset("pool", ubs[i][:, 0:16], 0.0, writes=[("ub_padl", i)])
        P.memset("pool", ubs[i][:, 16 + T:LP], 0.0, writes=[("ub_padr", i)])
    cnt = 0
    ucnt = 0
    for g in range(4):
        w = POOL_WINDOWS[g]
        half = w // 2
        for cc in range(8):
            ui = ucnt % 2
            ucnt += 1
            ub = ubs[ui]

            def epi(tb, pst, pkey, ub=ub, ui=ui):
                P.copy("act", ub[:, 16 + tb * 512:16 + (tb + 1) * 512], pst[:, :],
                       reads=[pkey], writes=[("ub", ui, tb)])
            gemm_cols(C, W, wsrc, g * 8 + cc, rot, epi, hT)
            ubk = [("ub", ui, tb) for tb in range(NTB)] + [("ub_padl", ui), ("ub_padr", ui)]
            P.tt("dve", sA[:, 0:LP - 1], ub[:, 0:LP - 1], ub[:, 1:LP], ALU.add, reads=ubk, writes=["sA"])
            cur, curk, n, sh = sA, "sA", LP - 1, 2
            oth, othk = sB, "sB"
            while sh < w:
                P.tt("dve" if sh == 2 else "pool", oth[:, 0:n - sh], cur[:, 0:n - sh], cur[:, sh:n], ALU.add,
                     reads=[curk], writes=[othk])
                cur, curk, oth, othk = oth, othk, cur, curk
                n -= sh
                sh *= 2
            off = 16 - half
            P.stt("dve", pg[:, cc, :], cur[:, off:off + T], 1.0 / w, ub[:, 16:16 + T], ALU.mult, ALU.subtract,
                  reads=[curk] + ubk, writes=[("pg", cc)])
            nl = half
            nr = half - 1
            re = C.redge[:, g * 16:(g + 1) * 16]
            P.tt("dve", C.sm[:, 4:4 + nl], cur[:, off:off + nl], re[:, 0:nl], ALU.mult, reads=[curk], writes=["edl"])
            P.tt("dve", pg[:, cc, 0:nl], C.sm[:, 4:4 + nl], ub[:, 16:16 + nl], ALU.subtract,
                 reads=["edl"] + ubk, writes=[("pg", cc)])
            if nr > 0:
                P.tt("dve", C.sm[:, 4:4 + nr], cur[:, off + T - nr:off + T], re[:, 8:8 + nr], ALU.mult,
                     reads=[curk], writes=["edl"])
                P.tt("dve", pg[:, cc, T - nr:T], C.sm[:, 4:4 + nr], ub[:, 16 + T - nr:16 + T], ALU.subtract,
                     reads=["edl"] + ubk, writes=[("pg", cc)])
        pgk = [("pg", cc) for cc in range(8)]
        for oc in range(8):
            j = g * 8 + oc
            sg = sgb[cnt % 2]
            ys = yst[cnt % 2]
            gate_chunk(C, W, wsrc, 48 + j, rot, hT, sg, ("sg", cnt % 2))
            wkey, wv = W.get(("pw_g", li, g, oc // 4),
                             C.pw_g[j_, g * 8 + (oc // 4) * 4: g * 8 + (oc // 4) * 4 + 4].rearrange("n p f -> p n f"),
                             [4, 8, 128])
            scol = C.vecs[:, V_PSCALE + j_ * 32 + j: V_PSCALE + j_ * 32 + j + 1]
            for tb in range(NTB):
                b = rot.next()
                pst = C.psum[b]
                P.mmgroup(pst[:, :], [(wv[:, oc % 4, kc, :], pg[:, kc, tb * 512:(tb + 1) * 512]) for kc in range(8)],
                          reads=[wkey] + pgk, writes=[("ps", b)])
                P.stt("dve", ys[:, tb * 512:(tb + 1) * 512], pst[:, :], scol, sg[:, tb * 512:(tb + 1) * 512],
                      ALU.mult, ALU.mult, reads=[("ps", b), (("sg", cnt % 2), tb)], writes=[("yst", cnt % 2, tb)])
            P.dma("sp", C.yS[j], ys, reads=[("yst", cnt % 2, tb) for tb in range(NTB)], writes=[("yS", j)])
            cnt += 1
    P.barrier()
    cross_attn(C, W, wsrc, rot, hT, 32, 48, li)


def phase_dn(C, li):
    P, A = C.P, C.A
    j_ = li // 2
    hT = A.view(O_HT, [16, T], BF16)
    LC = T + 4
    cbs = [A.view(O_R2 + i * LC * 4, [LC], F32) for i in range(2)]
    accs = [A.view(O_R2 + 2 * LC * 4 + i * T * 4, [T], F32) for i in range(2)]
    assert 2 * LC * 4 + 2 * T * 4 <= 33792
    r32s = [A.view(O_R1, [T], F32), A.view(O_R1 + 24576, [T], F32)]
    stb = [A.view(O_R1 + 8192 + i * 4096, [T], BF16) for i in range(2)]
    baf = A.view(O_R1 + 16384, [T], F32)
    sgb = [A.view(O_SG + i * 4096, [T], BF16) for i in range(2)]
    W = WStream(P, A, O_WB)
    rot = Rot([0, 1, 2, 3, 4, 5])

    def wsrc(kind, pair):
        if kind == "key":
            return ("dw_in", li, pair)
        n = 2 if pair * 2 + 2 <= 129 else 1
        return C.dw_in[j_, pair * 2:pair * 2 + n].rearrange("n p f -> p n f")

    for i in range(2):
        P.memset("pool", cbs[i][:, 0:2], 0.0, writes=[("cb_padl", i)])
        P.memset("pool", cbs[i][:, 2 + T:LC], 0.0, writes=[("cb_padr", i)])
    cnt = 0
    pending_tails = []
    for c in range(64):
        ci = c % 2
        cb = cbs[ci]

        def epi(tb, pst, pkey, cb=cb, ci=ci):
            P.copy("act" if tb % 2 else "dve", cb[:, 2 + tb * 512:2 + (tb + 1) * 512], pst[:, :],
                   reads=[pkey], writes=[("cb", ci, tb)])
        gemm_cols(C, W, wsrc, c, rot, epi, hT)
        while pending_tails:
            for (eng_, fn_, r_, w_, d_) in pending_tails.pop(0):
                P.add(eng_, fn_, r_, w_, d_)
        cbk = [("cb", ci, tb) for tb in range(NTB)] + [("cb_padl", ci), ("cb_padr", ci)]
        wc = C.vecs[:, V_CONV + j_ * 320 + c * 5: V_CONV + j_ * 320 + c * 5 + 5]
        acc = accs[ci]
        r32 = r32s[ci]
        ak = ("acc", ci)
        P.add("act", lambda e, cb=cb, wc=wc, acc=acc: e.mul(out=acc, in_=cb[:, 0:T], mul=wc[:, 0:1]),
              reads=cbk, writes=[ak])
        for k in range(1, 5):
            P.stt("dve", acc, cb[:, k:k + T], wc[:, k:k + 1], acc, ALU.mult, ALU.add, reads=cbk + [ak], writes=[ak])
        st_ = stb[cnt % 2]
        stk = ("stb", cnt % 2)
        cnt += 1
        if c >= 32:
            P.act(st_, acc, AF.Silu, reads=[ak], writes=[stk])
            P.dma("sp", C.vS[c - 32], st_, reads=[stk], writes=[("vS", c - 32)])
        else:
            r32k = [("r32", ci, tb) for tb in range(NTB)]
            P.act(acc, acc, AF.Silu, reads=[ak], writes=[ak])
            P.act(r32, acc, AF.Square, reads=[ak], writes=r32k)
            P.rec = []
            for tb in range(NTB):
                b = rot.next()
                pst = C.psum[b]
                P.add("pe", lambda e, pst=pst, tb=tb, r32=r32: e.matmul(pst[:, :], C.ones1,
                                                                       r32[:, tb * 512:(tb + 1) * 512],
                                                                       start=True, stop=True),
                      reads=[("r32", ci, tb)], writes=[("ps", b)])
                P.act(r32[:, tb * 512:(tb + 1) * 512], pst[:, :], AF.Ln, reads=[("ps", b)], writes=[("r32", ci, tb)],
                      bias=C.vecs[:, V_EPS:V_EPS + 1], scale=1.0)
            P.act(r32, r32, AF.Exp, reads=r32k, writes=r32k, scale=-0.5)
            qs = float(128 ** -0.5) if c < 16 else 1.0
            P.stt("dve", st_, acc, qs, r32, ALU.mult, ALU.mult, reads=[ak] + r32k, writes=[stk])
            dst = C.qS[c] if c < 16 else C.kS[c - 16]
            P.dma("sp", dst, st_, reads=[stk], writes=[("qk", c)])
            tail, P.rec = P.rec, None
            pending_tails.append(tail)
    for j in range(32):
        if j == 1:
            while pending_tails:
                for (eng_, fn_, r_, w_, d_) in pending_tails.pop(0):
                    P.add(eng_, fn_, r_, w_, d_)
        sg = sgb[j % 2]
        gate_chunk(C, W, wsrc, 80 + j, rot, hT, sg, ("sg", j % 2))
        P.dma("sp", C.sgS[j], sg, reads=[(("sg", j % 2), tb) for tb in range(NTB)], writes=[("sgS", j)])
    def epi_ba(tb, pst, pkey):
        P.copy("act", baf[:, tb * 512:(tb + 1) * 512], pst[:, :], reads=[pkey], writes=[("baf", tb)])
    gemm_cols(C, W, wsrc, 128, rot, epi_ba, hT)
    P.dma("sp", C.baS, baf, reads=[("baf", tb) for tb in range(NTB)], writes=["baS"])
    P.barrier()
    cross_attn(C, W, wsrc, rot, hT, 64, 80, li)
    P.barrier()
    dn_core(C, li)


def phase_out(C, li, xsrc):
    P, A = C.P, C.A
    TBK = 1024
    yblk = A.view(0, [48, TBK], BF16)
    XB = 98304
    xt = [A.view(XB + i * 2048, [512], F32) for i in range(2)]
    xo = [A.view(XB + 4096 + i * 2048, [512], F32) for i in range(2)]
    WB = XB + 8192
    W = WStream(P, A, WB, nslots=2, slot_bytes=12288)
    assert WB + 2 * 12288 <= O_CONST
    rot = Rot([0, 1, 2, 3])
    xs = xsrc.rearrange("(oc p) t -> oc p t", p=128)
    xd = C.xS.rearrange("(oc p) t -> oc p t", p=128)
    cnt = 0
    for hb in range(T // TBK):
        P.dma("sp", yblk, C.yS[:, :, hb * TBK:(hb + 1) * TBK].rearrange("c p t -> p c t"), reads=[], writes=["yblk"])
        for oc in range(16):
            wkey, wv = W.get(("w_out", li, oc, hb), C.w_out[li, oc], [48, 128])
            for sub in range(TBK // 512):
                tb = hb * (TBK // 512) + sub
                i2 = cnt % 2
                cnt += 1
                P.dma("sp", xt[i2], xs[oc][:, tb * 512:(tb + 1) * 512], reads=[("xS", oc, tb)], writes=[("xt", i2)])
                b = rot.next()
                pst = C.psum[b]
                P.mmgroup(pst[:, :], [(wv[:, kc, :], yblk[:, kc, sub * 512:(sub + 1) * 512]) for kc in range(48)],
                          reads=[wkey, "yblk"], writes=[("ps", b)])
                P.tt("dve", xo[i2], pst[:, :], xt[i2], ALU.add, reads=[("ps", b), ("xt", i2)], writes=[("xo", i2)])
                P.dma("sp", xd[oc][:, tb * 512:(tb + 1) * 512], xo[i2], reads=[("xo", i2)], writes=[("xS", oc, tb)])


def phase_final(C, xsrc):
    P, A = C.P, C.A
    xt = A.view(O_R1, [16, 512], F32)
    ot = A.view(0, [16, 512], F32)
    sq = [A.view(O_R2 + i * 2048, [512], F32) for i in range(2)]
    wcol = C.vecs[:, V_FIN:V_FIN + 16]
    srcv = xsrc.rearrange("(kc p) t -> p kc t", p=128)
    dstv = C.outT.rearrange("(kc p) t -> p kc t", p=128)
    ps = C.psum[6]
    for tb in range(NTB):
        P.dma("sp", xt, srcv[:, :, tb * 512:(tb + 1) * 512], reads=[], writes=["xt"])
        for kc in range(16):
            P.act(sq[kc % 2], xt[:, kc, :], AF.Square, reads=["xt"], writes=[("sq", kc % 2)])
            P.add("pe", (lambda kc=kc: lambda e: e.matmul(ps[:, :], C.onesm, sq[kc % 2],
                                                           start=(kc == 0), stop=(kc == 15)))(),
                  reads=[("sq", kc % 2)], writes=[("ps", 6)] if kc in (0, 15) else [])
        rsqrt_eps(C, C.rstd, ps[:, :], [("ps", 6)], "rstd")
        for kc in range(16):
            P.stt("dve", ot[:, kc, :], xt[:, kc, :], wcol[:, kc:kc + 1], C.rstd,
                  ALU.mult, ALU.mult, reads=["xt", "rstd"], writes=[("ot", kc)])
        P.dma("sp", dstv[:, :, tb * 512:(tb + 1) * 512], ot, reads=[("ot", kc) for kc in range(16)],
              writes=[("out", tb)])


def _blk(w, kc, nb):
    K, N = w.shape
    return np.ascontiguousarray(w.reshape(K // 128, 128, N // nb, nb).transpose(2, 1, 0, 3)).reshape(
        N // nb, 128, (K // 128) * nb)


def _col(v):
    return np.ascontiguousarray(v.reshape(-1, 128).T)


def prep_shared(inp):
    import ml_dtypes
    f = np.float32
    vecs = np.zeros((128, NVEC), f)
    for i in range(DEPTH):
        vecs[:, V_NORM + i * 16:V_NORM + (i + 1) * 16] = _col(inp["norm_w"][i])
        vecs[:, V_MNORM + i * 16:V_MNORM + (i + 1) * 16] = _col(inp["mem_norm_w"][i])
    vecs[:, V_FIN:V_FIN + 16] = _col(inp["final_norm_w"])
    for j in range(2):
        vecs[:, V_PSCALE + j * 32:V_PSCALE + (j + 1) * 32] = _col(inp["pool_scale"][j])
        cw = inp["dn_conv_w"][j]
        vecs[:, V_CONV + j * 320:V_CONV + (j + 1) * 320] = np.ascontiguousarray(
            cw.reshape(5, 64, 128).transpose(2, 1, 0)).reshape(128, 320)
        vecs[:, V_DNNORM + j] = inp["dn_norm_w"][j]
    vecs[:, V_EPS] = EPS
    vecs[:, V_ONE] = 1.0
    cmat = np.zeros((128, 3 * 128), f)
    cmat[:, 0:128] = np.eye(128, dtype=f)
    cmat[:, 128:256] = 1.0 / D
    cmat[:, 256:384] = 1.0
    cbf = np.eye(128, dtype=f).astype(ml_dtypes.bfloat16)
    redge = np.zeros((128, 64), f)
    for g, w in enumerate(POOL_WINDOWS):
        half = w // 2
        for t in range(half):
            redge[:, g * 16 + t] = 1.0 / (t + half)
        nr = half - 1
        for i in range(nr):
            t = T - nr + i
            redge[:, g * 16 + 8 + i] = 1.0 / (T - t + half)
    for j in range(2):
        dtb = inp["dn_dt_bias"][j]
        alog = inp["dn_a_log"][j]
        vecs[0:32, V_DTB + j] = dtb[1]
        vecs[32:64, V_DTB + j] = dtb[0]
        vecs[0:32, V_ALOG + j] = alog[1]
        vecs[32:64, V_ALOG + j] = alog[0]
    cmask = np.ones((128, T), f)
    cmask[:, ::64] = 0.0
    ii = np.arange(128)[:, None]
    jj = np.arange(128)[None, :]
    same = (ii // 64) == (jj // 64)
    masks = np.concatenate([(same & (jj < ii)), (same & (jj > ii)), (same & (jj <= ii)), (same & (jj >= ii))],
                           axis=1).astype(f)
    sh = {"vecs": vecs, "cmat": cmat, "cbf": cbf, "redge": redge, "cmask": cmask, "masks": masks}
    wkv = inp["w_kv_mem"]
    sh["w_kvk"] = np.stack([_blk(wkv[i][:, :2048], 16, 128) for i in range(DEPTH)])
    sh["w_kvv"] = np.stack([_blk(wkv[i][:, 2048:], 16, 256) for i in range(DEPTH)])
    sh["w_out"] = np.stack([_blk(inp["w_out"][i], 48, 128) for i in range(DEPTH)])
    sh["pw_in"] = np.stack([_blk(inp["pool_w_in"][j], 16, 128) for j in range(2)])
    sh["pw_g"] = np.stack([np.concatenate([_blk(inp["pool_w_group"][j][g], 8, 128) for g in range(4)])
                           for j in range(2)])
    dws = []
    for j in range(2):
        w = inp["dn_w_in"][j]
        ba = w[:, 16384:]
        w2 = np.concatenate([w[:, :16384], ba[:, 96:128], ba[:, 64:96], ba[:, 0:32], ba[:, 32:64]], axis=1)
        dws.append(_blk(w2, 16, 128))
    sh["dw_in"] = np.stack(dws)
    return sh


_NC_CACHE = {}


def kernel(**inp):
    inp = {k: np.asarray(v) for k, v in inp.items()}
    sh = prep_shared(inp)
    if "full" not in _NC_CACHE:
        _NC_CACHE["full"] = build_program()
    nc = _NC_CACHE["full"]
    in_maps = []
    for b in range(8):
        m = dict(sh)
        m["xT"] = np.ascontiguousarray(inp["x"][b].T)
        m["memT"] = np.ascontiguousarray(inp["mem"][b].T)
        in_maps.append(m)
    res = run_bass_kernel_spmd(nc, in_maps, core_ids=list(range(8)))
    out = np.stack([np.ascontiguousarray(r["outT"].T) for r in res.results])
    return out.astype(np.float32)


def dn_core(C, li):
    P, A = C.P, C.A
    j_ = li // 2
    off = [0]

    def alloc(shape, dt):
        n = int(np.prod(shape)) * (4 if dt == F32 else 2)
        n = (n + 31) // 32 * 32
        o = off[0]
        off[0] += n
        return A.view(o, shape, dt)

    BG = alloc([T], F32)
    TM1 = alloc([16, 128], F32)
    TMB = alloc([16, 64], F32)
    TMK = alloc([16, 64], F32)
    GTOT = alloc([32], F32)
    base = off[0]
    BAf = alloc([T], F32)
    G0 = alloc([T], F32)
    PF = alloc([T], F32)
    B2 = alloc([T], F32)
    cmask = alloc([T], F32)
    TMGD = alloc([16, 64], F32)
    assert off[0] <= O_CONST
    dtb = C.vecs[0:64, V_DTB + j_:V_DTB + j_ + 1]
    alog = C.vecs[0:64, V_ALOG + j_:V_ALOG + j_ + 1]
    one = C.vecs[0:64, V_ONE:V_ONE + 1]
    sm = C.sm
    P.dma("sp", BAf, C.baS, reads=[], writes=["BAf"])
    P.dma("sp", cmask, C.cmask_in, reads=[], writes=["cmask"])
    P.act(BG[64:128, :], BAf[64:128, :], AF.Sigmoid, reads=["BAf"], writes=["BGb"])
    P.act(sm[0:64, 12:13], alog, AF.Exp, reads=[], writes=["nA"])
    P.ts("dve", sm[0:64, 13:14], sm[0:64, 12:13], -1.0, None, ALU.mult, None, reads=["nA"], writes=["nA2"])
    P.ts("dve", B2[0:64, :], BAf[0:64, :], dtb, None, ALU.add, None, reads=["BAf"], writes=["B2"])
    P.ts("dve", PF[0:64, :], B2[0:64, :], -1.0, None, ALU.mult, None, reads=["B2"], writes=["PF"])
    P.tt("dve", PF[0:64, :], PF[0:64, :], B2[0:64, :], ALU.max, reads=["PF", "B2"], writes=["PF"])
    P.act(PF[0:64, :], PF[0:64, :], AF.Exp, reads=["PF"], writes=["PF"], scale=-1.0)
    P.act(PF[0:64, :], PF[0:64, :], AF.Ln, reads=["PF"], writes=["PF"], bias=one, scale=1.0)
    P.ts("dve", G0[0:64, :], B2[0:64, :], 0.0, None, ALU.max, None, reads=["B2"], writes=["G0"])
    P.tt("dve", G0[0:64, :], G0[0:64, :], PF[0:64, :], ALU.add, reads=["G0", "PF"], writes=["G0"])
    P.ts("dve", G0[0:64, :], G0[0:64, :], sm[0:64, 13:14], None, ALU.mult, None, reads=["G0", "nA2"], writes=["G0"])
    P.add("dve", lambda e: e.tensor_tensor_scan(out=PF[0:64, :], data0=cmask[0:64, :], data1=G0[0:64, :],
                                                 initial=0.0, op0=ALU.mult, op1=ALU.add),
          reads=["G0", "cmask", "PF"], writes=["PF"])
    PF3 = PF.rearrange("p (c k) -> p c k", k=64)
    BG3 = BG.rearrange("p (c k) -> p c k", k=64)
    B23 = B2.rearrange("p (c k) -> p c k", k=64)
    G03 = G0.rearrange("p (c k) -> p c k", k=64)
    P.memset("dve", GTOT, 0.0, writes=["GTOT"])
    P.copy("dve", GTOT[0:64, :], PF3[0:64, :, 63], reads=["PF", "GTOT"], writes=["GTOT"])
    P.copy("dve", BG[32:64, :], PF[32:64, :], reads=["PF"], writes=["BGf"])

    def gbc(r0, r1):
        return GTOT[r0:r1, :, None].broadcast_to([r1 - r0, 32, 64])
    P.tt("dve", B23[0:32], gbc(0, 32), PF3[0:32], ALU.subtract, reads=["GTOT", "PF"], writes=["B2"])
    P.tt("dve", BG3[0:32], B23[0:32], G03[0:32], ALU.add, reads=["B2", "G0"], writes=["BGr"])
    P.tt("dve", B23[0:64], gbc(0, 64), BG3[0:64], ALU.subtract, reads=["GTOT", "BGr", "BGf", "B2"], writes=["B2"])
    bgk = ["BGb", "BGf", "BGr"]
    for q4 in range(4):
        pst = C.psum[q4 % 2]
        for k in range(4):
            tau = q4 * 4 + k
            P.transpose(pst[:, k * 128:(k + 1) * 128], BG[:, tau * 128:(tau + 1) * 128], C.identf,
                        reads=bgk, writes=[("ps", q4 % 2)])
        P.copy("dve", TM1[:, q4 * 4:(q4 + 1) * 4, :], pst[:, :].rearrange("p (a b) -> p a b", a=4),
               reads=[("ps", q4 % 2)], writes=[("TM1", q4)])
    for q8 in range(2):
        pst = C.psum[2 + q8]
        for k in range(8):
            tau = q8 * 8 + k
            P.transpose(pst[:, k * 64:(k + 1) * 64], B2[0:64, tau * 128:(tau + 1) * 128], C.identf[0:64, 0:64],
                        reads=["B2"], writes=[("ps", 2 + q8)])
        P.copy("dve", TMGD[:, q8 * 8:(q8 + 1) * 8, :], pst[:, :].rearrange("p (a b) -> p a b", a=8),
               reads=[("ps", 2 + q8)], writes=[("TMGD", q8)])
    tm1k = [("TM1", q) for q in range(4)]
    P.act(TMK, TMGD, AF.Exp, reads=[("TMGD", 0), ("TMGD", 1)], writes=["TMK"])
    P.act(TMB, TM1[:, :, 0:64], AF.Exp, reads=tm1k, writes=["TMB"])
    P.tt("dve", TMB[:, :, 0:32], TMB[:, :, 0:32], TM1[:, :, 96:128], ALU.mult, reads=["TMB"] + tm1k, writes=["TMB"])
    P.tt("dve", TMB[:, :, 32:64], TMB[:, :, 32:64], TM1[:, :, 64:96], ALU.mult, reads=["TMB"] + tm1k, writes=["TMB"])
    P.barrier()

    off[0] = base
    qT = alloc([T], BF16)
    kT = alloc([T], BF16)
    Ktm = alloc([16, 128], BF16)
    vT = alloc([T], BF16)
    Vtm = alloc([16, 128], BF16)
    nGMs2 = [[alloc([4, 128], F32) for _ in range(2)] for _ in range(2)]
    KQMt2 = [alloc([4, 128], F32) for _ in range(2)]
    SCR = []
    for d in range(2):
        scr = Ctx()
        scr.t1 = alloc([4, 128], F32)
        scr.t2 = alloc([4, 128], F32)
        scr.tmp = alloc([4, 128], F32)
        scr.egr = alloc([512], F32)
        scr.Qb = [alloc([4, 128], BF16) for _ in range(2)]
        scr.Pb = [alloc([4, 128], BF16) for _ in range(2)]
        scr.Rb = [alloc([4, 128], BF16) for _ in range(2)]
        scr.bV = alloc([4, 128], BF16)
        scr.Kp = alloc([4, 128], BF16)
        SCR.append(scr)
    U = [alloc([16, 128], F32) for _ in range(2)]
    WT = [alloc([T], BF16) for _ in range(2)]
    QgT = [alloc([T], BF16) for _ in range(2)]
    Kd = [alloc([16, 128], BF16) for _ in range(2)]
    AT = [alloc([16, 128], BF16) for _ in range(2)]
    S = [alloc([128], F32) for _ in range(2)]
    Sbf = [alloc([128], BF16) for _ in range(2)]
    egt = [[alloc([32], F32) for _ in range(2)] for _ in range(2)]
    vnew = [alloc([128], BF16) for _ in range(2)]
    sel = [alloc([128], F32) for _ in range(4)]
    nsel = [alloc([128], F32) for _ in range(2)]
    oT = [alloc([T], F32) for _ in range(2)]
    sg = alloc([T], BF16)
    yst = alloc([T], BF16)
    osq = [alloc([512], F32) for _ in range(2)]
    rs = [alloc([512], F32) for _ in range(2)]
    assert off[0] <= O_CONST, off[0]
    Ms = [C.masks[:, 0, :], C.masks[:, 1, :]]
    Mt = [C.masks[:, 2, :], C.masks[:, 3, :]]

    def b4(ap):
        return ap[:, None, :].broadcast_to([128, 4, 128])

    def p4(pst):
        return pst[:, :].rearrange("p (a b) -> p a b", a=4)
    nwcol = C.vecs[:, V_DNNORM + j_:V_DNNORM + j_ + 1]
    rot = Rot([3, 4, 5, 6, 7])
    tm1k = []

    def head_setup(hv):
        h = hv // 2
        if hv % 2 == 0:
            P.dma("sp", qT, C.qS[h], reads=[], writes=["qT"])
            P.dma("sp", kT, C.kS[h], reads=[], writes=["kT"])
            yield
            for q8 in range(2):
                b = rot.next()
                ptb = C.psum[b][:, :].bitcast(BF16)
                for k in range(8):
                    tau = q8 * 8 + k
                    P.transpose(ptb[:, k * 128:(k + 1) * 128], kT[:, tau * 128:(tau + 1) * 128], C.identb,
                                reads=["kT"], writes=[("ps", b)])
                P.copy("act", Ktm[:, q8 * 8:(q8 + 1) * 8, :], ptb.rearrange("p (a b) -> p a b", a=8),
                       reads=[("ps", b)], writes=[("Ktm", q8)])
                yield
        P.dma("sp", vT, C.vS[hv], reads=[], writes=["vT"])
        yield
        for q8 in range(2):
            b = rot.next()
            ptb = C.psum[b][:, :].bitcast(BF16)
            for k in range(8):
                tau = q8 * 8 + k
                P.transpose(ptb[:, k * 128:(k + 1) * 128], vT[:, tau * 128:(tau + 1) * 128], C.identb,
                            reads=["vT"], writes=[("ps", b)])
            P.copy("act", Vtm[:, q8 * 8:(q8 + 1) * 8, :], ptb.rearrange("p (a b) -> p a b", a=8),
                   reads=[("ps", b)], writes=[("Vtm", q8)])
            yield
        rows = [32 + hv, hv, 64 + hv, 96 + hv]
        for i, r in enumerate(rows):
            P.ts(DEBUG.get("pe1", "dve"), sel[i], C.ones1, C.identf[:, r:r + 1], None, ALU.mult, None, reads=[], writes=[("sel", i)])
        for d in range(2):
            P.ts(DEBUG.get("pe1", "dve"), nsel[d], sel[d], -1.0, None, ALU.mult, None, reads=[("sel", d)], writes=[("nsel", d)])
        yield
        for d in range(2):
            b = rot.next()
            pst = C.psum[b]
            P.add("pe", lambda e, pst=pst, d=d: e.matmul(pst[:, 0:32], sel[d], GTOT, start=True, stop=True),
                  reads=[("sel", d)], writes=[("ps", b)])
            P.act(egt[hv % 2][d], pst[:, 0:32], AF.Exp, reads=[("ps", b)], writes=[("egt", hv % 2, d)])
            yield

    urot = [Rot([3, 4, 7]), Rot([5, 6])]

    def tg_shared(hv, tg, st):
        nGMs = nGMs2[st]
        rot = urot[st]
        b0, b1 = rot.next(), rot.next()
        psG, psKQ = C.psum[b0], C.psum[b1]

        def fg(e):
            ins = None
            for k in range(4):
                sl = slice((tg * 4 + k) * 128, (tg * 4 + k + 1) * 128)
                ins = e.matmul(psG[:, k * 128:(k + 1) * 128], kT[:, sl], kT[:, sl], start=True, stop=True)
            return ins

        def fkq(e):
            ins = None
            for k in range(4):
                sl = slice((tg * 4 + k) * 128, (tg * 4 + k + 1) * 128)
                ins = e.matmul(psKQ[:, k * 128:(k + 1) * 128], kT[:, sl], qT[:, sl], start=True, stop=True)
            return ins
        P.add("pe", fg, reads=["kT"], writes=[("ps", b0)])
        P.add("pe", fkq, reads=["kT", "qT"], writes=[("ps", b1)])
        yield
        for d in range(2):
            P.stt("dve", nGMs[d], p4(psG), -1.0, b4(Ms[d]), ALU.mult, ALU.mult, reads=[("ps", b0)],
                  writes=[("nGMs", st, d)])
            yield
        P.tt("dve", KQMt2[st], p4(psKQ), b4(Mt[1 - st]), ALU.mult, reads=[("ps", b1)], writes=[("KQMt", st)])
        yield

    def pre_unit(hv, tg, d):
        scr = SCR[d]
        nGMs = nGMs2[d]
        KQMt_d = KQMt2[d]
        t1, t2, tmp, egr, Qb, Pb, Rb, bV, Kp = scr.t1, scr.t2, scr.tmp, scr.egr, scr.Qb, scr.Pb, scr.Rb, scr.bV, scr.Kp
        sk = lambda n, *a: (n, d) + a
        tsl = slice(tg * 512, (tg + 1) * 512)
        cgc = [32 + hv, hv][d]
        cbe = [64 + hv, 96 + hv][d]
        rot = urot[d]
        bD = rot.next()
        psD = C.psum[bD]

        def fdiff(e):
            ins = None
            for k in range(4):
                sl = slice((tg * 4 + k) * 128, (tg * 4 + k + 1) * 128)
                e.matmul(psD[:, k * 128:(k + 1) * 128], sel[d], BG[:, sl], start=True, stop=False)
                ins = e.matmul(psD[:, k * 128:(k + 1) * 128], BG[:, sl], nsel[d], start=False, stop=True)
            return ins
        P.add("pe", fdiff, reads=[("sel", d), ("nsel", d)], writes=[("ps", bD)])
        yield
        P.ts("dve", t1, p4(psD), 0.0, None, ALU.max, None, reads=[("ps", bD)], writes=[sk("t1")])
        P.ts("dve", t2, p4(psD), 0.0, None, ALU.min, None, reads=[("ps", bD)], writes=[sk("t2")])
        bRg = rot.next()
        psRg = C.psum[bRg]
        P.add("pe", lambda e: e.matmul(psRg[:, :], sel[d], BG[:, tsl], start=True, stop=True),
              reads=[("sel", d)], writes=[("ps", bRg)])
        yield
        P.act(t1, t1, AF.Exp, reads=[sk("t1")], writes=[sk("t1")], scale=-1.0)
        P.act(t2, t2, AF.Exp, reads=[sk("t2")], writes=[sk("t2")])
        P.act(egr, psRg[:, :], AF.Exp, reads=[("ps", bRg)], writes=[sk("egr")])
        bRb = rot.next()
        psRb = C.psum[bRb]
        P.add("pe", lambda e: e.matmul(psRb[:, :], sel[2 + d], BG[:, tsl], start=True, stop=True),
              reads=[("sel", 2 + d)], writes=[("ps", bRb)])
        yield
        for k in range(4):
            tau = tg * 4 + k
            bei = TM1[:, tau, cbe:cbe + 1]
            P.stt("dve", Qb[0][:, k, :], t1[:, k, :], bei, nGMs[d][:, k, :], ALU.mult, ALU.mult,
                  reads=[sk("t1"), ("nGMs", d, d)], writes=[sk("Q", 0, k)])
        yield
        P.tt("dve", tmp, t2, p4(psRb), ALU.mult, reads=[sk("t2"), ("ps", bRb)], writes=[sk("tmp")])
        P.tt("dve", Pb[0], tmp, nGMs[1 - d], ALU.mult, reads=[sk("tmp"), ("nGMs", d, 1 - d)], writes=[sk("P", 0)])
        yield
        P.tt("dve", Rb[0], Pb[0], b4(C.identf), ALU.add, reads=[sk("P", 0)], writes=[sk("R", 0)])
        P.tt("dve", AT[d][:, tg * 4:(tg + 1) * 4, :], t2, KQMt_d, ALU.mult, reads=[sk("t2"), ("KQMt", d)],
             writes=[("AT", d, tg)])
        yield
        P.tt("dve", QgT[d][:, tsl], qT[:, tsl], egr, ALU.mult, reads=["qT", sk("egr")], writes=[("QgT", d, tg)])
        for k in range(4):
            tau = tg * 4 + k
            bei = TM1[:, tau, cbe:cbe + 1]
            if DEBUG.get("pe2", "dve") == "pool":
                bc = lambda ap: ap.broadcast_to([128, 128])
                P.tt("pool", bV[:, k, :], Vtm[:, tau, :], bc(bei), ALU.mult, reads=[("Vtm", tau // 8)],
                     writes=[sk("bV", k)])
                P.tt("pool", Kp[:, k, :], Ktm[:, tau, :], bc(TMB[:, tau, cgc:cgc + 1]), ALU.mult,
                     reads=[("Ktm", tau // 8)], writes=[sk("Kp", k)])
                P.tt("pool", Kd[d][:, tau, :], Ktm[:, tau, :], bc(TMK[:, tau, cgc:cgc + 1]), ALU.mult,
                     reads=[("Ktm", tau // 8)], writes=[("Kd", d, tau)])
            else:
                if DEBUG.get("nomul"):
                    P.ts("dve", bV[:, k, :], Vtm[:, tau, :], bei, None, ALU.mult, None, reads=[("Vtm", tau // 8)],
                         writes=[sk("bV", k)])
                    P.ts("dve", Kp[:, k, :], Ktm[:, tau, :], TMB[:, tau, cgc:cgc + 1], None, ALU.mult, None,
                         reads=[("Ktm", tau // 8)], writes=[sk("Kp", k)])
                    P.ts("dve", Kd[d][:, tau, :], Ktm[:, tau, :], TMK[:, tau, cgc:cgc + 1], None, ALU.mult, None,
                         reads=[("Ktm", tau // 8)], writes=[("Kd", d, tau)])
                    continue
                P.add("act", lambda e, k=k, tau=tau, bei=bei: e.mul(out=bV[:, k, :], in_=Vtm[:, tau, :], mul=bei),
                      reads=[("Vtm", tau // 8)], writes=[sk("bV", k)])
                P.add("act", lambda e, k=k, tau=tau: e.mul(out=Kp[:, k, :], in_=Ktm[:, tau, :],
                                                           mul=TMB[:, tau, cgc:cgc + 1]),
                      reads=[("Ktm", tau // 8)], writes=[sk("Kp", k)])
                P.add("act", lambda e, tau=tau: e.mul(out=Kd[d][:, tau, :], in_=Ktm[:, tau, :],
                                                      mul=TMK[:, tau, cgc:cgc + 1]),
                      reads=[("Ktm", tau // 8)], writes=[("Kd", d, tau)])
        yield
        qk = [sk("Q", 0, k) for k in range(4)]
        cur = 0
        rc = 0
        for m in range(1, 6):
            nxt = 1 - cur
            bA = rot.next()
            psA = C.psum[bA]

            def fq(e, psA=psA, cur=cur):
                ins = None
                for k in range(4):
                    ins = e.matmul(psA[:, k * 128:(k + 1) * 128], Pb[cur][:, k, :], Qb[cur][:, k, :],
                                   start=True, stop=True)
                return ins
            P.add("pe", fq, reads=qk + [sk("P", cur)], writes=[("ps", bA)])
            if m < 5:
                bB = rot.next()
                psB = C.psum[bB]

                def fp(e, psB=psB, cur=cur):
                    ins = None
                    for k in range(4):
                        ins = e.matmul(psB[:, k * 128:(k + 1) * 128], Qb[cur][:, k, :], Pb[cur][:, k, :],
                                       start=True, stop=True)
                    return ins
                P.add("pe", fp, reads=qk + [sk("P", cur)], writes=[("ps", bB)])
            P.copy("act", Qb[nxt], p4(psA), reads=[("ps", bA)], writes=[sk("Q", nxt)])
            if m < 5:
                P.copy("act", Pb[nxt], p4(psB), reads=[("ps", bB)], writes=[sk("P", nxt)])
            yield
            bC = rot.next()
            psC = C.psum[bC]

            def fr(e, psC=psC, nxt=nxt, rc=rc):
                ins = None
                for k in range(4):
                    e.matmul(psC[:, k * 128:(k + 1) * 128], Qb[nxt][:, k, :], Rb[rc][:, k, :], start=True, stop=False)
                    ins = e.matmul(psC[:, k * 128:(k + 1) * 128], C.identb, Rb[rc][:, k, :], start=False, stop=True)
                return ins
            P.add("pe", fr, reads=[sk("Q", nxt), sk("R", rc)], writes=[("ps", bC)])
            P.copy("act", Rb[1 - rc], p4(psC), reads=[("ps", bC)], writes=[sk("R", 1 - rc)])
            yield
            rc = 1 - rc
            cur = nxt
            qk = [sk("Q", cur)]
        TT = Rb[rc]
        ttk = sk("R", rc)
        bU = rot.next()
        psU = C.psum[bU]

        def fu(e):
            ins = None
            for k in range(4):
                ins = e.matmul(psU[:, k * 128:(k + 1) * 128], TT[:, k, :], bV[:, k, :], start=True, stop=True)
            return ins
        P.add("pe", fu, reads=[ttk] + [sk("bV", k) for k in range(4)], writes=[("ps", bU)])
        P.copy("act", U[d][:, tg * 4:(tg + 1) * 4, :], p4(psU), reads=[("ps", bU)], writes=[("U", d, tg)])
        yield
        bW = rot.next()
        psW = C.psum[bW]

        def fw(e):
            ins = None
            for k in range(4):
                ins = e.matmul(psW[:, k * 128:(k + 1) * 128], Kp[:, k, :], TT[:, k, :], start=True, stop=True)
            return ins
        P.add("pe", fw, reads=[ttk] + [sk("Kp", k) for k in range(4)], writes=[("ps", bW)])
        P.copy("act", WT[d][:, tsl], psW[:, :], reads=[("ps", bW)], writes=[("WT", d, tg)])
        yield

    def seq_group(hv, u, dirs=(0, 1)):
        eg = egt[hv % 2]
        if u == 0:
            for d in dirs:
                P.memset("dve", S[d], 0.0, writes=[("S", d)])
                P.memset("dve", Sbf[d], 0.0, writes=[("Sbf", d)])
        for st in range(8):
            step = u * 8 + st
            for d in dirs:
                c = step if d == 0 else 31 - step
                tau, hf = c // 2, c % 2
                tg = tau // 4
                r0 = 64 * hf
                bq = 1 + d
                psq = C.psum[bq]
                pso = C.psum[0]
                P.add("pe", lambda e, d=d, tau=tau, psq=psq: e.matmul(
                    psq[:, 0:128], WT[d][:, tau * 128:(tau + 1) * 128], Sbf[d], start=True, stop=True),
                    reads=[("WT", d, tg), ("Sbf", d)], writes=[("ps", bq)])
                P.tt("dve", vnew[d][r0:r0 + 64, :], U[d][r0:r0 + 64, tau, :], psq[r0:r0 + 64, 0:128], ALU.subtract,
                     reads=[("U", d, tg), ("ps", bq)], writes=[("vnew", d)])
                cs = d * 256 + (c % 4) * 64

                def fo(e, d=d, c=c, tau=tau, r0=r0, cs=cs, psq=psq):
                    e.matmul(pso[:, cs:cs + 64], Sbf[d], QgT[d][:, c * 64:(c + 1) * 64], start=True, stop=False)
                    e.matmul(pso[:, cs:cs + 64], vnew[d][r0:r0 + 64, :], AT[d][r0:r0 + 64, tau, r0:r0 + 64],
                             start=False, stop=True)
                    return e.matmul(psq[:, 128:256], Kd[d][r0:r0 + 64, tau, :], vnew[d][r0:r0 + 64, :],
                                    start=True, stop=True)
                P.add("pe", fo, reads=[("Sbf", d), ("QgT", d, tg), ("vnew", d), ("AT", d, tg), ("Kd", d, tau)],
                      writes=[("ps", 0), ("ps", bq)])
                P.stt("dve", Sbf[d], S[d], eg[d][:, c:c + 1], psq[:, 128:256], ALU.mult, ALU.add,
                      reads=[("S", d), ("egt", hv % 2, d), ("ps", bq)], writes=[("Sbf", d)])
                P.stt("dve", S[d], S[d], eg[d][:, c:c + 1], psq[:, 128:256], ALU.mult, ALU.add,
                      reads=[("S", d), ("egt", hv % 2, d), ("ps", bq)], writes=[("S", d)])
                last = (c % 4 == 3) if d == 0 else (c % 4 == 0)
                if last:
                    g4 = c // 4
                    P.copy("act", oT[d][:, g4 * 256:(g4 + 1) * 256], pso[:, d * 256:(d + 1) * 256],
                           reads=[("ps", 0)], writes=[("oT", d, g4)])
                yield

    def finalize(hv):
        P.dma("sp", sg, C.sgS[hv], reads=[], writes=["sg"])
        for tb in range(NTB):
            tsl = slice(tb * 512, (tb + 1) * 512)
            i2 = tb % 2
            otk = [("oT", d, g4) for d in range(2) for g4 in (2 * tb, 2 * tb + 1)]
            P.tt("dve", oT[0][:, tsl], oT[0][:, tsl], oT[1][:, tsl], ALU.add, reads=otk, writes=[("o", tb)])
            P.act(osq[i2], oT[0][:, tsl], AF.Square, reads=[("o", tb)], writes=[("osq", i2)])
            b = 1 + (tb % 2)
            pst = C.psum[b]
            P.add("pe", lambda e, pst=pst, i2=i2: e.matmul(pst[:, :], C.ones1, osq[i2], start=True, stop=True),
                  reads=[("osq", i2)], writes=[("ps", b)])
            P.act(rs[i2], pst[:, :], AF.Ln, reads=[("ps", b)], writes=[("rs", i2)],
                  bias=C.vecs[:, V_EPS:V_EPS + 1], scale=1.0 / 128)
            yield
            P.act(rs[i2], rs[i2], AF.Exp, reads=[("rs", i2)], writes=[("rs", i2)], scale=-0.5)
            P.stt("dve", osq[i2], oT[0][:, tsl], nwcol, rs[i2], ALU.mult, ALU.mult,
                  reads=[("o", tb), ("rs", i2), ("osq", i2)], writes=[("osq", i2)])
            P.tt("dve", yst[:, tsl], osq[i2], sg[:, tsl], ALU.mult, reads=[("osq", i2), "sg"], writes=[("yst", tb)])
            yield
        P.dma("sp", C.yS[hv], yst, reads=[("yst", tb) for tb in range(NTB)], writes=[("yS", hv)])
        yield

    def record(gen):
        P.rec = []
        for _ in gen:
            pass
        out = P.rec
        P.rec = None
        return out

    def merge(*lists):
        lists = [l for l in lists if l]
        pos = [0] * len(lists)
        out = []
        while True:
            best, bf = -1, 2.0
            for i, l in enumerate(lists):
                if pos[i] < len(l):
                    f = pos[i] / len(l)
                    if f < bf:
                        best, bf = i, f
            if best < 0:
                break
            out.append(lists[best][pos[best]])
            pos[best] += 1
        return out

    def pre_ops(hv, u):
        a = record(tg_shared(hv, u, 0)) + record(pre_unit(hv, u, 0))
        b_ = record(tg_shared(hv, 3 - u, 1)) + record(pre_unit(hv, 3 - u, 1))
        ops = merge(a, b_)
        if u == 0:
            ops = record(head_setup(hv)) + ops
        return ops

    def seq_ops(hv, u):
        ops = merge(record(seq_group(hv, u, (0,))), record(seq_group(hv, u, (1,))))
        if u == 3:
            ops = ops + record(finalize(hv))
        return ops

    def play(ops):
        for (eng, fn, reads, writes, dma) in ops:
            P.add(eng, fn, reads, writes, dma)

    groups = [(hv, u) for hv in range(DEBUG.get('nhv', 32)) for u in range(4)]
    play(pre_ops(0, 0))
    for gi, (hv, u) in enumerate(groups):
        so = seq_ops(hv, u)
        po = pre_ops(*groups[gi + 1]) if gi + 1 < len(groups) else []
        play(merge(so, po))
```
